# Optimizing a Trainium2 kernel written in Bass

```python
import math
import jax
import jax.numpy as jnp
from jax import lax
import numpy as np

D_MODEL = 2048
BATCH = 4
SEQ = 2048
DEPTH = 2

N_META = 16
N_BRANCH = 4
BRANCH_WIDTH = D_MODEL // 4
LRU_WIDTH = BRANCH_WIDTH
LRU_HEADS = 4
LRU_BLOCK = LRU_WIDTH // LRU_HEADS
LRU_CONV = 4
LRU_C = 8.0
POOL_WINDOWS = (2, 4, 8, 16)
POOL_WIDTH = BRANCH_WIDTH
POOL_GROUP = POOL_WIDTH // len(POOL_WINDOWS)
HGRN_HEADS = 4
HGRN_KDIM = 128
HGRN_VDIM = BRANCH_WIDTH // HGRN_HEADS
HGRN_KWIDTH = HGRN_HEADS * HGRN_KDIM
HGRN_VWIDTH = HGRN_HEADS * HGRN_VDIM
HGRN_CHUNK = 64
DIFF_HEADS = 4
DIFF_HEAD_DIM = 64
DIFF_VDIM = 2 * DIFF_HEAD_DIM
DIFF_QK_WIDTH = DIFF_HEADS * 2 * DIFF_HEAD_DIM
DIFF_V_WIDTH = DIFF_HEADS * DIFF_VDIM
ATTN_BLOCK = 128
REL_BUCKETS = 32
REL_MAX_DIST = 128
FFN_HIDDEN = ((8 * D_MODEL // 3 + 255) // 256) * 256
SPLIT_WIDTHS = (LRU_WIDTH, LRU_WIDTH, POOL_WIDTH, HGRN_KWIDTH, HGRN_KWIDTH, HGRN_VWIDTH, HGRN_VWIDTH, DIFF_QK_WIDTH, DIFF_QK_WIDTH, DIFF_V_WIDTH, N_BRANCH * D_MODEL)
N_IN = sum(SPLIT_WIDTHS)

kernel_name = 'hybrid_gated_four_mixer_block'


def rms_norm(x, w, eps=1e-6):
    xf = x.astype(jnp.float32)
    y = xf * lax.rsqrt(jnp.mean(xf * xf, axis=-1, keepdims=True) + eps)
    return (y * w.astype(jnp.float32)).astype(x.dtype)


def rg_lru_branch(u, gate, conv_w, conv_b, w_a, b_a, w_x, b_x, lam):
    bsz, t_len, width = u.shape
    xc = lax.conv_general_dilated(u, conv_w[:, None, :].astype(u.dtype), window_strides=(1,), padding=[(LRU_CONV - 1, 0)], dimension_numbers=('NWC', 'WIO', 'NWC'), feature_group_count=width) + conv_b
    xb = xc.reshape(bsz, t_len, LRU_HEADS, LRU_BLOCK)
    r = jax.nn.sigmoid(jnp.einsum('bthi,hij->bthj', xb, w_a).reshape(bsz, t_len, width) + b_a).astype(jnp.float32)
    i = jax.nn.sigmoid(jnp.einsum('bthi,hij->bthj', xb, w_x).reshape(bsz, t_len, width) + b_x).astype(jnp.float32)
    log_a = -LRU_C * r * jax.nn.softplus(-lam.astype(jnp.float32))
    a = jnp.exp(log_a)
    b = jnp.sqrt(-jnp.expm1(2.0 * log_a)) * (i * xc.astype(jnp.float32))

    def combine(left, right):
        a_l, b_l = left
        a_r, b_r = right
        return a_l * a_r, a_r * b_l + b_r

    _, h = lax.associative_scan(combine, (a, b), axis=1)
    return h.astype(u.dtype) * jax.nn.gelu(gate)


def multiscale_pool_branch(u, pool_w, pool_scale):
    bsz, t_len, width = u.shape
    uf = u.astype(jnp.float32)
    cs = jnp.pad(jnp.cumsum(uf, axis=1), ((0, 0), (1, 0), (0, 0)))
    t = jnp.arange(t_len)
    groups = []
    for gi, win in enumerate(POOL_WINDOWS):
        sl = slice(gi * POOL_GROUP, (gi + 1) * POOL_GROUP)
        csg = cs[:, :, sl]
        start = jnp.maximum(t + 1 - win, 0)
        win_sum = csg[:, 1:] - jnp.take(csg, start, axis=1)
        count = jnp.minimum(t + 1, win).astype(jnp.float32)
        groups.append(win_sum / count[None, :, None] - uf[:, :, sl])
    pooled = jnp.stack(groups, axis=2).astype(u.dtype)
    y = jnp.einsum('btgi,gij->btgj', pooled, pool_w).reshape(bsz, t_len, width)
    return y * pool_scale


def hgrn2_chunk(state, chunk):
    q, k, v, g = chunk
    length = q.shape[2]
    b = jnp.cumsum(g, axis=2)
    causal = jnp.tril(jnp.ones((length, length), dtype=bool))
    rel = b[:, :, :, None, :] - b[:, :, None, :, :]
    decay = jnp.exp(jnp.where(causal[None, None, :, :, None], rel, -jnp.inf))
    scores = jnp.einsum('bhtc,bhsc,bhtsc->bhts', q, k, decay)
    o = jnp.einsum('bhtc,bhcv->bhtv', q * jnp.exp(b), state) + jnp.einsum('bhts,bhsv->bhtv', scores, v)
    b_last = b[:, :, -1:, :]
    new_state = jnp.exp(b_last[:, :, 0, :])[..., None] * state + jnp.einsum('bhsc,bhsv->bhcv', k * jnp.exp(b_last - b), v)
    return new_state, o


def hgrn2_branch(q, f_logit, v, out_gate, lower_bound, norm_w):
    bsz, t_len, _ = q.shape

    def heads(a, d):
        return a.astype(jnp.float32).reshape(bsz, t_len, HGRN_HEADS, d).transpose(0, 2, 1, 3)

    lb = lower_bound.astype(jnp.float32)
    z = f_logit.astype(jnp.float32)
    log_f = jnp.logaddexp(jnp.log(lb), jnp.log1p(-lb) + jax.nn.log_sigmoid(z))
    k_in = (1.0 - lb) * jax.nn.sigmoid(-z)
    qh, kh, gh, vh = heads(q, HGRN_KDIM), heads(k_in, HGRN_KDIM), heads(log_f, HGRN_KDIM), heads(v, HGRN_VDIM)
    state0 = jnp.zeros((bsz, HGRN_HEADS, HGRN_KDIM, HGRN_VDIM), jnp.float32)
    state, o_meta = hgrn2_chunk(state0, (qh[:, :, :N_META], kh[:, :, :N_META], vh[:, :, :N_META], gh[:, :, :N_META]))
    n_chunks = (t_len - N_META) // HGRN_CHUNK

    def to_chunks(a):
        a = a[:, :, N_META:]
        return a.reshape(bsz, HGRN_HEADS, n_chunks, HGRN_CHUNK, a.shape[-1]).transpose(2, 0, 1, 3, 4)

    _, o_real = lax.scan(hgrn2_chunk, state, (to_chunks(qh), to_chunks(kh), to_chunks(vh), to_chunks(gh)))
    o_real = o_real.transpose(1, 2, 0, 3, 4).reshape(bsz, HGRN_HEADS, t_len - N_META, HGRN_VDIM)
    o = jnp.concatenate([o_meta, o_real], axis=2).transpose(0, 2, 1, 3)
    o = rms_norm(o, norm_w).reshape(bsz, t_len, HGRN_VWIDTH)
    return o.astype(q.dtype) * jax.nn.silu(out_gate)


def t5_bucket(q_pos, k_pos):
    n = jnp.maximum(q_pos[:, None] - k_pos[None, :], 0)
    max_exact = REL_BUCKETS // 2
    nf = jnp.maximum(n, 1).astype(jnp.float32)
    large = max_exact + (jnp.log(nf / max_exact) / math.log(REL_MAX_DIST / max_exact) * (REL_BUCKETS - max_exact)).astype(jnp.int32)
    large = jnp.minimum(large, REL_BUCKETS - 1)
    return jnp.where(n < max_exact, n, large)


def diff_attention(q, k, v, rel_bias, lam_vec, subln_w, lambda_init):
    bsz, t_len = q.shape[0], q.shape[1]
    lf = lam_vec.astype(jnp.float32)
    lam = jnp.exp(jnp.sum(lf[0] * lf[1])) - jnp.exp(jnp.sum(lf[2] * lf[3])) + lambda_init
    scale = DIFF_HEAD_DIM ** -0.5
    bounds = [(0, N_META)] + [(N_META + j * ATTN_BLOCK, N_META + (j + 1) * ATTN_BLOCK) for j in range((t_len - N_META) // ATTN_BLOCK)]
    outs = []
    for qs, qe in bounds:
        q_pos = jnp.arange(qs, qe)
        k_pos = jnp.arange(qe)
        bias = rel_bias[t5_bucket(q_pos, k_pos)].astype(jnp.float32).transpose(2, 0, 1)
        s = jnp.einsum('bqhmd,bkhmd->bhmqk', q[:, qs:qe], k[:, :qe]).astype(jnp.float32) * scale + bias[None, :, None]
        s = jnp.where(q_pos[:, None] >= k_pos[None, :], s, -jnp.inf)
        p = jax.nn.softmax(s, axis=-1)
        w = (p[:, :, 0] - lam * p[:, :, 1]).astype(v.dtype)
        outs.append(jnp.einsum('bhqk,bkhd->bqhd', w, v[:, :qe]))
    o = jnp.concatenate(outs, axis=1)
    o = rms_norm(o, subln_w) * (1.0 - lambda_init)
    return o.reshape(bsz, t_len, DIFF_V_WIDTH)


def token_mixer(hn, w_in, conv_w, conv_b, w_a, b_a, w_x, b_x, lam, pool_w, pool_scale, lower_bound, hgrn_norm_w, rel_bias, diff_lam, diff_subln_w, lambda_init, w_branch, w_out):
    bsz, t_len, _ = hn.shape
    z = hn @ w_in
    parts = jnp.split(z, np.cumsum(SPLIT_WIDTHS)[:-1].tolist(), axis=-1)
    lru_x, lru_gate, pool_u, hq, hf, hi, hog, dq, dk, dv, gates = parts
    y_a = rg_lru_branch(lru_x, lru_gate, conv_w, conv_b, w_a, b_a, w_x, b_x, lam)
    y_b = multiscale_pool_branch(pool_u, pool_w, pool_scale)
    y_c = hgrn2_branch(hq, hf, hi, hog, lower_bound, hgrn_norm_w)
    y_d = diff_attention(dq.reshape(bsz, t_len, DIFF_HEADS, 2, DIFF_HEAD_DIM), dk.reshape(bsz, t_len, DIFF_HEADS, 2, DIFF_HEAD_DIM), dv.reshape(bsz, t_len, DIFF_HEADS, DIFF_VDIM), rel_bias, diff_lam, diff_subln_w, lambda_init)
    ys = jnp.stack([y_a, y_b, y_c, y_d], axis=2)
    proj = jnp.einsum('btkc,kcd->btkd', ys, w_branch)
    g = jax.nn.sigmoid(gates.reshape(bsz, t_len, N_BRANCH, D_MODEL))
    merged = jnp.sum(g * proj, axis=2)
    return merged @ w_out


def swiglu(h, w_gu, w_down):
    g, u = jnp.split(h @ w_gu, 2, axis=-1)
    return (jax.nn.silu(g) * u) @ w_down


def setup_inputs(seed: int = 0) -> dict:
    key = jax.random.key(seed)
    ks = jax.random.split(key, 32)
    f32 = jnp.float32

    def nrm(k, shape, scale):
        return jax.random.normal(k, shape, f32) * scale

    u = jax.random.uniform(ks[9], (DEPTH, LRU_WIDTH), f32, minval=0.9, maxval=0.999)
    a_base = u ** (1.0 / LRU_C)
    return {
        'x': nrm(ks[0], (BATCH, SEQ, D_MODEL), 1.0),
        'meta_tokens': nrm(ks[1], (N_META, D_MODEL), 1.0),
        'rel_bias': nrm(ks[2], (REL_BUCKETS, DIFF_HEADS), 0.5),
        'hgrn_lower_bounds': nrm(ks[3], (DEPTH, HGRN_KWIDTH), 0.1),
        'norm_mix_pre': 1.0 + nrm(ks[4], (DEPTH, D_MODEL), 0.02),
        'norm_mix_post': 1.0 + nrm(ks[5], (DEPTH, D_MODEL), 0.02),
        'norm_ffn_pre': 1.0 + nrm(ks[6], (DEPTH, D_MODEL), 0.02),
        'norm_ffn_post': 1.0 + nrm(ks[7], (DEPTH, D_MODEL), 0.02),
        'w_in': nrm(ks[8], (DEPTH, D_MODEL, N_IN), D_MODEL ** -0.5),
        'lru_conv_w': nrm(ks[10], (DEPTH, LRU_CONV, LRU_WIDTH), LRU_CONV ** -0.5),
        'lru_conv_b': nrm(ks[11], (DEPTH, LRU_WIDTH), 0.01),
        'lru_w_a': nrm(ks[12], (DEPTH, LRU_HEADS, LRU_BLOCK, LRU_BLOCK), LRU_BLOCK ** -0.5),
        'lru_b_a': nrm(ks[13], (DEPTH, LRU_WIDTH), 0.01),
        'lru_w_x': nrm(ks[14], (DEPTH, LRU_HEADS, LRU_BLOCK, LRU_BLOCK), LRU_BLOCK ** -0.5),
        'lru_b_x': nrm(ks[15], (DEPTH, LRU_WIDTH), 0.01),
        'lru_lambda': jnp.log(a_base) - jnp.log1p(-a_base),
        'pool_w': nrm(ks[16], (DEPTH, len(POOL_WINDOWS), POOL_GROUP, POOL_GROUP), POOL_GROUP ** -0.5),
        'pool_scale': 1.0 + nrm(ks[17], (DEPTH, POOL_WIDTH), 0.02),
        'hgrn_norm': 1.0 + nrm(ks[18], (DEPTH, HGRN_VDIM), 0.02),
        'diff_lambda': nrm(ks[19], (DEPTH, 4, DIFF_HEAD_DIM), 0.1),
        'diff_subln': 1.0 + nrm(ks[20], (DEPTH, DIFF_VDIM), 0.02),
        'w_branch': nrm(ks[21], (DEPTH, N_BRANCH, BRANCH_WIDTH, D_MODEL), BRANCH_WIDTH ** -0.5),
        'w_out': nrm(ks[22], (DEPTH, D_MODEL, D_MODEL), D_MODEL ** -0.5),
        'ffn_w_gu': nrm(ks[23], (DEPTH, D_MODEL, 2 * FFN_HIDDEN), D_MODEL ** -0.5),
        'ffn_w_down': nrm(ks[24], (DEPTH, FFN_HIDDEN, D_MODEL), FFN_HIDDEN ** -0.5),
    }


def reference(x, meta_tokens, rel_bias, hgrn_lower_bounds, norm_mix_pre, norm_mix_post, norm_ffn_pre, norm_ffn_post, w_in, lru_conv_w, lru_conv_b, lru_w_a, lru_b_a, lru_w_x, lru_b_x, lru_lambda, pool_w, pool_scale, hgrn_norm, diff_lambda, diff_subln, w_branch, w_out, ffn_w_gu, ffn_w_down):
    bsz = x.shape[0]
    meta = jnp.broadcast_to(meta_tokens[None].astype(x.dtype), (bsz, N_META, D_MODEL))
    h = jnp.concatenate([meta, x], axis=1)
    lb_cum = jnp.cumsum(jax.nn.softmax(hgrn_lower_bounds.astype(jnp.float32), axis=0), axis=0)
    lower_bounds = lb_cum - lb_cum[0:1]
    for layer in range(DEPTH):
        lambda_init = 0.8 - 0.6 * math.exp(-0.3 * layer)
        mix = token_mixer(rms_norm(h, norm_mix_pre[layer]), w_in[layer], lru_conv_w[layer], lru_conv_b[layer], lru_w_a[layer], lru_b_a[layer], lru_w_x[layer], lru_b_x[layer], lru_lambda[layer], pool_w[layer], pool_scale[layer], lower_bounds[layer], hgrn_norm[layer], rel_bias, diff_lambda[layer], diff_subln[layer], lambda_init, w_branch[layer], w_out[layer])
        h = h + rms_norm(mix, norm_mix_post[layer])
        ffn = swiglu(rms_norm(h, norm_ffn_pre[layer]), ffn_w_gu[layer], ffn_w_down[layer])
        h = h + rms_norm(ffn, norm_ffn_post[layer])
    return h[:, N_META:]
```

```python
import numpy as np, math
from contextlib import ExitStack
import concourse.bass as bass
import concourse.mybir as mybir
from concourse.bass_utils import run_bass_kernel_spmd

F32 = mybir.dt.float32
BF16 = mybir.dt.bfloat16
AF = mybir.ActivationFunctionType
ALU = mybir.AluOpType
AX = mybir.AxisListType


class Buf:
    def __init__(self, t, name):
        self.t = t
        self.name = name
        self.w = None
        self.r = []

    def __getitem__(self, idx):
        return self.t[idx]


class Ctx:
    NDS = 24

    def __init__(self, nc, es):
        self.nc, self.es = nc, es
        self.engs = {'pe': nc.tensor, 'act': nc.scalar, 'dve': nc.vector,
                     'pool': nc.gpsimd, 'sp': nc.sync}
        self.sem = {k: es.enter_context(nc.semaphore('s_' + k)) for k in ['pe', 'act', 'dve', 'pool']}
        self.cnt = {k: 0 for k in self.sem}
        self.dsem = [es.enter_context(nc.semaphore('d%d' % i)) for i in range(self.NDS)]
        self.dcnt = [0] * self.NDS
        self.dnext = 0
        self.seen = {e: {} for e in self.engs}
        self.nbuf = 0

    def sb(self, shape, dt, name=None, es=None):
        self.nbuf += 1
        name = name or 'sb%d' % self.nbuf
        t = (es or self.es).enter_context(self.nc.sbuf_tensor(name, list(shape), dt))
        return Buf(t, name)

    def ps(self, shape, dt, name=None, es=None):
        self.nbuf += 1
        name = name or 'ps%d' % self.nbuf
        t = (es or self.es).enter_context(self.nc.psum_tensor(name, list(shape), dt))
        return Buf(t, name)

    def dram(self, name, shape, dt, kind='Internal'):
        t = self.nc.dram_tensor(name, list(shape), dt, kind=kind)
        return Buf(t.ap(), name)

    def _need(self, e, tk, raw):
        if tk is None:
            return
        key, val = tk
        if key == e and e == 'pe':
            return
        if self.seen[e].get(key, 0) >= val:
            return
        if isinstance(key, str):
            self.engs[e].wait_ge(self.sem[key], val)
        else:
            self.engs[e].wait_ge(self.dsem[key[1]], 16 * val)
        self.seen[e][key] = val

    def _deps(self, e, reads, writes):
        for b in reads:
            self._need(e, b.w, True)
        for b in writes:
            self._need(e, b.w, False)
            for tk in b.r:
                self._need(e, tk, False)

    def _commit(self, tk, reads, writes):
        for b in reads:
            b.r.append(tk)
            if len(b.r) > 12:
                d = {}
                for k, v in b.r:
                    d[k] = max(d.get(k, 0), v)
                b.r = list(d.items())
        for b in writes:
            b.w = tk
            b.r = []

    def op(self, e, fn, reads=(), writes=()):
        self._deps(e, reads, writes)
        ins = fn(self.engs[e])
        self.cnt[e] += 1
        ins.then_inc(self.sem[e], 1)
        self._commit((e, self.cnt[e]), reads, writes)
        return ins

    def mm(self, items, reads, out, start=True, stop=True):
        e = 'pe'
        self._deps(e, reads, [out])
        n = len(items)
        ins = None
        for i, (o, l, r) in enumerate(items):
            ins = self.nc.tensor.matmul(o, l, r, start=(start and i == 0), stop=(stop and i == n - 1))
        self.cnt[e] += 1
        ins.then_inc(self.sem[e], 1)
        self._commit((e, self.cnt[e]), reads, [out])

    def mm_raw(self, fns, reads, out):
        e = 'pe'
        self._deps(e, reads, [out])
        ins = None
        for f in fns:
            ins = f(self.nc.tensor)
        self.cnt[e] += 1
        ins.then_inc(self.sem[e], 1)
        self._commit((e, self.cnt[e]), reads, [out])

    def dma(self, q, out_ap, in_ap, reads=(), writes=(), **kw):
        i = self.dnext
        self.dnext = (self.dnext + 1) % self.NDS
        if self.dcnt[i] > 0:
            self._need(q, (('d', i), self.dcnt[i]), True)
        self._deps(q, reads, writes)
        ins = self.engs[q].dma_start(out=out_ap, in_=in_ap, **kw)
        ins.then_inc(self.dsem[i], 16)
        self.dcnt[i] += 1
        self._commit((('d', i), self.dcnt[i]), reads, writes)

    def barrier(self):
        tks = [(k, self.cnt[k]) for k in self.cnt if self.cnt[k] > 0]
        tks += [(('d', i), self.dcnt[i]) for i in range(self.NDS) if self.dcnt[i] > 0]
        for e in self.engs:
            for tk in tks:
                if tk[0] == e:
                    continue
                self._need(e, tk, True)

    def final_wait(self):
        tks = [(k, self.cnt[k]) for k in self.cnt if self.cnt[k] > 0]
        tks += [(('d', i), self.dcnt[i]) for i in range(self.NDS) if self.dcnt[i] > 0]
        for tk in tks:
            self._need('sp', tk, True)

def _vec(c, dst_ap, dstb, srcb, src_ap):
    c.dma('sp', dst_ap, src_ap, reads=[srcb], writes=[dstb], allow_slow_non_contiguous=True)


def mixer_lru(c, g, I, l, Z, YS, psf):
    with ExitStack() as s:
        CW = c.sb([128, 4, 4], F32, es=s); CB = c.sb([128, 4], F32, es=s)
        BA = c.sb([128, 4], F32, es=s); BX = c.sb([128, 4], F32, es=s); LM = c.sb([128, 4], F32, es=s)
        CN = c.sb([128, 4], F32, es=s)
        for j_ in range(4):
            _vec(c, CW[:, :, j_], CW, I['lru_conv_w'], I['lru_conv_w'].t[l, j_, :].rearrange("(h p) -> p h", p=128))
        _vec(c, CB[:, :], CB, I['lru_conv_b'], I['lru_conv_b'].t[l, :].rearrange("(h p) -> p h", p=128))
        _vec(c, BA[:, :], BA, I['lru_b_a'], I['lru_b_a'].t[l, :].rearrange("(h p) -> p h", p=128))
        _vec(c, BX[:, :], BX, I['lru_b_x'], I['lru_b_x'].t[l, :].rearrange("(h p) -> p h", p=128))
        _vec(c, LM[:, :], LM, I['lru_lambda'], I['lru_lambda'].t[l, :].rearrange("(h p) -> p h", p=128))
        c.op('act', lambda e: e.activation(CN[:, :], LM[:, :], AF.Exp, scale=-1.0), [LM], [CN])
        c.op('dve', lambda e: e.tensor_scalar_add(CN[:, :], CN[:, :], 1.0), [CN], [CN])
        c.op('act', lambda e: e.activation(CN[:, :], CN[:, :], AF.Ln), [CN], [CN])
        c.op('dve', lambda e: e.tensor_scalar_mul(CN[:, :], CN[:, :], -8.0), [CN], [CN])
        Wst = c.sb([128, 2, 4, 128], F32, es=s); Wb = c.sb([128, 2, 4, 128], BF16, es=s)
        c.dma('sp', Wst[:, 0, :, :], I['lru_w_a'].t[l, :, :, :].rearrange("h i j -> i h j"), reads=[I['lru_w_a']], writes=[Wst])
        c.dma('sp', Wst[:, 1, :, :], I['lru_w_x'].t[l, :, :, :].rearrange("h i j -> i h j"), reads=[I['lru_w_x']], writes=[Wst])
        c.op('act', lambda e: e.copy(Wb[:, :, :, :], Wst[:, :, :, :]), [Wst], [Wb])
        U = c.sb([128, T], F32, es=s); Gt = c.sb([128, T], F32, es=s); XC = c.sb([128, T], F32, es=s)
        R = c.sb([128, T], F32, es=s); IG = c.sb([128, T], F32, es=s); A = c.sb([128, T], F32, es=s)
        TH = c.sb([128, T], F32, es=s); P1 = c.sb([128, T], F32, es=s); HS = c.sb([128, T], F32, es=s)
        XCb = c.sb([128, T], BF16, es=s); Yb = c.sb([128, T], BF16, es=s)
        for h in range(4):
            c.dma('sp', U[:, :], Z.t[h * 128:(h + 1) * 128, :], reads=[Z], writes=[U])
            c.dma('sp', Gt[:, :], Z.t[512 + h * 128:512 + (h + 1) * 128, :], reads=[Z], writes=[Gt])
            c.op('dve', lambda e: e.tensor_scalar(XC[:, :], U[:, :], CW[:, h, 3:4], CB[:, h:h + 1], ALU.mult, ALU.add), [U, CW, CB], [XC])
            for j in range(3):
                sh = 3 - j
                c.op('dve', lambda e: e.scalar_tensor_tensor(XC[:, sh:T], U[:, 0:T - sh], CW[:, h, j:j + 1], XC[:, sh:T], ALU.mult, ALU.add), [U, CW, XC], [XC])
            c.op('act', lambda e: e.copy(XCb[:, :], XC[:, :]), [XC], [XCb])
            for (t0, n) in CH512:
                ps = psf()
                c.mm([(ps[:, 0:n], Wb[:, 0, h, :], XCb[:, t0:t0 + n])], [Wb, XCb], ps)
                c.op('act', lambda e: e.activation(R[:, t0:t0 + n], ps[:, 0:n], AF.Sigmoid, bias=BA[:, h:h + 1]), [ps, BA], [R])
                ps2 = psf()
                c.mm([(ps2[:, 0:n], Wb[:, 1, h, :], XCb[:, t0:t0 + n])], [Wb, XCb], ps2)
                c.op('act', lambda e: e.activation(IG[:, t0:t0 + n], ps2[:, 0:n], AF.Sigmoid, bias=BX[:, h:h + 1]), [ps2, BX], [IG])
            c.op('act', lambda e: e.activation(A[:, :], R[:, :], AF.Exp, scale=CN[:, h:h + 1]), [R, CN], [A])
            c.op('act', lambda e: e.activation(TH[:, :], R[:, :], AF.Tanh, scale=CN[:, h:h + 1]), [R, CN], [TH])
            c.op('dve', lambda e: e.tensor_scalar(P1[:, :], TH[:, :], -1.0, 1.0, ALU.mult, ALU.add), [TH], [P1])
            c.op('dve', lambda e: e.reciprocal(P1[:, :], P1[:, :]), [P1], [P1])
            c.op('dve', lambda e: e.scalar_tensor_tensor(P1[:, :], TH[:, :], -2.0, P1[:, :], ALU.mult, ALU.mult), [TH, P1], [P1])
            c.op('act', lambda e: e.activation(P1[:, :], P1[:, :], AF.Sqrt), [P1], [P1])
            c.op('dve', lambda e: e.tensor_tensor(IG[:, :], IG[:, :], XC[:, :], ALU.mult), [IG, XC], [IG])
            c.op('dve', lambda e: e.tensor_tensor(P1[:, :], P1[:, :], IG[:, :], ALU.mult), [P1, IG], [P1])
            c.op('dve', lambda e: e.tensor_tensor_scan(HS[:, :], A[:, :], P1[:, :], 0.0, ALU.mult, ALU.add), [A, P1], [HS])
            c.op('act', lambda e: e.activation(Gt[:, :], Gt[:, :], AF.Gelu_apprx_tanh), [Gt], [Gt])
            c.op('dve', lambda e: e.tensor_tensor(Yb[:, :], HS[:, :], Gt[:, :], ALU.mult), [HS, Gt], [Yb])
            c.dma('sp', YS.t[h * 128:(h + 1) * 128, :], Yb[:, :], reads=[Yb], writes=[YS])


def mixer_pool(c, g, I, l, Z, YS, psf, IC):
    with ExitStack() as s:
        PSc = c.sb([128, 4], F32, es=s)
        _vec(c, PSc[:, :], PSc, I['pool_scale'], I['pool_scale'].t[l, :].rearrange("(h p) -> p h", p=128))
        Wst = c.sb([128, 4, 128], F32, es=s); Wb = c.sb([128, 4, 128], BF16, es=s)
        c.dma('sp', Wst[:, :, :], I['pool_w'].t[l, :, :, :].rearrange("h i j -> i h j"), reads=[I['pool_w']], writes=[Wst])
        c.op('act', lambda e: e.copy(Wb[:, :, :], Wst[:, :, :]), [Wst], [Wb])
        TP = T + 16
        Up = c.sb([128, TP], F32, es=s); Pa = c.sb([128, TP], F32, es=s); Pb = c.sb([128, TP], F32, es=s)
        PL = c.sb([128, T], F32, es=s); PLb = c.sb([128, T], BF16, es=s); Yb = c.sb([128, T], BF16, es=s)
        t16 = c.sb([128, 16], F32, es=s)
        for b_ in (Up, Pa, Pb):
            c.op('dve', lambda e: e.memset(b_[:, 0:16], 0.0), [], [b_])
        for gi in range(4):
            win = 2 ** (gi + 1)
            c.dma('sp', Up[:, 16:TP], Z.t[1024 + gi * 128:1024 + (gi + 1) * 128, :], reads=[Z], writes=[Up])
            src = Up
            dsts = [Pa, Pb]
            for si, step in enumerate([1, 2, 4, 8][:gi + 1]):
                dst = dsts[si % 2]
                c.op('dve', lambda e: e.tensor_tensor(dst[:, 16:TP], src[:, 16:TP], src[:, 16 - step:TP - step], ALU.add), [src], [dst])
                src = dst
            c.op('dve', lambda e: e.scalar_tensor_tensor(PL[:, :], src[:, 16:TP], 1.0 / win, Up[:, 16:TP], ALU.mult, ALU.subtract), [src, Up], [PL])
            c.op('dve', lambda e: e.tensor_tensor(t16[:, :], src[:, 16:32], IC[:, gi, :], ALU.mult), [src, IC], [t16])
            c.op('dve', lambda e: e.tensor_tensor(PL[:, 0:16], t16[:, :], Up[:, 16:32], ALU.subtract), [t16, Up], [PL])
            c.op('act', lambda e: e.copy(PLb[:, :], PL[:, :]), [PL], [PLb])
            for (t0, n) in CH512:
                ps = psf()
                c.mm([(ps[:, 0:n], Wb[:, gi, :], PLb[:, t0:t0 + n])], [Wb, PLb], ps)
                c.op('dve', lambda e: e.tensor_scalar_mul(Yb[:, t0:t0 + n], ps[:, 0:n], PSc[:, gi:gi + 1]), [ps, PSc], [Yb])
            c.dma('sp', YS.t[512 + gi * 128:512 + (gi + 1) * 128, :], Yb[:, :], reads=[Yb], writes=[YS])


def mixer_hgrn(c, g, I, l, Z, YS, psf, psb, MK, HM, idb, ones_b, epsb):
    with ExitStack() as s:
        LBS = c.sb([128, 2, 4], F32, es=s)
        for l_ in range(2):
            _vec(c, LBS[:, l_, :], LBS, I['hgrn_lower_bounds'], I['hgrn_lower_bounds'].t[l_, :].rearrange("(h p) -> p h", p=128))
        LB = c.sb([128, 4], F32, es=s); OML = c.sb([128, 4], F32, es=s); NOML = c.sb([128, 4], F32, es=s)
        if l == 0:
            c.op('dve', lambda e: e.memset(LB[:, :], 0.0), [], [LB])
        else:
            SM = c.sb([128, 4], F32, es=s)
            c.op('act', lambda e: e.activation(LBS[:, :, :], LBS[:, :, :], AF.Exp), [LBS], [LBS])
            c.op('dve', lambda e: e.tensor_tensor(SM[:, :], LBS[:, 0, :], LBS[:, 1, :], ALU.add), [LBS], [SM])
            c.op('dve', lambda e: e.reciprocal(SM[:, :], SM[:, :]), [SM], [SM])
            c.op('dve', lambda e: e.tensor_tensor(LB[:, :], LBS[:, 1, :], SM[:, :], ALU.mult), [LBS, SM], [LB])
        c.op('dve', lambda e: e.tensor_scalar(OML[:, :], LB[:, :], -1.0, 1.0, ALU.mult, ALU.add), [LB], [OML])
        c.op('dve', lambda e: e.tensor_scalar_add(NOML[:, :], LB[:, :], -1.0), [LB], [NOML])
        HNW = c.sb([128, 1], F32, es=s)
        _vec(c, HNW[:, :], HNW, I['hgrn_norm'], I['hgrn_norm'].t[l, :].rearrange("(p o) -> p o", o=1))
        def ft():
            return c.sb([128, T], F32, es=s)
        Qf, Fz, Vf, OG, SIG, Gg, Kk, Bc, TMP, EK, Oo = [ft() for _ in range(11)]
        QT, KT, KH, VB, Yb, SQ = [c.sb([128, T], BF16, es=s) for _ in range(6)]
        Dd = c.sb([128, 33], F32, es=s)
        S = c.sb([128, 128], F32, es=s)
        Sbf = c.sb([128, 33, 128], BF16, es=s)
        ktok = [c.sb([128, 128], BF16, es=s) for _ in range(2)]
        vtok = [c.sb([128, 128], BF16, es=s) for _ in range(2)]
        Asb = [c.sb([128, 128], BF16, es=s) for _ in range(2)]
        for h in range(4):
            c.dma('sp', Qf[:, :], Z.t[1536 + h * 128:1536 + (h + 1) * 128, :], reads=[Z], writes=[Qf])
            c.dma('sp', Fz[:, :], Z.t[2048 + h * 128:2048 + (h + 1) * 128, :], reads=[Z], writes=[Fz])
            c.dma('sp', Vf[:, :], Z.t[2560 + h * 128:2560 + (h + 1) * 128, :], reads=[Z], writes=[Vf])
            c.dma('sp', OG[:, :], Z.t[3072 + h * 128:3072 + (h + 1) * 128, :], reads=[Z], writes=[OG])
            c.op('act', lambda e: e.activation(SIG[:, :], Fz[:, :], AF.Sigmoid), [Fz], [SIG])
            c.op('dve', lambda e: e.tensor_scalar(Gg[:, :], SIG[:, :], OML[:, h:h + 1], LB[:, h:h + 1], ALU.mult, ALU.add), [SIG, OML, LB], [Gg])
            c.op('act', lambda e: e.activation(Gg[:, :], Gg[:, :], AF.Ln), [Gg], [Gg])
            c.op('dve', lambda e: e.tensor_scalar(Kk[:, :], SIG[:, :], NOML[:, h:h + 1], OML[:, h:h + 1], ALU.mult, ALU.add), [SIG, OML, NOML], [Kk])
            c.op('dve', lambda e: e.tensor_tensor_scan(Bc[:, :], MK[:, :], Gg[:, :], 0.0, ALU.mult, ALU.add), [MK, Gg], [Bc])
            c.op('act', lambda e: e.activation(TMP[:, :], Bc[:, :], AF.Exp), [Bc], [TMP])
            c.op('dve', lambda e: e.tensor_tensor(QT[:, :], Qf[:, :], TMP[:, :], ALU.mult), [Qf, TMP], [QT])
            c.op('dve', lambda e: e.tensor_scalar(TMP[:, :], Bc[:, :], -1.0, 80.0, ALU.mult, ALU.min), [Bc], [TMP])
            c.op('act', lambda e: e.activation(TMP[:, :], TMP[:, :], AF.Exp), [TMP], [TMP])
            c.op('dve', lambda e: e.tensor_tensor(KT[:, :], Kk[:, :], TMP[:, :], ALU.mult), [Kk, TMP], [KT])
            c.op('act', lambda e: e.activation(Dd[:, 0:1], Bc[:, 15:16], AF.Exp), [Bc], [Dd])
            c.op('act', lambda e: e.activation(Dd[:, 1:33], Bc[:, 79:T:64], AF.Exp), [Bc], [Dd])
            chunks = [(0, 16)] + [(16 + 64 * n, 64) for n in range(32)]
            for (c0, cl) in chunks:
                c.op('act', lambda e: e.activation(EK[:, c0:c0 + cl], Bc[:, c0:c0 + cl], AF.Exp, bias=Bc[:, c0 + cl - 1:c0 + cl], scale=-1.0), [Bc], [EK])
            c.op('dve', lambda e: e.tensor_tensor(KH[:, :], Kk[:, :], EK[:, :], ALU.mult), [Kk, EK], [KH])
            c.op('act', lambda e: e.copy(VB[:, :], Vf[:, :]), [Vf], [VB])
            c.op('dve', lambda e: e.memset(S[:, :], 0.0), [], [S])
            n_ = 0
            for bi, (t0, L) in enumerate(BLKS):
                kt = ktok[bi % 2]; vt = vtok[bi % 2]; ab = Asb[bi % 2]
                pb1 = psb()
                c.mm_raw([lambda e: e.transpose(pb1[0:L, 0:128], KH[:, t0:t0 + L], idb[:, :])], [KH, idb], pb1)
                c.op('act', lambda e: e.copy(kt[0:L, :], pb1[0:L, 0:128]), [pb1], [kt])
                pb2 = psb()
                c.mm_raw([lambda e: e.transpose(pb2[0:L, 0:128], VB[:, t0:t0 + L], idb[:, :])], [VB, idb], pb2)
                c.op('act', lambda e: e.copy(vt[0:L, :], pb2[0:L, 0:128]), [pb2], [vt])
                cks = [(0, 16)] if bi == 0 else [(0, 64), (64, 64)]
                n_first = n_
                for (r0, rl) in cks:
                    ps = psf()
                    c.mm([(ps[:, 0:128], kt[r0:r0 + rl, :], vt[r0:r0 + rl, :])], [kt, vt], ps)
                    c.op('act', lambda e: e.copy(Sbf[:, n_, :], S[:, :]), [S], [Sbf])
                    c.op('dve', lambda e: e.scalar_tensor_tensor(S[:, :], S[:, :], Dd[:, n_:n_ + 1], ps[:, 0:128], ALU.mult, ALU.add), [S, Dd, ps], [S])
                    n_ += 1
                psA = psf()
                c.mm([(psA[0:L, 0:L], KT[:, t0:t0 + L], QT[:, t0:t0 + L])], [KT, QT], psA)
                c.op('dve', lambda e: e.tensor_tensor(ab[0:L, 0:L], psA[0:L, 0:L], HM[0:L, 0:L], ALU.mult), [psA, HM], [ab])
                pso = psf()
                items = [(pso[:, 0:L], vt[0:L, :], ab[0:L, 0:L])]
                for ci, (r0, rl) in enumerate(cks):
                    items.append((pso[:, r0:r0 + rl], Sbf[:, n_first + ci, :], QT[:, t0 + r0:t0 + r0 + rl]))
                c.mm(items, [vt, ab, Sbf, QT], pso)
                c.op('act', lambda e: e.copy(Oo[:, t0:t0 + L], pso[:, 0:L]), [pso], [Oo])
            c.op('act', lambda e: e.activation(SQ[:, :], Oo[:, :], AF.Square), [Oo], [SQ])
            for (t0, n) in CH512:
                ps = psf()
                c.mm([(ps[:, 0:n], ones_b[:, :], SQ[:, t0:t0 + n])], [ones_b, SQ], ps)
                c.op('act', lambda e: e.activation(TMP[:, t0:t0 + n], ps[:, 0:n], AF.Sqrt, bias=epsb[:, 0:1], scale=1.0 / 128), [ps, epsb], [TMP])
            c.op('dve', lambda e: e.reciprocal(TMP[:, :], TMP[:, :]), [TMP], [TMP])
            c.op('dve', lambda e: e.scalar_tensor_tensor(Oo[:, :], Oo[:, :], HNW[:, 0:1], TMP[:, :], ALU.mult, ALU.mult), [Oo, HNW, TMP], [Oo])
            c.op('act', lambda e: e.activation(OG[:, :], OG[:, :], AF.Silu), [OG], [OG])
            c.op('dve', lambda e: e.tensor_tensor(Yb[:, :], Oo[:, :], OG[:, :], ALU.mult), [Oo, OG], [Yb])
            c.dma('sp', YS.t[1024 + h * 128:1024 + (h + 1) * 128, :], Yb[:, :], reads=[Yb], writes=[YS])


def mixer_attn(c, g, I, l, Z, YS, psf, psb, idb, RB, EB0, EB1, EBM0, EBMM, epsb, lambda_init):
    SCALE = 0.125
    with ExitStack() as s:
        DL = c.sb([128, 4, 64], F32, es=s)
        c.dma('sp', DL[:, :, :], I['diff_lambda'].t[l, :, :].rearrange("a b -> (a b)").partition_broadcast(128), reads=[I['diff_lambda']], writes=[DL])
        PR = c.sb([128, 2, 64], F32, es=s); SS = c.sb([128, 2], F32, es=s); LAMN = c.sb([128, 1], F32, es=s)
        c.op('dve', lambda e: e.tensor_tensor(PR[:, 0, :], DL[:, 0, :], DL[:, 1, :], ALU.mult), [DL], [PR])
        c.op('dve', lambda e: e.tensor_tensor(PR[:, 1, :], DL[:, 2, :], DL[:, 3, :], ALU.mult), [DL], [PR])
        c.op('dve', lambda e: e.reduce_sum(SS[:, :], PR[:, :, :], AX.X), [PR], [SS])
        c.op('act', lambda e: e.activation(SS[:, :], SS[:, :], AF.Exp), [SS], [SS])
        c.op('dve', lambda e: e.tensor_tensor(LAMN[:, :], SS[:, 1:2], SS[:, 0:1], ALU.subtract), [SS], [LAMN])
        c.op('dve', lambda e: e.tensor_scalar_add(LAMN[:, :], LAMN[:, :], -lambda_init), [LAMN], [LAMN])
        SUBW = c.sb([128, 128], F32, es=s)
        c.dma('sp', SUBW[:, :], I['diff_subln'].t[l, :].partition_broadcast(128), reads=[I['diff_subln']], writes=[SUBW])
        c.op('dve', lambda e: e.tensor_scalar_mul(SUBW[:, :], SUBW[:, :], 1.0 - lambda_init), [SUBW], [SUBW])
        Qf = c.sb([128, T], F32, es=s); Kf = c.sb([128, T], F32, es=s); Vf = c.sb([128, T], F32, es=s)
        qb = c.sb([128, T], BF16, es=s); kb = c.sb([128, T], BF16, es=s); vb = c.sb([128, T], BF16, es=s)
        V1 = c.sb([128, 17, 132], BF16, es=s)
        c.op('dve', lambda e: e.memset(V1[:, :, 128:129], 1.0), [], [V1])
        YD = c.sb([128, T], BF16, es=s)
        Eb = [c.sb([128, 512], BF16, es=s) for _ in range(3)]
        Om = [c.sb([128, 4, 132], F32, es=s) for _ in range(2)]
        rr = [c.sb([128, 2], F32, es=s) for _ in range(2)]
        ot = [c.sb([128, 128], F32, es=s) for _ in range(2)]
        jk = [c.sb([128, 128], F32, es=s) for _ in range(2)]
        s1 = [c.sb([128, 1], F32, es=s) for _ in range(2)]
        yb = [c.sb([128, 128], BF16, es=s) for _ in range(2)]
        PACC = g.PACC
        g.ei = 0
        g.ci = 0

        def combine(h, Lq, j, t0):
            i = g.ci % 2; g.ci += 1
            r = rr[i]; o = ot[i]; sq = jk[i]; s_ = s1[i]; y = yb[i]
            c.op('dve', lambda e: e.reciprocal(r[0:Lq, 0:1], Om[0][0:Lq, j, 128:129]), [Om[0]], [r])
            c.op('dve', lambda e: e.reciprocal(r[0:Lq, 1:2], Om[1][0:Lq, j, 128:129]), [Om[1]], [r])
            c.op('dve', lambda e: e.tensor_tensor(r[0:Lq, 1:2], r[0:Lq, 1:2], LAMN[0:Lq, 0:1], ALU.mult), [r, LAMN], [r])
            c.op('dve', lambda e: e.tensor_scalar_mul(o[0:Lq, :], Om[0][0:Lq, j, 0:128], r[0:Lq, 0:1]), [Om[0], r], [o])
            c.op('dve', lambda e: e.scalar_tensor_tensor(o[0:Lq, :], Om[1][0:Lq, j, 0:128], r[0:Lq, 1:2], o[0:Lq, :], ALU.mult, ALU.add), [Om[1], r, o], [o])
            c.op('act', lambda e: e.activation(sq[0:Lq, :], o[0:Lq, :], AF.Square, accum_out=s_[0:Lq, 0:1]), [o], [sq, s_])
            c.op('act', lambda e: e.activation(s_[0:Lq, :], s_[0:Lq, :], AF.Sqrt, bias=epsb[0:Lq, 0:1], scale=1.0 / 128), [s_, epsb], [s_])
            c.op('dve', lambda e: e.reciprocal(s_[0:Lq, :], s_[0:Lq, :]), [s_], [s_])
            c.op('dve', lambda e: e.scalar_tensor_tensor(y[0:Lq, :], o[0:Lq, :], s_[0:Lq, 0:1], SUBW[0:Lq, :], ALU.mult, ALU.mult), [o, s_, SUBW], [y])
            pb = psb()
            c.mm_raw([lambda e: e.transpose(pb[:, 0:Lq], y[0:Lq, :], idb[0:Lq, 0:Lq])], [y, idb], pb)
            c.op('act', lambda e: e.copy(YD[:, t0:t0 + Lq], pb[:, 0:Lq]), [pb], [YD])

        for h in range(4):
            c.dma('sp', Qf[:, :], Z.t[3584 + h * 128:3584 + (h + 1) * 128, :], reads=[Z], writes=[Qf])
            c.dma('sp', Kf[:, :], Z.t[4096 + h * 128:4096 + (h + 1) * 128, :], reads=[Z], writes=[Kf])
            c.dma('sp', Vf[:, :], Z.t[4608 + h * 128:4608 + (h + 1) * 128, :], reads=[Z], writes=[Vf])
            c.op('act', lambda e: e.copy(qb[:, :], Qf[:, :]), [Qf], [qb])
            c.op('act', lambda e: e.copy(kb[:, :], Kf[:, :]), [Kf], [kb])
            c.op('act', lambda e: e.copy(vb[:, :], Vf[:, :]), [Vf], [vb])
            for bi, (t0, L) in enumerate(BLKS):
                pb = psb()
                c.mm_raw([lambda e: e.transpose(pb[0:L, 0:128], vb[:, t0:t0 + L], idb[:, :])], [vb, idb], pb)
                c.op('act', lambda e: e.copy(V1[0:L, bi, 0:128], pb[0:L, 0:128]), [pb], [V1])
            cb = RB[:, 31 * 4 + h:31 * 4 + h + 1]
            for m in range(2):
                ms = slice(m * 64, (m + 1) * 64)
                ps = psf(4, 6)
                c.mm([(ps[0:16, 0:16], kb[ms, 0:16], qb[ms, 0:16])], [kb, qb], ps)
                eb = Eb[g.ei % 3]; g.ei += 1
                c.op('act', lambda e: e.activation(eb[0:16, 0:16], ps[0:16, 0:16], AF.Exp, scale=SCALE), [ps], [eb])
                c.op('dve', lambda e: e.tensor_tensor(eb[0:16, 0:16], eb[0:16, 0:16], EBMM[0:16, h, :], ALU.mult), [eb, EBMM], [eb])
                pa = PACC[0]
                c.mm([(pa[0:16, 0:129], eb[0:16, 0:16], V1[0:16, 0, 0:129])], [eb, V1], pa)
                c.op('act', lambda e: e.copy(Om[m][0:16, 0, 0:129], pa[0:16, 0:129]), [pa], [Om[m]])
            combine(h, 16, 0, 0)
            for qc in range(4):
                ct0 = 16 + 512 * qc
                for m in range(2):
                    ms = slice(m * 64, (m + 1) * 64)
                    ps = psf(4, 6)
                    c.mm([(ps[0:16, 0:512], kb[ms, 0:16], qb[ms, ct0:ct0 + 512])], [kb, qb], ps)
                    eb = Eb[g.ei % 3]; g.ei += 1
                    if qc == 0:
                        c.op('act', lambda e: e.activation(eb[0:16, 0:128], ps[0:16, 0:128], AF.Exp, scale=SCALE), [ps], [eb])
                        c.op('dve', lambda e: e.tensor_tensor(eb[0:16, 0:128], eb[0:16, 0:128], EBM0[0:16, h, :], ALU.mult), [eb, EBM0], [eb])
                        c.op('act', lambda e: e.activation(eb[0:16, 128:512], ps[0:16, 128:512], AF.Exp, scale=SCALE, bias=cb[0:16, :]), [ps, RB], [eb])
                    else:
                        c.op('act', lambda e: e.activation(eb[0:16, 0:512], ps[0:16, 0:512], AF.Exp, scale=SCALE, bias=cb[0:16, :]), [ps, RB], [eb])
                    for j in range(4):
                        c.mm([(PACC[j][:, 0:129], eb[0:16, j * 128:(j + 1) * 128], V1[0:16, 0, 0:129])], [eb, V1], PACC[j], start=True, stop=False)
                    for kbi in range(4 * qc + 4):
                        kt0 = 16 + 128 * kbi
                        j0 = max(kbi - 4 * qc, 0)
                        ncols = (4 - j0) * 128
                        q0 = ct0 + j0 * 128
                        ps = psf(4, 6)
                        c.mm([(ps[:, 0:ncols], kb[ms, kt0:kt0 + 128], qb[ms, q0:q0 + ncols])], [kb, qb], ps)
                        eb = Eb[g.ei % 3]; g.ei += 1
                        nnear = 0
                        for j in range(j0, 4):
                            dl = 4 * qc + j - kbi
                            if dl <= 1:
                                nnear += 1
                        if nnear > 0:
                            c.op('act', lambda e: e.activation(eb[:, 0:nnear * 128], ps[:, 0:nnear * 128], AF.Exp, scale=SCALE), [ps], [eb])
                        if ncols > nnear * 128:
                            c.op('act', lambda e: e.activation(eb[:, nnear * 128:ncols], ps[:, nnear * 128:ncols], AF.Exp, scale=SCALE, bias=cb), [ps, RB], [eb])
                        for j in range(j0, 4):
                            dl = 4 * qc + j - kbi
                            co = (j - j0) * 128
                            if dl == 0:
                                c.op('dve', lambda e: e.tensor_tensor(eb[:, co:co + 128], eb[:, co:co + 128], EB0[:, h, :], ALU.mult), [eb, EB0], [eb])
                            elif dl == 1:
                                c.op('dve', lambda e: e.tensor_tensor(eb[:, co:co + 128], eb[:, co:co + 128], EB1[:, h, :], ALU.mult), [eb, EB1], [eb])
                        for j in range(j0, 4):
                            dl = 4 * qc + j - kbi
                            co = (j - j0) * 128
                            c.mm([(PACC[j][:, 0:129], eb[:, co:co + 128], V1[:, kbi + 1, 0:129])], [eb, V1], PACC[j], start=False, stop=(dl == 0))
                    for j in range(4):
                        c.op('act', lambda e: e.copy(Om[m][:, j, 0:129], PACC[j][:, 0:129]), [PACC[j]], [Om[m]])
                for j in range(4):
                    combine(h, 128, j, ct0 + j * 128)
            c.dma('sp', YS.t[1536 + h * 128:1536 + (h + 1) * 128, :], YD[:, :], reads=[YD], writes=[YS])

D = 2048
T = 2064
NM = 16
KC = 16
FH = 5632
NIN = 13312
EPS = 1e-6
SUP = 1032
SUBS = [(0, 344), (344, 344), (688, 344)]
CH512 = [(0, 512), (512, 512), (1024, 512), (1536, 512), (2048, 16)]
BLKS = [(0, 16)] + [(16 + 128 * j, 128) for j in range(16)]
DEPTH = 2


def t5_onehot():
    n = np.arange(0, 272)
    max_exact = 16
    nf = np.maximum(n, 1).astype(np.float32)
    large = max_exact + (np.log(nf / np.float32(max_exact)) / np.float32(math.log(128 / max_exact)) * np.float32(32 - max_exact)).astype(np.int32)
    large = np.minimum(large, 31)
    b = np.where(n < max_exact, n, large)
    oh = np.zeros((32, 400), np.float32)
    oh[b, 128 + n] = 1.0
    return oh


class K:
    pass


def build_program(dbg=(), stop_after=None, nlayers=DEPTH):
    nc = bass.Bass("TRN2", target_bir_lowering=False)
    es = ExitStack()
    with es:
        c = Ctx(nc, es)
        g = K()
        I = {}
        def inp(name, shape):
            I[name] = c.dram(name, shape, F32, kind="ExternalInput")
        inp('x', [2048, D]); inp('meta_tokens', [NM, D]); inp('rel_bias', [32, 4]); inp('hgrn_lower_bounds', [2, 512])
        for n_ in ['norm_mix_pre', 'norm_mix_post', 'norm_ffn_pre', 'norm_ffn_post']:
            inp(n_, [2, D])
        inp('w_in', [2, D, NIN]); inp('lru_conv_w', [2, 4, 512]); inp('lru_conv_b', [2, 512])
        inp('lru_w_a', [2, 4, 128, 128]); inp('lru_b_a', [2, 512]); inp('lru_w_x', [2, 4, 128, 128]); inp('lru_b_x', [2, 512])
        inp('lru_lambda', [2, 512]); inp('pool_w', [2, 4, 128, 128]); inp('pool_scale', [2, 512]); inp('hgrn_norm', [2, 128])
        inp('diff_lambda', [2, 4, 64]); inp('diff_subln', [2, 128]); inp('w_branch', [2, 4, 512, D]); inp('w_out', [2, D, D])
        inp('ffn_w_gu', [2, D, 2 * FH]); inp('ffn_w_down', [2, FH, D]); inp('t5oh', [32, 400])
        OUT = c.dram('out', [2048, D], F32, kind="ExternalOutput")

        def scr(name, shape, dt):
            return c.dram(name, shape, dt, kind=("ExternalOutput" if name in dbg else "Internal"))
        H = scr('H', [D, T], F32)
        HN = scr('HN', [D, T], BF16)
        Z = scr('Z', [5120, T], F32)
        YS = scr('YS', [D, T], BF16)
        MIX = scr('MIX', [D, T], F32)
        BD = scr('BD', [4, 400], F32)
        Hv = H.t.rearrange("(k p) t -> p k t", p=128)
        HNv = HN.t.rearrange("(k p) t -> p k t", p=128)
        YSv = YS.t.rearrange("(k p) t -> p k t", p=128)
        MIXv = MIX.t.rearrange("(k p) t -> p k t", p=128)

        PSF = [c.ps([128, 512], F32, name='psf%d' % i) for i in range(6)]
        PSB = [c.ps([128, 1024], BF16, name='psb%d' % i) for i in range(2)]
        g.pi = 0
        def psf(lo=0, hi=6):
            g.pi += 1
            return PSF[lo + g.pi % (hi - lo)]
        g.pb = 0
        g.PACC = PSF[0:4]
        def psb():
            g.pb += 1
            return PSB[g.pb % 2]

        ones_f = c.sb([128, 128], F32, 'ones_f')
        c.op('pool', lambda e: e.memset(ones_f[:, :], 1.0), [], [ones_f])
        ones_b = c.sb([128, 128], BF16, 'ones_b')
        c.op('pool', lambda e: e.memset(ones_b[:, :], 1.0), [], [ones_b])
        idf = c.sb([128, 128], F32, 'idf')
        c.op('pool', lambda e: e.affine_select(idf[:, :], ones_f[:, :], [[-1, 128]], ALU.is_equal, 0.0, base=0, channel_multiplier=1), [ones_f], [idf])
        idb = c.sb([128, 128], BF16, 'idb')
        J128 = c.sb([128, 128], F32, 'J128')
        c.op('pool', lambda e: e.affine_select(J128[:, :], ones_f[:, :], [[1, 128]], ALU.is_equal, 0.0, base=-127, channel_multiplier=1), [ones_f], [J128])
        J16 = c.sb([16, 16], F32, 'J16')
        c.op('pool', lambda e: e.affine_select(J16[:, :], ones_f[0:16, 0:16], [[1, 16]], ALU.is_equal, 0.0, base=-15, channel_multiplier=1), [ones_f], [J16])
        epsb = c.sb([128, 1], F32, 'epsb')
        c.op('pool', lambda e: e.memset(epsb[:, :], EPS), [], [epsb])

        MK = c.sb([128, T], F32, 'MK')
        c.op('pool', lambda e: e.memset(MK[:, :], 1.0), [], [MK])
        c.op('pool', lambda e: e.memset(MK[:, 0:1], 0.0), [], [MK])
        c.op('pool', lambda e: e.memset(MK[:, 16:T:64], 0.0), [], [MK])
        HM = c.sb([128, 128], F32, 'HM')
        c.op('pool', lambda e: e.affine_select(HM[:, :], ones_f[:, :], [[1, 128]], ALU.is_ge, 0.0, base=0, channel_multiplier=-1), [ones_f], [HM])
        c.op('pool', lambda e: e.memset(HM[0:64, 64:128], 0.0), [], [HM])
        IC = c.sb([128, 4, 16], F32, 'IC')
        ioti = c.sb([128, 16], mybir.dt.int32, 'ioti')
        c.op('pool', lambda e: e.iota(ioti[:, :], [[1, 16]], base=1, channel_multiplier=0), [], [ioti])
        iotf = c.sb([128, 16], F32, 'iotf')

        CM = c.sb([128, 128], F32, 'CM')
        c.op('pool', lambda e: e.affine_select(CM[:, :], ones_f[:, :], [[1, 128]], ALU.is_ge, 0.0, base=0, channel_multiplier=-1), [ones_f], [CM])
        c.barrier()
        c.op('dve', lambda e: e.tensor_copy(idb[:, :], idf[:, :]), [idf], [idb])
        c.op('dve', lambda e: e.tensor_copy(iotf[:, :], ioti[:, :]), [ioti], [iotf])
        for gi in range(4):
            c.op('dve', lambda e, gi=gi: e.tensor_scalar_min(IC[:, gi, :], iotf[:, :], float(2 ** (gi + 1))), [iotf], [IC])
        c.op('dve', lambda e: e.reciprocal(IC[:, :, :], IC[:, :, :]), [IC], [IC])

        def vecload(dst_ap, src_ap, dstb, srcb):
            c.dma('sp', dst_ap, src_ap, reads=[srcb], writes=[dstb], allow_slow_non_contiguous=True)

        NW = c.sb([128, 4, 2, 16], F32, 'NW')
        for i_, n_ in enumerate(['norm_mix_pre', 'norm_mix_post', 'norm_ffn_pre', 'norm_ffn_post']):
            for l in range(DEPTH):
                vecload(NW[:, i_, l, :], I[n_].t[l, :].rearrange("(k p) -> p k", p=128), NW, I[n_])

        def norm_sub(sc, src_v, srcb, t0, n, wk, l, dst_fn, dstb, resid=False):
            xt = sc.xt[sc.i % 2]; sq = sc.sq[sc.i % 2]; rs = sc.rs[sc.i % 2]; sc.i += 1
            c.dma('sp', xt[:, :, 0:n], src_v[:, :, t0:t0 + n], reads=[srcb], writes=[xt])
            c.op('act', lambda e: e.activation(sq[:, :, 0:n], xt[:, :, 0:n], AF.Square), [xt], [sq])
            ps = psf(4, 6)
            c.mm([(ps[:, 0:n], ones_b[:, :], sq[:, k, 0:n]) for k in range(KC)], [ones_b, sq], ps)
            c.op('act', lambda e: e.activation(rs[:, 0:n], ps[:, 0:n], AF.Sqrt, bias=epsb[:, 0:1], scale=1.0 / D), [ps, epsb], [rs])
            c.op('dve', lambda e: e.reciprocal(rs[:, 0:n], rs[:, 0:n]), [rs], [rs])
            if not resid:
                for k in range(KC):
                    c.op('dve', lambda e, k=k: e.scalar_tensor_tensor(dst_fn(k), xt[:, k, 0:n], NW[:, wk, l, k:k + 1], rs[:, 0:n], ALU.mult, ALU.mult), [xt, NW, rs], [dstb])
            else:
                ht = sc.ht[sc.i % 2]
                c.dma('sp', ht[:, :, 0:n], Hv[:, :, t0:t0 + n], reads=[H], writes=[ht])
                for k in range(KC):
                    c.op('dve', lambda e, k=k: e.scalar_tensor_tensor(xt[:, k, 0:n], xt[:, k, 0:n], NW[:, wk, l, k:k + 1], rs[:, 0:n], ALU.mult, ALU.mult), [xt, NW, rs], [xt])
                c.op('dve', lambda e: e.tensor_tensor(ht[:, :, 0:n], ht[:, :, 0:n], xt[:, :, 0:n], ALU.add), [ht, xt], [ht])
                c.dma('sp', Hv[:, :, t0:t0 + n], ht[:, :, 0:n], reads=[ht], writes=[H])

        class NormScr:
            def __init__(self, s, resid=False):
                self.i = 0
                self.xt = [c.sb([128, KC, 344], F32, es=s) for _ in range(2)]
                self.sq = [c.sb([128, KC, 344], BF16, es=s) for _ in range(2)]
                self.rs = [c.sb([128, 344], F32, es=s) for _ in range(2)]
                if resid:
                    self.ht = [c.sb([128, KC, 344], F32, es=s) for _ in range(2)]

        class GemmScr:
            def __init__(self, s, kc, wmax):
                self.i = 0
                self.kc = kc
                self.st = [c.sb([128, kc, wmax], F32, es=s) for _ in range(2)]
                self.wb = [c.sb([128, kc, wmax], BF16, es=s) for _ in range(2)]

        def gemm(gs, W2d, Wb, groups, x_fn, xbufs, subs, epi):
            kc = gs.kc
            def load(gi):
                col0, width = groups[gi]
                st = gs.st[gs.i % 2]; wb = gs.wb[gs.i % 2]; gs.i += 1
                c.dma('sp', st[:, :, 0:width], W2d[:, col0:col0 + width].rearrange("(k p) n -> p k n", p=128), reads=[Wb], writes=[st])
                c.op('dve', lambda e: e.tensor_copy(wb[:, :, 0:width], st[:, :, 0:width]), [st], [wb])
                return wb
            nxt = load(0)
            for gi, (col0, width) in enumerate(groups):
                wb = nxt
                if gi + 1 < len(groups):
                    nxt = load(gi + 1)
                for bi in range(width // 128):
                    for si, (t0, n) in enumerate(subs):
                        ps = psf(0, 4)
                        c.mm([(ps[:, 0:n], wb[:, k, bi * 128:(bi + 1) * 128], x_fn(k, t0, n)) for k in range(kc)], [wb] + xbufs, ps)
                        epi(ps, col0 + bi * 128, si, t0, n)

        with ExitStack() as s:
            xin = [c.sb([128, D], F32, es=s) for _ in range(2)]
            stg = [c.sb([128, KC, 128], F32, es=s) for _ in range(2)]
            for bi, (t0, L) in enumerate(BLKS):
                xi = xin[bi % 2]; sg = stg[bi % 2]
                if bi == 0:
                    c.dma('sp', xi[0:L, :], I['meta_tokens'].t[:, :], reads=[I['meta_tokens']], writes=[xi])
                else:
                    c.dma('sp', xi[0:L, :], I['x'].t[t0 - 16:t0 - 16 + L, :], reads=[I['x']], writes=[xi])
                for kg in range(4):
                    ps = psf(0, 4)
                    c.mm_raw([(lambda e, j=j: e.transpose(ps[:, j * 128:j * 128 + L], xi[0:L, (4 * kg + j) * 128:(4 * kg + j + 1) * 128], idf[0:L, 0:L])) for j in range(4)], [xi, idf], ps)
                    c.op('act', lambda e: e.copy(sg[:, 4 * kg:4 * kg + 4, 0:L], ps[:, :].rearrange("p (j t) -> p j t", j=4)[:, :, 0:L]), [ps], [sg])
                c.dma('sp', Hv[:, :, t0:t0 + L], sg[:, :, 0:L], reads=[sg], writes=[H])
        c.barrier()

        RB = c.sb([128, 128], F32, 'RB')
        c.dma('sp', RB[:, :], I['rel_bias'].t.rearrange("a b -> (a b)").partition_broadcast(128), reads=[I['rel_bias']], writes=[RB])
        EB0 = c.sb([128, 4, 128], BF16, 'EB0'); EB1 = c.sb([128, 4, 128], BF16, 'EB1')
        EBM0 = c.sb([16, 4, 128], BF16, 'EBM0'); EBMM = c.sb([16, 4, 16], BF16, 'EBMM')
        with ExitStack() as s:
            rbs = c.sb([32, 4], F32, es=s); ohs = c.sb([32, 400], F32, es=s); bds = c.sb([4, 400], F32, es=s)
            c.dma('sp', rbs[:, :], I['rel_bias'].t[:, :], reads=[I['rel_bias']], writes=[rbs])
            c.dma('sp', ohs[:, :], I['t5oh'].t[:, :], reads=[I['t5oh']], writes=[ohs])
            ps = psf()
            c.mm([(ps[0:4, 0:400], rbs[:, :], ohs[:, :])], [rbs, ohs], ps)
            c.op('dve', lambda e: e.tensor_copy(bds[:, :], ps[0:4, 0:400]), [ps], [bds])
            c.dma('sp', BD.t[:, :], bds[:, :], reads=[bds], writes=[BD])
            hk = [c.sb([128, 128], F32, es=s) for _ in range(2)]
            ef = [c.sb([128, 128], F32, es=s) for _ in range(2)]
            i_ = 0
            for h in range(4):
                for (dst, base, P, N, causal) in [(EB0, 1, 128, 128, True), (EB1, 129, 128, 128, False), (EBM0, 129, 16, 128, False), (EBMM, 113, 16, 16, True)]:
                    hb = hk[i_ % 2]; eb = ef[i_ % 2]; i_ += 1
                    c.dma('sp', hb[0:P, 0:N], bass.AP(BD.t.tensor, h * 400 + base, [[1, P], [1, N]]), reads=[BD], writes=[hb])
                    ps = psf()
                    Jm = J128 if P == 128 else J16
                    c.mm([(ps[0:P, 0:N], Jm[0:P, 0:P], hb[0:P, 0:N])], [Jm, hb], ps)
                    c.op('act', lambda e: e.activation(eb[0:P, 0:N], ps[0:P, 0:N], AF.Exp), [ps], [eb])
                    if causal:
                        c.op('dve', lambda e: e.tensor_tensor(dst[0:P, h, 0:N], eb[0:P, 0:N], CM[0:P, 0:N], ALU.mult), [eb, CM], [dst])
                    else:
                        c.op('dve', lambda e: e.tensor_copy(dst[0:P, h, 0:N], eb[0:P, 0:N]), [eb], [dst])
        c.barrier()

        for l in range(nlayers):
            lambda_init = 0.8 - 0.6 * math.exp(-0.3 * l)
            Win = I['w_in'].t[l, :, :]
            for sp_ in range(2):
                s0 = sp_ * SUP
                with ExitStack() as s:
                    hnT = c.sb([128, KC, SUP], BF16, es=s)
                    nsc = NormScr(s)
                    for (o, n) in SUBS:
                        norm_sub(nsc, Hv, H, s0 + o, n, 0, l, lambda k, o=o, n=n: hnT[:, k, o:o + n], hnT)
                    c.dma('sp', HNv[:, :, s0:s0 + SUP], hnT[:, :, :], reads=[hnT], writes=[HN])
                    gs = GemmScr(s, KC, 256)
                    zo = [c.sb([128, 344], F32, es=s) for _ in range(3)]
                    zi = [0]
                    def epi(ps, col, si, t0, n):
                        ob = zo[zi[0] % 3]; zi[0] += 1
                        c.op('act', lambda e: e.copy(ob[:, 0:n], ps[:, 0:n]), [ps], [ob])
                        c.dma('sp', Z.t[col:col + 128, s0 + t0:s0 + t0 + n], ob[:, 0:n], reads=[ob], writes=[Z])
                    gemm(gs, Win, I['w_in'], [(cb * 256, 256) for cb in range(20)], lambda k, t0, n: hnT[:, k, t0:t0 + n], [hnT], SUBS, epi)
                c.barrier()
            if stop_after == 'p1':
                break

            mixer_lru(c, g, I, l, Z, YS, psf)
            c.barrier()
            mixer_pool(c, g, I, l, Z, YS, psf, IC)
            c.barrier()
            mixer_hgrn(c, g, I, l, Z, YS, psf, psb, MK, HM, idb, ones_b, epsb)
            c.barrier()
            mixer_attn(c, g, I, l, Z, YS, psf, psb, idb, RB, EB0, EB1, EBM0, EBMM, epsb, lambda_init)
            c.barrier()
            if stop_after == 'p2':
                break

            for sp_ in range(2):
                s0 = sp_ * SUP
                with ExitStack() as s3:
                    mergedT = c.sb([128, KC, SUP], BF16, es=s3)
                    with ExitStack() as s:
                        hnT = c.sb([128, KC, SUP], BF16, es=s)
                        ysT = c.sb([128, KC, SUP], BF16, es=s)
                        c.dma('sp', hnT[:, :, :], HNv[:, :, s0:s0 + SUP], reads=[HN], writes=[hnT])
                        c.dma('sp', ysT[:, :, :], YSv[:, :, s0:s0 + SUP], reads=[YS], writes=[ysT])
                        gst = [c.sb([128, KC, 128], F32, es=s) for _ in range(3)]
                        wg = [c.sb([128, 4, KC, 128], BF16, es=s) for _ in range(2)]
                        pst = [c.sb([128, 4, 128], F32, es=s) for _ in range(3)]
                        wp = [c.sb([128, 4, 4, 128], BF16, es=s) for _ in range(2)]
                        sgb = [c.sb([128, 344], F32, es=s) for _ in range(3)]
                        tmpb = [c.sb([128, 344], F32, es=s) for _ in range(3)]
                        acc = [c.sb([128, 344], F32, es=s) for _ in range(3)]
                        ii = 0
                        g.li = 0
                        def load3(db):
                            wgb = wg[db % 2]; wpb = wp[db % 2]
                            for k in range(4):
                                st = gst[g.li % 3]; st2 = pst[g.li % 3]; g.li += 1
                                col = 5120 + k * D + db * 128
                                c.dma('sp', st[:, :, :], Win[:, col:col + 128].rearrange("(k p) n -> p k n", p=128), reads=[I['w_in']], writes=[st])
                                c.op('dve', lambda e: e.tensor_copy(wgb[:, k, :, :], st[:, :, :]), [st], [wgb])
                                c.dma('sp', st2[:, :, :], I['w_branch'].t[l, k, :, db * 128:(db + 1) * 128].rearrange("(k p) n -> p k n", p=128), reads=[I['w_branch']], writes=[st2])
                                c.op('dve', lambda e: e.tensor_copy(wpb[:, k, :, :], st2[:, :, :]), [st2], [wpb])
                        load3(0)
                        for db in range(16):
                            wgb = wg[db % 2]; wpb = wp[db % 2]
                            if db + 1 < 16:
                                load3(db + 1)
                            for si, (o, n) in enumerate(SUBS):
                                ac = acc[si]
                                for k in range(4):
                                    sg = sgb[ii % 3]; tm = tmpb[ii % 3]; ii += 1
                                    ps = psf(0, 4)
                                    c.mm([(ps[:, 0:n], wgb[:, k, kk, :], hnT[:, kk, o:o + n]) for kk in range(KC)], [wgb, hnT], ps)
                                    c.op('act', lambda e: e.activation(sg[:, 0:n], ps[:, 0:n], AF.Sigmoid), [ps], [sg])
                                    ps2 = psf(0, 4)
                                    c.mm([(ps2[:, 0:n], wpb[:, k, kk, :], ysT[:, 4 * k + kk, o:o + n]) for kk in range(4)], [wpb, ysT], ps2)
                                    if k == 0:
                                        c.op('dve', lambda e: e.tensor_tensor(ac[:, 0:n], sg[:, 0:n], ps2[:, 0:n], ALU.mult), [sg, ps2], [ac])
                                    else:
                                        c.op('dve', lambda e: e.tensor_tensor(tm[:, 0:n], sg[:, 0:n], ps2[:, 0:n], ALU.mult), [sg, ps2], [tm])
                                        if k < 3:
                                            c.op('dve', lambda e: e.tensor_tensor(ac[:, 0:n], ac[:, 0:n], tm[:, 0:n], ALU.add), [ac, tm], [ac])
                                        else:
                                            c.op('dve', lambda e: e.tensor_tensor(mergedT[:, db, o:o + n], ac[:, 0:n], tm[:, 0:n], ALU.add), [ac, tm], [mergedT])
                    c.barrier()
                    with ExitStack() as s:
                        gs = GemmScr(s, KC, 256)
                        zo = [c.sb([128, 344], F32, es=s) for _ in range(3)]
                        zi = [0]
                        def epi(ps, col, si, t0, n):
                            ob = zo[zi[0] % 3]; zi[0] += 1
                            c.op('act', lambda e: e.copy(ob[:, 0:n], ps[:, 0:n]), [ps], [ob])
                            c.dma('sp', MIX.t[col:col + 128, s0 + t0:s0 + t0 + n], ob[:, 0:n], reads=[ob], writes=[MIX])
                        gemm(gs, I['w_out'].t[l, :, :], I['w_out'], [(cb * 256, 256) for cb in range(8)], lambda k, t0, n: mergedT[:, k, t0:t0 + n], [mergedT], SUBS, epi)
                    c.barrier()
                with ExitStack() as s:
                    nsc = NormScr(s, resid=True)
                    for (o, n) in SUBS:
                        norm_sub(nsc, MIXv, MIX, s0 + o, n, 1, l, None, None, resid=True)
                c.barrier()
                with ExitStack() as s3:
                    hid = c.sb([128, 44, SUP], BF16, es=s3)
                    with ExitStack() as s:
                        hnT = c.sb([128, KC, SUP], BF16, es=s)
                        with ExitStack() as sn:
                            nsc = NormScr(sn)
                            for (o, n) in SUBS:
                                norm_sub(nsc, Hv, H, s0 + o, n, 2, l, lambda k, o=o, n=n: hnT[:, k, o:o + n], hnT)
                        c.barrier()
                        Wgu = I['ffn_w_gu'].t[l, :, :]
                        st = [c.sb([128, KC, 2, 128], F32, es=s) for _ in range(2)]
                        wb = [c.sb([128, KC, 2, 128], BF16, es=s) for _ in range(2)]
                        sgb = [c.sb([128, 344], F32, es=s) for _ in range(3)]
                        ii = 0
                        def loadgu(j):
                            stj = st[j % 2]; wbj = wb[j % 2]
                            c.dma('sp', stj[:, :, 0, :], Wgu[:, j * 128:(j + 1) * 128].rearrange("(k p) n -> p k n", p=128), reads=[I['ffn_w_gu']], writes=[stj])
                            c.dma('sp', stj[:, :, 1, :], Wgu[:, FH + j * 128:FH + (j + 1) * 128].rearrange("(k p) n -> p k n", p=128), reads=[I['ffn_w_gu']], writes=[stj])
                            c.op('dve', lambda e: e.tensor_copy(wbj[:, :, :, :], stj[:, :, :, :]), [stj], [wbj])
                        loadgu(0)
                        for j in range(44):
                            wbj = wb[j % 2]
                            if j + 1 < 44:
                                loadgu(j + 1)
                            for si, (o, n) in enumerate(SUBS):
                                sg = sgb[ii % 3]; ii += 1
                                ps = psf(0, 4)
                                c.mm([(ps[:, 0:n], wbj[:, kk, 0, :], hnT[:, kk, o:o + n]) for kk in range(KC)], [wbj, hnT], ps)
                                c.op('act', lambda e: e.activation(sg[:, 0:n], ps[:, 0:n], AF.Silu), [ps], [sg])
                                ps2 = psf(0, 4)
                                c.mm([(ps2[:, 0:n], wbj[:, kk, 1, :], hnT[:, kk, o:o + n]) for kk in range(KC)], [wbj, hnT], ps2)
                                c.op('dve', lambda e: e.tensor_tensor(hid[:, j, o:o + n], sg[:, 0:n], ps2[:, 0:n], ALU.mult), [sg, ps2], [hid])
                    c.barrier()
                    with ExitStack() as s:
                        gs = GemmScr(s, 44, 128)
                        zo = [c.sb([128, 344], F32, es=s) for _ in range(3)]
                        zi = [0]
                        def epi(ps, col, si, t0, n):
                            ob = zo[zi[0] % 3]; zi[0] += 1
                            c.op('act', lambda e: e.copy(ob[:, 0:n], ps[:, 0:n]), [ps], [ob])
                            c.dma('sp', MIX.t[col:col + 128, s0 + t0:s0 + t0 + n], ob[:, 0:n], reads=[ob], writes=[MIX])
                        gemm(gs, I['ffn_w_down'].t[l, :, :], I['ffn_w_down'], [(cb * 128, 128) for cb in range(16)], lambda k, t0, n: hid[:, k, t0:t0 + n], [hid], SUBS, epi)
                    c.barrier()
                with ExitStack() as s:
                    nsc = NormScr(s, resid=True)
                    for (o, n) in SUBS:
                        norm_sub(nsc, MIXv, MIX, s0 + o, n, 3, l, None, None, resid=True)
                c.barrier()

        if stop_after is None:
            with ExitStack() as s:
                hin = [c.sb([128, KC, 128], F32, es=s) for _ in range(2)]
                ostg = [c.sb([128, D], F32, es=s) for _ in range(2)]
                for j in range(16):
                    t0 = 16 + 128 * j
                    hi_ = hin[j % 2]; og = ostg[j % 2]
                    c.dma('sp', hi_[:, :, :], Hv[:, :, t0:t0 + 128], reads=[H], writes=[hi_])
                    for kg in range(4):
                        ps = psf(0, 4)
                        c.mm_raw([(lambda e, jj=jj: e.transpose(ps[:, jj * 128:(jj + 1) * 128], hi_[:, 4 * kg + jj, :], idf[:, :])) for jj in range(4)], [hi_, idf], ps)
                        c.op('act', lambda e: e.copy(og[:, kg * 512:(kg + 1) * 512], ps[:, :]), [ps], [og])
                    c.dma('sp', OUT.t[128 * j:128 * (j + 1), :], og[:, :], reads=[og], writes=[OUT])
        else:
            zt = c.sb([128, D], F32, 'zt')
            c.op('dve', lambda e: e.memset(zt[:, :], 0.0), [], [zt])
            c.dma('sp', OUT.t[0:128, :], zt[:, :], reads=[zt], writes=[OUT])
        c.barrier()
        c.final_wait()
    return nc


def kernel(**inputs):
    nc = build_program()
    oh = t5_onehot()
    active = {0: 0, 1: 1, 4: 2, 5: 3}
    in_maps = []
    zeros = {}
    for core in range(8):
        m = {}
        for k, v in inputs.items():
            v = np.asarray(v)
            if core in active:
                m[k] = np.ascontiguousarray(v[active[core]]) if k == 'x' else np.ascontiguousarray(v)
            else:
                if k not in zeros:
                    zeros[k] = np.zeros(v.shape[1:] if k == 'x' else v.shape, v.dtype)
                m[k] = zeros[k]
        m['t5oh'] = oh
        in_maps.append(m)
    res = run_bass_kernel_spmd(nc, in_maps, core_ids=list(range(8)))
    cores = sorted(active, key=lambda c_: active[c_])
    return np.stack([np.asarray(res.results[c_]['out']) for c_ in cores], axis=0).astype(np.float32)
```

```python
import numpy as np, math
from contextlib import ExitStack
import concourse.bass as bass
import concourse.mybir as mybir
from concourse.bass_utils import run_bass_kernel_spmd

F32 = mybir.dt.float32
BF16 = mybir.dt.bfloat16
AF = mybir.ActivationFunctionType
ALU = mybir.AluOpType
AX = mybir.AxisListType


class Buf:
    def __init__(self, t, name):
        self.t = t
        self.name = name
        self.w = None
        self.r = []

    def __getitem__(self, idx):
        return self.t[idx]


class Ctx:
    NDS = 24

    def __init__(self, nc, es):
        self.nc, self.es = nc, es
        self.engs = {'pe': nc.tensor, 'act': nc.scalar, 'dve': nc.vector,
                     'pool': nc.gpsimd, 'sp': nc.sync}
        self.sem = {k: es.enter_context(nc.semaphore('s_' + k)) for k in ['pe', 'act', 'dve', 'pool']}
        self.cnt = {k: 0 for k in self.sem}
        self.dsem = [es.enter_context(nc.semaphore('d%d' % i)) for i in range(self.NDS)]
        self.dcnt = [0] * self.NDS
        self.dnext = 0
        self.seen = {e: {} for e in self.engs}
        self.nbuf = 0

    def sb(self, shape, dt, name=None, es=None):
        self.nbuf += 1
        name = name or 'sb%d' % self.nbuf
        t = (es or self.es).enter_context(self.nc.sbuf_tensor(name, list(shape), dt))
        return Buf(t, name)

    def ps(self, shape, dt, name=None, es=None):
        self.nbuf += 1
        name = name or 'ps%d' % self.nbuf
        t = (es or self.es).enter_context(self.nc.psum_tensor(name, list(shape), dt))
        return Buf(t, name)

    def dram(self, name, shape, dt, kind='Internal'):
        t = self.nc.dram_tensor(name, list(shape), dt, kind=kind)
        return Buf(t.ap(), name)

    def _need(self, e, tk, raw):
        if tk is None:
            return
        key, val = tk
        if key == e and e == 'pe':
            return
        if self.seen[e].get(key, 0) >= val:
            return
        if isinstance(key, str):
            self.engs[e].wait_ge(self.sem[key], val)
        else:
            self.engs[e].wait_ge(self.dsem[key[1]], 16 * val)
        self.seen[e][key] = val

    def _deps(self, e, reads, writes):
        for b in reads:
            self._need(e, b.w, True)
        for b in writes:
            self._need(e, b.w, False)
            for tk in b.r:
                self._need(e, tk, False)

    def _commit(self, tk, reads, writes):
        for b in reads:
            b.r.append(tk)
            if len(b.r) > 12:
                d = {}
                for k, v in b.r:
                    d[k] = max(d.get(k, 0), v)
                b.r = list(d.items())
        for b in writes:
            b.w = tk
            b.r = []

    def op(self, e, fn, reads=(), writes=()):
        self._deps(e, reads, writes)
        ins = fn(self.engs[e])
        self.cnt[e] += 1
        ins.then_inc(self.sem[e], 1)
        self._commit((e, self.cnt[e]), reads, writes)
        return ins

    def mm(self, items, reads, out, start=True, stop=True):
        e = 'pe'
        self._deps(e, reads, [out])
        n = len(items)
        ins = None
        for i, (o, l, r) in enumerate(items):
            ins = self.nc.tensor.matmul(o, l, r, start=(start and i == 0), stop=(stop and i == n - 1))
        self.cnt[e] += 1
        ins.then_inc(self.sem[e], 1)
        self._commit((e, self.cnt[e]), reads, [out])

    def mm_raw(self, fns, reads, out):
        e = 'pe'
        self._deps(e, reads, [out])
        ins = None
        for f in fns:
            ins = f(self.nc.tensor)
        self.cnt[e] += 1
        ins.then_inc(self.sem[e], 1)
        self._commit((e, self.cnt[e]), reads, [out])

    def dma(self, q, out_ap, in_ap, reads=(), writes=(), **kw):
        i = self.dnext
        self.dnext = (self.dnext + 1) % self.NDS
        if self.dcnt[i] > 0:
            self._need(q, (('d', i), self.dcnt[i]), True)
        self._deps(q, reads, writes)
        ins = self.engs[q].dma_start(out=out_ap, in_=in_ap, **kw)
        ins.then_inc(self.dsem[i], 16)
        self.dcnt[i] += 1
        self._commit((('d', i), self.dcnt[i]), reads, writes)

    def barrier(self):
        tks = [(k, self.cnt[k]) for k in self.cnt if self.cnt[k] > 0]
        tks += [(('d', i), self.dcnt[i]) for i in range(self.NDS) if self.dcnt[i] > 0]
        for e in self.engs:
            for tk in tks:
                if tk[0] == e:
                    continue
                self._need(e, tk, True)

    def final_wait(self):
        tks = [(k, self.cnt[k]) for k in self.cnt if self.cnt[k] > 0]
        tks += [(('d', i), self.dcnt[i]) for i in range(self.NDS) if self.dcnt[i] > 0]
        for tk in tks:
            self._need('sp', tk, True)

def _vec(c, dst_ap, dstb, srcb, src_ap):
    c.dma('sp', dst_ap, src_ap, reads=[srcb], writes=[dstb], allow_slow_non_contiguous=True)


def mixer_lru(c, g, I, l, Z, YS, psf):
    with ExitStack() as s:
        CW = c.sb([128, 4, 4], F32, es=s); CB = c.sb([128, 4], F32, es=s)
        BA = c.sb([128, 4], F32, es=s); BX = c.sb([128, 4], F32, es=s); LM = c.sb([128, 4], F32, es=s)
        CN = c.sb([128, 4], F32, es=s)
        for j_ in range(4):
            _vec(c, CW[:, :, j_], CW, I['lru_conv_w'], I['lru_conv_w'].t[l, j_, :].rearrange("(h p) -> p h", p=128))
        _vec(c, CB[:, :], CB, I['lru_conv_b'], I['lru_conv_b'].t[l, :].rearrange("(h p) -> p h", p=128))
        _vec(c, BA[:, :], BA, I['lru_b_a'], I['lru_b_a'].t[l, :].rearrange("(h p) -> p h", p=128))
        _vec(c, BX[:, :], BX, I['lru_b_x'], I['lru_b_x'].t[l, :].rearrange("(h p) -> p h", p=128))
        _vec(c, LM[:, :], LM, I['lru_lambda'], I['lru_lambda'].t[l, :].rearrange("(h p) -> p h", p=128))
        c.op('act', lambda e: e.activation(CN[:, :], LM[:, :], AF.Exp, scale=-1.0), [LM], [CN])
        c.op('dve', lambda e: e.tensor_scalar_add(CN[:, :], CN[:, :], 1.0), [CN], [CN])
        c.op('act', lambda e: e.activation(CN[:, :], CN[:, :], AF.Ln), [CN], [CN])
        c.op('dve', lambda e: e.tensor_scalar_mul(CN[:, :], CN[:, :], -8.0), [CN], [CN])
        Wst = c.sb([128, 2, 4, 128], F32, es=s); Wb = c.sb([128, 2, 4, 128], BF16, es=s)
        c.dma('sp', Wst[:, 0, :, :], I['lru_w_a'].t[l, :, :, :].rearrange("h i j -> i h j"), reads=[I['lru_w_a']], writes=[Wst])
        c.dma('sp', Wst[:, 1, :, :], I['lru_w_x'].t[l, :, :, :].rearrange("h i j -> i h j"), reads=[I['lru_w_x']], writes=[Wst])
        c.op('act', lambda e: e.copy(Wb[:, :, :, :], Wst[:, :, :, :]), [Wst], [Wb])
        U = c.sb([128, T], F32, es=s); Gt = c.sb([128, T], F32, es=s); XC = c.sb([128, T], F32, es=s)
        R = c.sb([128, T], F32, es=s); IG = c.sb([128, T], F32, es=s); A = c.sb([128, T], F32, es=s)
        TH = c.sb([128, T], F32, es=s); P1 = c.sb([128, T], F32, es=s); HS = c.sb([128, T], F32, es=s)
        XCb = c.sb([128, T], BF16, es=s); Yb = c.sb([128, T], BF16, es=s)
        for h in range(4):
            c.dma('sp', U[:, :], Z.t[h * 128:(h + 1) * 128, :], reads=[Z], writes=[U])
            c.dma('sp', Gt[:, :], Z.t[512 + h * 128:512 + (h + 1) * 128, :], reads=[Z], writes=[Gt])
            c.op('dve', lambda e: e.tensor_scalar(XC[:, :], U[:, :], CW[:, h, 3:4], CB[:, h:h + 1], ALU.mult, ALU.add), [U, CW, CB], [XC])
            for j in range(3):
                sh = 3 - j
                c.op('dve', lambda e: e.scalar_tensor_tensor(XC[:, sh:T], U[:, 0:T - sh], CW[:, h, j:j + 1], XC[:, sh:T], ALU.mult, ALU.add), [U, CW, XC], [XC])
            c.op('act', lambda e: e.copy(XCb[:, :], XC[:, :]), [XC], [XCb])
            for (t0, n) in CH512:
                ps = psf()
                c.mm([(ps[:, 0:n], Wb[:, 0, h, :], XCb[:, t0:t0 + n])], [Wb, XCb], ps)
                c.op('act', lambda e: e.activation(R[:, t0:t0 + n], ps[:, 0:n], AF.Sigmoid, bias=BA[:, h:h + 1]), [ps, BA], [R])
                ps2 = psf()
                c.mm([(ps2[:, 0:n], Wb[:, 1, h, :], XCb[:, t0:t0 + n])], [Wb, XCb], ps2)
                c.op('act', lambda e: e.activation(IG[:, t0:t0 + n], ps2[:, 0:n], AF.Sigmoid, bias=BX[:, h:h + 1]), [ps2, BX], [IG])
            c.op('act', lambda e: e.activation(A[:, :], R[:, :], AF.Exp, scale=CN[:, h:h + 1]), [R, CN], [A])
            c.op('act', lambda e: e.activation(TH[:, :], R[:, :], AF.Tanh, scale=CN[:, h:h + 1]), [R, CN], [TH])
            c.op('dve', lambda e: e.tensor_scalar(P1[:, :], TH[:, :], -1.0, 1.0, ALU.mult, ALU.add), [TH], [P1])
            c.op('dve', lambda e: e.reciprocal(P1[:, :], P1[:, :]), [P1], [P1])
            c.op('dve', lambda e: e.scalar_tensor_tensor(P1[:, :], TH[:, :], -2.0, P1[:, :], ALU.mult, ALU.mult), [TH, P1], [P1])
            c.op('act', lambda e: e.activation(P1[:, :], P1[:, :], AF.Sqrt), [P1], [P1])
            c.op('dve', lambda e: e.tensor_tensor(IG[:, :], IG[:, :], XC[:, :], ALU.mult), [IG, XC], [IG])
            c.op('dve', lambda e: e.tensor_tensor(P1[:, :], P1[:, :], IG[:, :], ALU.mult), [P1, IG], [P1])
            c.op('dve', lambda e: e.tensor_tensor_scan(HS[:, :], A[:, :], P1[:, :], 0.0, ALU.mult, ALU.add), [A, P1], [HS])
            c.op('act', lambda e: e.activation(Gt[:, :], Gt[:, :], AF.Gelu_apprx_tanh), [Gt], [Gt])
            c.op('dve', lambda e: e.tensor_tensor(Yb[:, :], HS[:, :], Gt[:, :], ALU.mult), [HS, Gt], [Yb])
            c.dma('sp', YS.t[h * 128:(h + 1) * 128, :], Yb[:, :], reads=[Yb], writes=[YS])


def mixer_pool(c, g, I, l, Z, YS, psf, IC):
    with ExitStack() as s:
        PSc = c.sb([128, 4], F32, es=s)
        _vec(c, PSc[:, :], PSc, I['pool_scale'], I['pool_scale'].t[l, :].rearrange("(h p) -> p h", p=128))
        Wst = c.sb([128, 4, 128], F32, es=s); Wb = c.sb([128, 4, 128], BF16, es=s)
        c.dma('sp', Wst[:, :, :], I['pool_w'].t[l, :, :, :].rearrange("h i j -> i h j"), reads=[I['pool_w']], writes=[Wst])
        c.op('act', lambda e: e.copy(Wb[:, :, :], Wst[:, :, :]), [Wst], [Wb])
        TP = T + 16
        Up = c.sb([128, TP], F32, es=s); Pa = c.sb([128, TP], F32, es=s); Pb = c.sb([128, TP], F32, es=s)
        PL = c.sb([128, T], F32, es=s); PLb = c.sb([128, T], BF16, es=s); Yb = c.sb([128, T], BF16, es=s)
        t16 = c.sb([128, 16], F32, es=s)
        for b_ in (Up, Pa, Pb):
            c.op('dve', lambda e: e.memset(b_[:, 0:16], 0.0), [], [b_])
        for gi in range(4):
            win = 2 ** (gi + 1)
            c.dma('sp', Up[:, 16:TP], Z.t[1024 + gi * 128:1024 + (gi + 1) * 128, :], reads=[Z], writes=[Up])
            src = Up
            dsts = [Pa, Pb]
            for si, step in enumerate([1, 2, 4, 8][:gi + 1]):
                dst = dsts[si % 2]
                c.op('dve', lambda e: e.tensor_tensor(dst[:, 16:TP], src[:, 16:TP], src[:, 16 - step:TP - step], ALU.add), [src], [dst])
                src = dst
            c.op('dve', lambda e: e.scalar_tensor_tensor(PL[:, :], src[:, 16:TP], 1.0 / win, Up[:, 16:TP], ALU.mult, ALU.subtract), [src, Up], [PL])
            c.op('dve', lambda e: e.tensor_tensor(t16[:, :], src[:, 16:32], IC[:, gi, :], ALU.mult), [src, IC], [t16])
            c.op('dve', lambda e: e.tensor_tensor(PL[:, 0:16], t16[:, :], Up[:, 16:32], ALU.subtract), [t16, Up], [PL])
            c.op('act', lambda e: e.copy(PLb[:, :], PL[:, :]), [PL], [PLb])
            for (t0, n) in CH512:
                ps = psf()
                c.mm([(ps[:, 0:n], Wb[:, gi, :], PLb[:, t0:t0 + n])], [Wb, PLb], ps)
                c.op('dve', lambda e: e.tensor_scalar_mul(Yb[:, t0:t0 + n], ps[:, 0:n], PSc[:, gi:gi + 1]), [ps, PSc], [Yb])
            c.dma('sp', YS.t[512 + gi * 128:512 + (gi + 1) * 128, :], Yb[:, :], reads=[Yb], writes=[YS])


def mixer_hgrn(c, g, I, l, Z, YS, psf, psb, MK, HM, idb, ones_b, epsb):
    with ExitStack() as s:
        LBS = c.sb([128, 2, 4], F32, es=s)
        for l_ in range(2):
            _vec(c, LBS[:, l_, :], LBS, I['hgrn_lower_bounds'], I['hgrn_lower_bounds'].t[l_, :].rearrange("(h p) -> p h", p=128))
        LB = c.sb([128, 4], F32, es=s); OML = c.sb([128, 4], F32, es=s); NOML = c.sb([128, 4], F32, es=s)
        if l == 0:
            c.op('dve', lambda e: e.memset(LB[:, :], 0.0), [], [LB])
        else:
            SM = c.sb([128, 4], F32, es=s)
            c.op('act', lambda e: e.activation(LBS[:, :, :], LBS[:, :, :], AF.Exp), [LBS], [LBS])
            c.op('dve', lambda e: e.tensor_tensor(SM[:, :], LBS[:, 0, :], LBS[:, 1, :], ALU.add), [LBS], [SM])
            c.op('dve', lambda e: e.reciprocal(SM[:, :], SM[:, :]), [SM], [SM])
            c.op('dve', lambda e: e.tensor_tensor(LB[:, :], LBS[:, 1, :], SM[:, :], ALU.mult), [LBS, SM], [LB])
        c.op('dve', lambda e: e.tensor_scalar(OML[:, :], LB[:, :], -1.0, 1.0, ALU.mult, ALU.add), [LB], [OML])
        c.op('dve', lambda e: e.tensor_scalar_add(NOML[:, :], LB[:, :], -1.0), [LB], [NOML])
        HNW = c.sb([128, 1], F32, es=s)
        _vec(c, HNW[:, :], HNW, I['hgrn_norm'], I['hgrn_norm'].t[l, :].rearrange("(p o) -> p o", o=1))
        def ft():
            return c.sb([128, T], F32, es=s)
        Qf, Fz, Vf, OG, SIG, Gg, Kk, Bc, TMP, EK, Oo = [ft() for _ in range(11)]
        QT, KT, KH, VB, Yb, SQ = [c.sb([128, T], BF16, es=s) for _ in range(6)]
        Dd = c.sb([128, 33], F32, es=s)
        S = c.sb([128, 128], F32, es=s)
        Sbf = c.sb([128, 33, 128], BF16, es=s)
        ktok = [c.sb([128, 128], BF16, es=s) for _ in range(2)]
        vtok = [c.sb([128, 128], BF16, es=s) for _ in range(2)]
        Asb = [c.sb([128, 128], BF16, es=s) for _ in range(2)]
        for h in range(4):
            c.dma('sp', Qf[:, :], Z.t[1536 + h * 128:1536 + (h + 1) * 128, :], reads=[Z], writes=[Qf])
            c.dma('sp', Fz[:, :], Z.t[2048 + h * 128:2048 + (h + 1) * 128, :], reads=[Z], writes=[Fz])
            c.dma('sp', Vf[:, :], Z.t[2560 + h * 128:2560 + (h + 1) * 128, :], reads=[Z], writes=[Vf])
            c.dma('sp', OG[:, :], Z.t[3072 + h * 128:3072 + (h + 1) * 128, :], reads=[Z], writes=[OG])
            c.op('act', lambda e: e.activation(SIG[:, :], Fz[:, :], AF.Sigmoid), [Fz], [SIG])
            c.op('dve', lambda e: e.tensor_scalar(Gg[:, :], SIG[:, :], OML[:, h:h + 1], LB[:, h:h + 1], ALU.mult, ALU.add), [SIG, OML, LB], [Gg])
            c.op('act', lambda e: e.activation(Gg[:, :], Gg[:, :], AF.Ln), [Gg], [Gg])
            c.op('dve', lambda e: e.tensor_scalar(Kk[:, :], SIG[:, :], NOML[:, h:h + 1], OML[:, h:h + 1], ALU.mult, ALU.add), [SIG, OML, NOML], [Kk])
            c.op('dve', lambda e: e.tensor_tensor_scan(Bc[:, :], MK[:, :], Gg[:, :], 0.0, ALU.mult, ALU.add), [MK, Gg], [Bc])
            c.op('act', lambda e: e.activation(TMP[:, :], Bc[:, :], AF.Exp), [Bc], [TMP])
            c.op('dve', lambda e: e.tensor_tensor(QT[:, :], Qf[:, :], TMP[:, :], ALU.mult), [Qf, TMP], [QT])
            c.op('dve', lambda e: e.tensor_scalar(TMP[:, :], Bc[:, :], -1.0, 80.0, ALU.mult, ALU.min), [Bc], [TMP])
            c.op('act', lambda e: e.activation(TMP[:, :], TMP[:, :], AF.Exp), [TMP], [TMP])
            c.op('dve', lambda e: e.tensor_tensor(KT[:, :], Kk[:, :], TMP[:, :], ALU.mult), [Kk, TMP], [KT])
            c.op('act', lambda e: e.activation(Dd[:, 0:1], Bc[:, 15:16], AF.Exp), [Bc], [Dd])
            c.op('act', lambda e: e.activation(Dd[:, 1:33], Bc[:, 79:T:64], AF.Exp), [Bc], [Dd])
            chunks = [(0, 16)] + [(16 + 64 * n, 64) for n in range(32)]
            for (c0, cl) in chunks:
                c.op('act', lambda e: e.activation(EK[:, c0:c0 + cl], Bc[:, c0:c0 + cl], AF.Exp, bias=Bc[:, c0 + cl - 1:c0 + cl], scale=-1.0), [Bc], [EK])
            c.op('dve', lambda e: e.tensor_tensor(KH[:, :], Kk[:, :], EK[:, :], ALU.mult), [Kk, EK], [KH])
            c.op('act', lambda e: e.copy(VB[:, :], Vf[:, :]), [Vf], [VB])
            c.op('dve', lambda e: e.memset(S[:, :], 0.0), [], [S])
            n_ = 0
            def tposes(bi):
                t0, L = BLKS[bi]
                kt = ktok[bi % 2]; vt = vtok[bi % 2]
                pb1 = psb()
                c.mm_raw([lambda e: e.transpose(pb1[0:L, 0:128], KH[:, t0:t0 + L], idb[:, :])], [KH, idb], pb1)
                c.op('act', lambda e: e.copy(kt[0:L, :], pb1[0:L, 0:128]), [pb1], [kt])
                pb2 = psb()
                c.mm_raw([lambda e: e.transpose(pb2[0:L, 0:128], VB[:, t0:t0 + L], idb[:, :])], [VB, idb], pb2)
                c.op('act', lambda e: e.copy(vt[0:L, :], pb2[0:L, 0:128]), [pb2], [vt])
            tposes(0)
            for bi, (t0, L) in enumerate(BLKS):
                kt = ktok[bi % 2]; vt = vtok[bi % 2]; ab = Asb[bi % 2]
                if bi + 1 < len(BLKS):
                    tposes(bi + 1)
                cks = [(0, 16)] if bi == 0 else [(0, 64), (64, 64)]
                n_first = n_
                for (r0, rl) in cks:
                    ps = psf()
                    c.mm([(ps[:, 0:128], kt[r0:r0 + rl, :], vt[r0:r0 + rl, :])], [kt, vt], ps)
                    c.op('act', lambda e: e.copy(Sbf[:, n_, :], S[:, :]), [S], [Sbf])
                    c.op('dve', lambda e: e.scalar_tensor_tensor(S[:, :], S[:, :], Dd[:, n_:n_ + 1], ps[:, 0:128], ALU.mult, ALU.add), [S, Dd, ps], [S])
                    n_ += 1
                psA = psf()
                c.mm([(psA[0:L, 0:L], KT[:, t0:t0 + L], QT[:, t0:t0 + L])], [KT, QT], psA)
                c.op('dve', lambda e: e.tensor_tensor(ab[0:L, 0:L], psA[0:L, 0:L], HM[0:L, 0:L], ALU.mult), [psA, HM], [ab])
                pso = psf()
                items = [(pso[:, 0:L], vt[0:L, :], ab[0:L, 0:L])]
                for ci, (r0, rl) in enumerate(cks):
                    items.append((pso[:, r0:r0 + rl], Sbf[:, n_first + ci, :], QT[:, t0 + r0:t0 + r0 + rl]))
                c.mm(items, [vt, ab, Sbf, QT], pso)
                c.op('act', lambda e: e.copy(Oo[:, t0:t0 + L], pso[:, 0:L]), [pso], [Oo])
            c.op('act', lambda e: e.activation(SQ[:, :], Oo[:, :], AF.Square), [Oo], [SQ])
            for (t0, n) in CH512:
                ps = psf()
                c.mm([(ps[:, 0:n], ones_b[:, :], SQ[:, t0:t0 + n])], [ones_b, SQ], ps)
                c.op('act', lambda e: e.activation(TMP[:, t0:t0 + n], ps[:, 0:n], AF.Sqrt, bias=epsb[:, 0:1], scale=1.0 / 128), [ps, epsb], [TMP])
            c.op('dve', lambda e: e.reciprocal(TMP[:, :], TMP[:, :]), [TMP], [TMP])
            c.op('dve', lambda e: e.scalar_tensor_tensor(Oo[:, :], Oo[:, :], HNW[:, 0:1], TMP[:, :], ALU.mult, ALU.mult), [Oo, HNW, TMP], [Oo])
            c.op('act', lambda e: e.activation(OG[:, :], OG[:, :], AF.Silu), [OG], [OG])
            c.op('dve', lambda e: e.tensor_tensor(Yb[:, :], Oo[:, :], OG[:, :], ALU.mult), [Oo, OG], [Yb])
            c.dma('sp', YS.t[1024 + h * 128:1024 + (h + 1) * 128, :], Yb[:, :], reads=[Yb], writes=[YS])


def mixer_attn(c, g, I, l, Z, YS, psf, psb, idb, RB, EB0, EB1, EBM0, EBMM, epsb, lambda_init):
    SCALE = 0.125
    with ExitStack() as s:
        DL = c.sb([128, 4, 64], F32, es=s)
        c.dma('sp', DL[:, :, :], I['diff_lambda'].t[l, :, :].rearrange("a b -> (a b)").partition_broadcast(128), reads=[I['diff_lambda']], writes=[DL])
        PR = c.sb([128, 2, 64], F32, es=s); SS = c.sb([128, 2], F32, es=s); LAMN = c.sb([128, 1], F32, es=s)
        c.op('dve', lambda e: e.tensor_tensor(PR[:, 0, :], DL[:, 0, :], DL[:, 1, :], ALU.mult), [DL], [PR])
        c.op('dve', lambda e: e.tensor_tensor(PR[:, 1, :], DL[:, 2, :], DL[:, 3, :], ALU.mult), [DL], [PR])
        c.op('dve', lambda e: e.reduce_sum(SS[:, :], PR[:, :, :], AX.X), [PR], [SS])
        c.op('act', lambda e: e.activation(SS[:, :], SS[:, :], AF.Exp), [SS], [SS])
        c.op('dve', lambda e: e.tensor_tensor(LAMN[:, :], SS[:, 1:2], SS[:, 0:1], ALU.subtract), [SS], [LAMN])
        c.op('dve', lambda e: e.tensor_scalar_add(LAMN[:, :], LAMN[:, :], -lambda_init), [LAMN], [LAMN])
        SUBW = c.sb([128, 128], F32, es=s)
        c.dma('sp', SUBW[:, :], I['diff_subln'].t[l, :].partition_broadcast(128), reads=[I['diff_subln']], writes=[SUBW])
        c.op('dve', lambda e: e.tensor_scalar_mul(SUBW[:, :], SUBW[:, :], 1.0 - lambda_init), [SUBW], [SUBW])
        Qf = c.sb([128, T], F32, es=s); Kf = c.sb([128, T], F32, es=s); Vf = c.sb([128, T], F32, es=s)
        qb = c.sb([128, T], BF16, es=s); kb = c.sb([128, T], BF16, es=s); vb = c.sb([128, T], BF16, es=s)
        V1 = c.sb([128, 17, 132], BF16, es=s)
        c.op('dve', lambda e: e.memset(V1[:, :, 128:129], 1.0), [], [V1])
        YD = c.sb([128, T], BF16, es=s)
        Eb = [c.sb([128, 512], BF16, es=s) for _ in range(3)]
        Om = [c.sb([128, 4, 132], F32, es=s) for _ in range(2)]
        rr = [c.sb([128, 2], F32, es=s) for _ in range(2)]
        ot = [c.sb([128, 128], F32, es=s) for _ in range(2)]
        jk = [c.sb([128, 128], F32, es=s) for _ in range(2)]
        s1 = [c.sb([128, 1], F32, es=s) for _ in range(2)]
        yb = [c.sb([128, 128], BF16, es=s) for _ in range(2)]
        PACC = g.PACC
        g.ei = 0
        g.ci = 0

        def combine(h, Lq, j, t0):
            i = g.ci % 2; g.ci += 1
            r = rr[i]; o = ot[i]; sq = jk[i]; s_ = s1[i]; y = yb[i]
            c.op('dve', lambda e: e.reciprocal(r[0:Lq, 0:1], Om[0][0:Lq, j, 128:129]), [Om[0]], [r])
            c.op('dve', lambda e: e.reciprocal(r[0:Lq, 1:2], Om[1][0:Lq, j, 128:129]), [Om[1]], [r])
            c.op('dve', lambda e: e.tensor_tensor(r[0:Lq, 1:2], r[0:Lq, 1:2], LAMN[0:Lq, 0:1], ALU.mult), [r, LAMN], [r])
            c.op('dve', lambda e: e.tensor_scalar_mul(o[0:Lq, :], Om[0][0:Lq, j, 0:128], r[0:Lq, 0:1]), [Om[0], r], [o])
            c.op('dve', lambda e: e.scalar_tensor_tensor(o[0:Lq, :], Om[1][0:Lq, j, 0:128], r[0:Lq, 1:2], o[0:Lq, :], ALU.mult, ALU.add), [Om[1], r, o], [o])
            c.op('act', lambda e: e.activation(sq[0:Lq, :], o[0:Lq, :], AF.Square, accum_out=s_[0:Lq, 0:1]), [o], [sq, s_])
            c.op('act', lambda e: e.activation(s_[0:Lq, :], s_[0:Lq, :], AF.Sqrt, bias=epsb[0:Lq, 0:1], scale=1.0 / 128), [s_, epsb], [s_])
            c.op('dve', lambda e: e.reciprocal(s_[0:Lq, :], s_[0:Lq, :]), [s_], [s_])
            c.op('dve', lambda e: e.scalar_tensor_tensor(y[0:Lq, :], o[0:Lq, :], s_[0:Lq, 0:1], SUBW[0:Lq, :], ALU.mult, ALU.mult), [o, s_, SUBW], [y])
            pb = psb()
            c.mm_raw([lambda e: e.transpose(pb[:, 0:Lq], y[0:Lq, :], idb[0:Lq, 0:Lq])], [y, idb], pb)
            c.op('act', lambda e: e.copy(YD[:, t0:t0 + Lq], pb[:, 0:Lq]), [pb], [YD])

        for h in range(4):
            c.dma('sp', Qf[:, :], Z.t[3584 + h * 128:3584 + (h + 1) * 128, :], reads=[Z], writes=[Qf])
            c.dma('sp', Kf[:, :], Z.t[4096 + h * 128:4096 + (h + 1) * 128, :], reads=[Z], writes=[Kf])
            c.dma('sp', Vf[:, :], Z.t[4608 + h * 128:4608 + (h + 1) * 128, :], reads=[Z], writes=[Vf])
            c.op('act', lambda e: e.copy(qb[:, :], Qf[:, :]), [Qf], [qb])
            c.op('act', lambda e: e.copy(kb[:, :], Kf[:, :]), [Kf], [kb])
            c.op('act', lambda e: e.copy(vb[:, :], Vf[:, :]), [Vf], [vb])
            for bi, (t0, L) in enumerate(BLKS):
                pb = psb()
                c.mm_raw([lambda e: e.transpose(pb[0:L, 0:128], vb[:, t0:t0 + L], idb[:, :])], [vb, idb], pb)
                c.op('act', lambda e: e.copy(V1[0:L, bi, 0:128], pb[0:L, 0:128]), [pb], [V1])
            cb = RB[:, 31 * 4 + h:31 * 4 + h + 1]
            for m in range(2):
                ms = slice(m * 64, (m + 1) * 64)
                ps = psf(4, 6)
                c.mm([(ps[0:16, 0:16], kb[ms, 0:16], qb[ms, 0:16])], [kb, qb], ps)
                eb = Eb[g.ei % 3]; g.ei += 1
                c.op('act', lambda e: e.activation(eb[0:16, 0:16], ps[0:16, 0:16], AF.Exp, scale=SCALE), [ps], [eb])
                c.op('dve', lambda e: e.tensor_tensor(eb[0:16, 0:16], eb[0:16, 0:16], EBMM[0:16, h, :], ALU.mult), [eb, EBMM], [eb])
                pa = PACC[0]
                c.mm([(pa[0:16, 0:129], eb[0:16, 0:16], V1[0:16, 0, 0:129])], [eb, V1], pa)
                c.op('act', lambda e: e.copy(Om[m][0:16, 0, 0:129], pa[0:16, 0:129]), [pa], [Om[m]])
            combine(h, 16, 0, 0)
            for qc in range(4):
                ct0 = 16 + 512 * qc
                for m in range(2):
                    ms = slice(m * 64, (m + 1) * 64)
                    tasks = ['meta'] + list(range(4 * qc + 4))
                    def score(tk):
                        ps = psf(4, 6)
                        if tk == 'meta':
                            c.mm([(ps[0:16, 0:512], kb[ms, 0:16], qb[ms, ct0:ct0 + 512])], [kb, qb], ps)
                        else:
                            kt0 = 16 + 128 * tk
                            j0 = max(tk - 4 * qc, 0)
                            ncols = (4 - j0) * 128
                            q0 = ct0 + j0 * 128
                            c.mm([(ps[:, 0:ncols], kb[ms, kt0:kt0 + 128], qb[ms, q0:q0 + ncols])], [kb, qb], ps)
                        return ps
                    def consume(tk, ps):
                        eb = Eb[g.ei % 3]; g.ei += 1
                        if tk == 'meta':
                            if qc == 0:
                                c.op('act', lambda e: e.activation(eb[0:16, 0:128], ps[0:16, 0:128], AF.Exp, scale=SCALE), [ps], [eb])
                                c.op('dve', lambda e: e.tensor_tensor(eb[0:16, 0:128], eb[0:16, 0:128], EBM0[0:16, h, :], ALU.mult), [eb, EBM0], [eb])
                                c.op('act', lambda e: e.activation(eb[0:16, 128:512], ps[0:16, 128:512], AF.Exp, scale=SCALE, bias=cb[0:16, :]), [ps, RB], [eb])
                            else:
                                c.op('act', lambda e: e.activation(eb[0:16, 0:512], ps[0:16, 0:512], AF.Exp, scale=SCALE, bias=cb[0:16, :]), [ps, RB], [eb])
                            for j in range(4):
                                c.mm([(PACC[j][:, 0:129], eb[0:16, j * 128:(j + 1) * 128], V1[0:16, 0, 0:129])], [eb, V1], PACC[j], start=True, stop=False)
                            return
                        kbi = tk
                        j0 = max(kbi - 4 * qc, 0)
                        ncols = (4 - j0) * 128
                        nnear = 0
                        for j in range(j0, 4):
                            if 4 * qc + j - kbi <= 1:
                                nnear += 1
                        if nnear > 0:
                            c.op('act', lambda e: e.activation(eb[:, 0:nnear * 128], ps[:, 0:nnear * 128], AF.Exp, scale=SCALE), [ps], [eb])
                        if ncols > nnear * 128:
                            c.op('act', lambda e: e.activation(eb[:, nnear * 128:ncols], ps[:, nnear * 128:ncols], AF.Exp, scale=SCALE, bias=cb), [ps, RB], [eb])
                        for j in range(j0, 4):
                            dl = 4 * qc + j - kbi
                            co = (j - j0) * 128
                            if dl == 0:
                                c.op('dve', lambda e: e.tensor_tensor(eb[:, co:co + 128], eb[:, co:co + 128], EB0[:, h, :], ALU.mult), [eb, EB0], [eb])
                            elif dl == 1:
                                c.op('dve', lambda e: e.tensor_tensor(eb[:, co:co + 128], eb[:, co:co + 128], EB1[:, h, :], ALU.mult), [eb, EB1], [eb])
                        for j in range(j0, 4):
                            dl = 4 * qc + j - kbi
                            co = (j - j0) * 128
                            c.mm([(PACC[j][:, 0:129], eb[:, co:co + 128], V1[:, kbi + 1, 0:129])], [eb, V1], PACC[j], start=False, stop=(dl == 0))
                    psn = score(tasks[0])
                    for ti, tk in enumerate(tasks):
                        pscur = psn
                        if ti + 1 < len(tasks):
                            psn = score(tasks[ti + 1])
                        consume(tk, pscur)
                    for j in range(4):
                        c.op('act', lambda e: e.copy(Om[m][:, j, 0:129], PACC[j][:, 0:129]), [PACC[j]], [Om[m]])
                for j in range(4):
                    combine(h, 128, j, ct0 + j * 128)
            c.dma('sp', YS.t[1536 + h * 128:1536 + (h + 1) * 128, :], YD[:, :], reads=[YD], writes=[YS])

D = 2048
T = 2064
NM = 16
KC = 16
FH = 5632
NIN = 13312
EPS = 1e-6
SUP = 1032
SUBS = [(0, 344), (344, 344), (688, 344)]
CH512 = [(0, 512), (512, 512), (1024, 512), (1536, 512), (2048, 16)]
BLKS = [(0, 16)] + [(16 + 128 * j, 128) for j in range(16)]
DEPTH = 2


def t5_onehot():
    n = np.arange(0, 272)
    max_exact = 16
    nf = np.maximum(n, 1).astype(np.float32)
    large = max_exact + (np.log(nf / np.float32(max_exact)) / np.float32(math.log(128 / max_exact)) * np.float32(32 - max_exact)).astype(np.int32)
    large = np.minimum(large, 31)
    b = np.where(n < max_exact, n, large)
    oh = np.zeros((32, 400), np.float32)
    oh[b, 128 + n] = 1.0
    return oh


class K:
    pass


def build_program(dbg=(), stop_after=None, nlayers=DEPTH):
    nc = bass.Bass("TRN2", target_bir_lowering=False)
    es = ExitStack()
    with es:
        c = Ctx(nc, es)
        g = K()
        I = {}
        def inp(name, shape):
            I[name] = c.dram(name, shape, F32, kind="ExternalInput")
        inp('x', [2048, D]); inp('meta_tokens', [NM, D]); inp('rel_bias', [32, 4]); inp('hgrn_lower_bounds', [2, 512])
        for n_ in ['norm_mix_pre', 'norm_mix_post', 'norm_ffn_pre', 'norm_ffn_post']:
            inp(n_, [2, D])
        inp('w_in', [2, D, NIN]); inp('lru_conv_w', [2, 4, 512]); inp('lru_conv_b', [2, 512])
        inp('lru_w_a', [2, 4, 128, 128]); inp('lru_b_a', [2, 512]); inp('lru_w_x', [2, 4, 128, 128]); inp('lru_b_x', [2, 512])
        inp('lru_lambda', [2, 512]); inp('pool_w', [2, 4, 128, 128]); inp('pool_scale', [2, 512]); inp('hgrn_norm', [2, 128])
        inp('diff_lambda', [2, 4, 64]); inp('diff_subln', [2, 128]); inp('w_branch', [2, 4, 512, D]); inp('w_out', [2, D, D])
        inp('ffn_w_gu', [2, D, 2 * FH]); inp('ffn_w_down', [2, FH, D]); inp('t5oh', [32, 400])
        OUT = c.dram('out', [2048, D], F32, kind="ExternalOutput")

        def scr(name, shape, dt):
            return c.dram(name, shape, dt, kind=("ExternalOutput" if name in dbg else "Internal"))
        H = scr('H', [D, T], F32)
        HN = scr('HN', [D, T], BF16)
        Z = scr('Z', [5120, T], F32)
        YS = scr('YS', [D, T], BF16)
        MIX = scr('MIX', [D, T], F32)
        BD = scr('BD', [4, 400], F32)
        Hv = H.t.rearrange("(k p) t -> p k t", p=128)
        HNv = HN.t.rearrange("(k p) t -> p k t", p=128)
        YSv = YS.t.rearrange("(k p) t -> p k t", p=128)
        MIXv = MIX.t.rearrange("(k p) t -> p k t", p=128)

        PSF = [c.ps([128, 512], F32, name='psf%d' % i) for i in range(6)]
        PSB = [c.ps([128, 1024], BF16, name='psb%d' % i) for i in range(2)]
        g.pi = 0
        def psf(lo=0, hi=6):
            g.pi += 1
            return PSF[lo + g.pi % (hi - lo)]
        g.pb = 0
        g.PACC = PSF[0:4]
        def psb():
            g.pb += 1
            return PSB[g.pb % 2]

        ones_f = c.sb([128, 128], F32, 'ones_f')
        c.op('pool', lambda e: e.memset(ones_f[:, :], 1.0), [], [ones_f])
        ones_b = c.sb([128, 128], BF16, 'ones_b')
        c.op('pool', lambda e: e.memset(ones_b[:, :], 1.0), [], [ones_b])
        idf = c.sb([128, 128], F32, 'idf')
        c.op('pool', lambda e: e.affine_select(idf[:, :], ones_f[:, :], [[-1, 128]], ALU.is_equal, 0.0, base=0, channel_multiplier=1), [ones_f], [idf])
        idb = c.sb([128, 128], BF16, 'idb')
        J128 = c.sb([128, 128], F32, 'J128')
        c.op('pool', lambda e: e.affine_select(J128[:, :], ones_f[:, :], [[1, 128]], ALU.is_equal, 0.0, base=-127, channel_multiplier=1), [ones_f], [J128])
        J16 = c.sb([16, 16], F32, 'J16')
        c.op('pool', lambda e: e.affine_select(J16[:, :], ones_f[0:16, 0:16], [[1, 16]], ALU.is_equal, 0.0, base=-15, channel_multiplier=1), [ones_f], [J16])
        epsb = c.sb([128, 1], F32, 'epsb')
        c.op('pool', lambda e: e.memset(epsb[:, :], EPS), [], [epsb])

        MK = c.sb([128, T], F32, 'MK')
        c.op('pool', lambda e: e.memset(MK[:, :], 1.0), [], [MK])
        c.op('pool', lambda e: e.memset(MK[:, 0:1], 0.0), [], [MK])
        c.op('pool', lambda e: e.memset(MK[:, 16:T:64], 0.0), [], [MK])
        HM = c.sb([128, 128], F32, 'HM')
        c.op('pool', lambda e: e.affine_select(HM[:, :], ones_f[:, :], [[1, 128]], ALU.is_ge, 0.0, base=0, channel_multiplier=-1), [ones_f], [HM])
        c.op('pool', lambda e: e.memset(HM[0:64, 64:128], 0.0), [], [HM])
        IC = c.sb([128, 4, 16], F32, 'IC')
        ioti = c.sb([128, 16], mybir.dt.int32, 'ioti')
        c.op('pool', lambda e: e.iota(ioti[:, :], [[1, 16]], base=1, channel_multiplier=0), [], [ioti])
        iotf = c.sb([128, 16], F32, 'iotf')

        CM = c.sb([128, 128], F32, 'CM')
        c.op('pool', lambda e: e.affine_select(CM[:, :], ones_f[:, :], [[1, 128]], ALU.is_ge, 0.0, base=0, channel_multiplier=-1), [ones_f], [CM])
        c.barrier()
        c.op('dve', lambda e: e.tensor_copy(idb[:, :], idf[:, :]), [idf], [idb])
        c.op('dve', lambda e: e.tensor_copy(iotf[:, :], ioti[:, :]), [ioti], [iotf])
        for gi in range(4):
            c.op('dve', lambda e, gi=gi: e.tensor_scalar_min(IC[:, gi, :], iotf[:, :], float(2 ** (gi + 1))), [iotf], [IC])
        c.op('dve', lambda e: e.reciprocal(IC[:, :, :], IC[:, :, :]), [IC], [IC])

        def vecload(dst_ap, src_ap, dstb, srcb):
            c.dma('sp', dst_ap, src_ap, reads=[srcb], writes=[dstb], allow_slow_non_contiguous=True)

        NW = c.sb([128, 4, 2, 16], F32, 'NW')
        for i_, n_ in enumerate(['norm_mix_pre', 'norm_mix_post', 'norm_ffn_pre', 'norm_ffn_post']):
            for l in range(DEPTH):
                vecload(NW[:, i_, l, :], I[n_].t[l, :].rearrange("(k p) -> p k", p=128), NW, I[n_])

        def norm_sub(sc, src_v, srcb, t0, n, wk, l, dst_fn, dstb, resid=False):
            xt = sc.xt[sc.i % 2]; sq = sc.sq[sc.i % 2]; rs = sc.rs[sc.i % 2]; sc.i += 1
            c.dma('sp', xt[:, :, 0:n], src_v[:, :, t0:t0 + n], reads=[srcb], writes=[xt])
            c.op('act', lambda e: e.activation(sq[:, :, 0:n], xt[:, :, 0:n], AF.Square), [xt], [sq])
            ps = psf(4, 6)
            c.mm([(ps[:, 0:n], ones_b[:, :], sq[:, k, 0:n]) for k in range(KC)], [ones_b, sq], ps)
            c.op('act', lambda e: e.activation(rs[:, 0:n], ps[:, 0:n], AF.Sqrt, bias=epsb[:, 0:1], scale=1.0 / D), [ps, epsb], [rs])
            c.op('dve', lambda e: e.reciprocal(rs[:, 0:n], rs[:, 0:n]), [rs], [rs])
            if not resid:
                for k in range(KC):
                    c.op('dve', lambda e, k=k: e.scalar_tensor_tensor(dst_fn(k), xt[:, k, 0:n], NW[:, wk, l, k:k + 1], rs[:, 0:n], ALU.mult, ALU.mult), [xt, NW, rs], [dstb])
            else:
                ht = sc.ht[sc.i % 2]
                c.dma('sp', ht[:, :, 0:n], Hv[:, :, t0:t0 + n], reads=[H], writes=[ht])
                for k in range(KC):
                    c.op('dve', lambda e, k=k: e.scalar_tensor_tensor(xt[:, k, 0:n], xt[:, k, 0:n], NW[:, wk, l, k:k + 1], rs[:, 0:n], ALU.mult, ALU.mult), [xt, NW, rs], [xt])
                c.op('dve', lambda e: e.tensor_tensor(ht[:, :, 0:n], ht[:, :, 0:n], xt[:, :, 0:n], ALU.add), [ht, xt], [ht])
                c.dma('sp', Hv[:, :, t0:t0 + n], ht[:, :, 0:n], reads=[ht], writes=[H])

        class NormScr:
            def __init__(self, s, resid=False):
                self.i = 0
                self.xt = [c.sb([128, KC, 344], F32, es=s) for _ in range(2)]
                self.sq = [c.sb([128, KC, 344], BF16, es=s) for _ in range(2)]
                self.rs = [c.sb([128, 344], F32, es=s) for _ in range(2)]
                if resid:
                    self.ht = [c.sb([128, KC, 344], F32, es=s) for _ in range(2)]

        class GemmScr:
            def __init__(self, s, kc, wmax):
                self.i = 0
                self.kc = kc
                self.st = [c.sb([128, kc, wmax], F32, es=s) for _ in range(2)]
                self.wb = [c.sb([128, kc, wmax], BF16, es=s) for _ in range(2)]

        def gemm(gs, W2d, Wb, groups, x_fn, xbufs, subs, epi):
            kc = gs.kc
            def load(gi):
                col0, width = groups[gi]
                st = gs.st[gs.i % 2]; wb = gs.wb[gs.i % 2]; gs.i += 1
                c.dma('sp', st[:, :, 0:width], W2d[:, col0:col0 + width].rearrange("(k p) n -> p k n", p=128), reads=[Wb], writes=[st])
                c.op('dve', lambda e: e.tensor_copy(wb[:, :, 0:width], st[:, :, 0:width]), [st], [wb])
                return wb
            nxt = load(0)
            for gi, (col0, width) in enumerate(groups):
                wb = nxt
                if gi + 1 < len(groups):
                    nxt = load(gi + 1)
                for bi in range(width // 128):
                    for si, (t0, n) in enumerate(subs):
                        ps = psf(0, 4)
                        c.mm([(ps[:, 0:n], wb[:, k, bi * 128:(bi + 1) * 128], x_fn(k, t0, n)) for k in range(kc)], [wb] + xbufs, ps)
                        epi(ps, col0 + bi * 128, si, t0, n)

        with ExitStack() as s:
            xin = [c.sb([128, D], F32, es=s) for _ in range(2)]
            stg = [c.sb([128, KC, 128], F32, es=s) for _ in range(2)]
            for bi, (t0, L) in enumerate(BLKS):
                xi = xin[bi % 2]; sg = stg[bi % 2]
                if bi == 0:
                    c.dma('sp', xi[0:L, :], I['meta_tokens'].t[:, :], reads=[I['meta_tokens']], writes=[xi])
                else:
                    c.dma('sp', xi[0:L, :], I['x'].t[t0 - 16:t0 - 16 + L, :], reads=[I['x']], writes=[xi])
                for kg in range(4):
                    ps = psf(0, 4)
                    c.mm_raw([(lambda e, j=j: e.transpose(ps[:, j * 128:j * 128 + L], xi[0:L, (4 * kg + j) * 128:(4 * kg + j + 1) * 128], idf[0:L, 0:L])) for j in range(4)], [xi, idf], ps)
                    c.op('act', lambda e: e.copy(sg[:, 4 * kg:4 * kg + 4, 0:L], ps[:, :].rearrange("p (j t) -> p j t", j=4)[:, :, 0:L]), [ps], [sg])
                c.dma('sp', Hv[:, :, t0:t0 + L], sg[:, :, 0:L], reads=[sg], writes=[H])
        c.barrier()

        RB = c.sb([128, 128], F32, 'RB')
        c.dma('sp', RB[:, :], I['rel_bias'].t.rearrange("a b -> (a b)").partition_broadcast(128), reads=[I['rel_bias']], writes=[RB])
        EB0 = c.sb([128, 4, 128], BF16, 'EB0'); EB1 = c.sb([128, 4, 128], BF16, 'EB1')
        EBM0 = c.sb([16, 4, 128], BF16, 'EBM0'); EBMM = c.sb([16, 4, 16], BF16, 'EBMM')
        with ExitStack() as s:
            rbs = c.sb([32, 4], F32, es=s); ohs = c.sb([32, 400], F32, es=s); bds = c.sb([4, 400], F32, es=s)
            c.dma('sp', rbs[:, :], I['rel_bias'].t[:, :], reads=[I['rel_bias']], writes=[rbs])
            c.dma('sp', ohs[:, :], I['t5oh'].t[:, :], reads=[I['t5oh']], writes=[ohs])
            ps = psf()
            c.mm([(ps[0:4, 0:400], rbs[:, :], ohs[:, :])], [rbs, ohs], ps)
            c.op('dve', lambda e: e.tensor_copy(bds[:, :], ps[0:4, 0:400]), [ps], [bds])
            c.dma('sp', BD.t[:, :], bds[:, :], reads=[bds], writes=[BD])
            hk = [c.sb([128, 128], F32, es=s) for _ in range(2)]
            ef = [c.sb([128, 128], F32, es=s) for _ in range(2)]
            i_ = 0
            for h in range(4):
                for (dst, base, P, N, causal) in [(EB0, 1, 128, 128, True), (EB1, 129, 128, 128, False), (EBM0, 129, 16, 128, False), (EBMM, 113, 16, 16, True)]:
                    hb = hk[i_ % 2]; eb = ef[i_ % 2]; i_ += 1
                    c.dma('sp', hb[0:P, 0:N], bass.AP(BD.t.tensor, h * 400 + base, [[1, P], [1, N]]), reads=[BD], writes=[hb])
                    ps = psf()
                    Jm = J128 if P == 128 else J16
                    c.mm([(ps[0:P, 0:N], Jm[0:P, 0:P], hb[0:P, 0:N])], [Jm, hb], ps)
                    c.op('act', lambda e: e.activation(eb[0:P, 0:N], ps[0:P, 0:N], AF.Exp), [ps], [eb])
                    if causal:
                        c.op('dve', lambda e: e.tensor_tensor(dst[0:P, h, 0:N], eb[0:P, 0:N], CM[0:P, 0:N], ALU.mult), [eb, CM], [dst])
                    else:
                        c.op('dve', lambda e: e.tensor_copy(dst[0:P, h, 0:N], eb[0:P, 0:N]), [eb], [dst])
        c.barrier()

        for l in range(nlayers):
            lambda_init = 0.8 - 0.6 * math.exp(-0.3 * l)
            Win = I['w_in'].t[l, :, :]
            for sp_ in range(2):
                s0 = sp_ * SUP
                with ExitStack() as s:
                    hnT = c.sb([128, KC, SUP], BF16, es=s)
                    nsc = NormScr(s)
                    for (o, n) in SUBS:
                        norm_sub(nsc, Hv, H, s0 + o, n, 0, l, lambda k, o=o, n=n: hnT[:, k, o:o + n], hnT)
                    c.dma('sp', HNv[:, :, s0:s0 + SUP], hnT[:, :, :], reads=[hnT], writes=[HN])
                    gs = GemmScr(s, KC, 256)
                    zo = [c.sb([128, 344], F32, es=s) for _ in range(3)]
                    zi = [0]
                    def epi(ps, col, si, t0, n):
                        ob = zo[zi[0] % 3]; zi[0] += 1
                        c.op('act', lambda e: e.copy(ob[:, 0:n], ps[:, 0:n]), [ps], [ob])
                        c.dma('sp', Z.t[col:col + 128, s0 + t0:s0 + t0 + n], ob[:, 0:n], reads=[ob], writes=[Z])
                    gemm(gs, Win, I['w_in'], [(cb * 256, 256) for cb in range(20)], lambda k, t0, n: hnT[:, k, t0:t0 + n], [hnT], SUBS, epi)
                c.barrier()
            if stop_after == 'p1':
                break

            mixer_lru(c, g, I, l, Z, YS, psf)
            c.barrier()
            mixer_pool(c, g, I, l, Z, YS, psf, IC)
            c.barrier()
            mixer_hgrn(c, g, I, l, Z, YS, psf, psb, MK, HM, idb, ones_b, epsb)
            c.barrier()
            mixer_attn(c, g, I, l, Z, YS, psf, psb, idb, RB, EB0, EB1, EBM0, EBMM, epsb, lambda_init)
            c.barrier()
            if stop_after == 'p2':
                break

            for sp_ in range(2):
                s0 = sp_ * SUP
                with ExitStack() as s3:
                    mergedT = c.sb([128, KC, SUP], BF16, es=s3)
                    with ExitStack() as s:
                        hnT = c.sb([128, KC, SUP], BF16, es=s)
                        ysT = c.sb([128, KC, SUP], BF16, es=s)
                        c.dma('sp', hnT[:, :, :], HNv[:, :, s0:s0 + SUP], reads=[HN], writes=[hnT])
                        c.dma('sp', ysT[:, :, :], YSv[:, :, s0:s0 + SUP], reads=[YS], writes=[ysT])
                        gst = [c.sb([128, KC, 128], F32, es=s) for _ in range(3)]
                        wg = [c.sb([128, 4, KC, 128], BF16, es=s) for _ in range(2)]
                        pst = [c.sb([128, 4, 128], F32, es=s) for _ in range(3)]
                        wp = [c.sb([128, 4, 4, 128], BF16, es=s) for _ in range(2)]
                        sgb = [c.sb([128, 344], F32, es=s) for _ in range(3)]
                        tmpb = [c.sb([128, 344], F32, es=s) for _ in range(3)]
                        acc = [c.sb([128, 344], F32, es=s) for _ in range(3)]
                        ii = 0
                        g.li = 0
                        def load3(db):
                            wgb = wg[db % 2]; wpb = wp[db % 2]
                            for k in range(4):
                                st = gst[g.li % 3]; st2 = pst[g.li % 3]; g.li += 1
                                col = 5120 + k * D + db * 128
                                c.dma('sp', st[:, :, :], Win[:, col:col + 128].rearrange("(k p) n -> p k n", p=128), reads=[I['w_in']], writes=[st])
                                c.op('dve', lambda e: e.tensor_copy(wgb[:, k, :, :], st[:, :, :]), [st], [wgb])
                                c.dma('sp', st2[:, :, :], I['w_branch'].t[l, k, :, db * 128:(db + 1) * 128].rearrange("(k p) n -> p k n", p=128), reads=[I['w_branch']], writes=[st2])
                                c.op('dve', lambda e: e.tensor_copy(wpb[:, k, :, :], st2[:, :, :]), [st2], [wpb])
                        load3(0)
                        for db in range(16):
                            wgb = wg[db % 2]; wpb = wp[db % 2]
                            if db + 1 < 16:
                                load3(db + 1)
                            for si, (o, n) in enumerate(SUBS):
                                ac = acc[si]
                                for k in range(4):
                                    sg = sgb[ii % 3]; tm = tmpb[ii % 3]; ii += 1
                                    ps = psf(0, 4)
                                    c.mm([(ps[:, 0:n], wgb[:, k, kk, :], hnT[:, kk, o:o + n]) for kk in range(KC)], [wgb, hnT], ps)
                                    c.op('act', lambda e: e.activation(sg[:, 0:n], ps[:, 0:n], AF.Sigmoid), [ps], [sg])
                                    ps2 = psf(0, 4)
                                    c.mm([(ps2[:, 0:n], wpb[:, k, kk, :], ysT[:, 4 * k + kk, o:o + n]) for kk in range(4)], [wpb, ysT], ps2)
                                    if k == 0:
                                        c.op('dve', lambda e: e.tensor_tensor(ac[:, 0:n], sg[:, 0:n], ps2[:, 0:n], ALU.mult), [sg, ps2], [ac])
                                    else:
                                        c.op('dve', lambda e: e.tensor_tensor(tm[:, 0:n], sg[:, 0:n], ps2[:, 0:n], ALU.mult), [sg, ps2], [tm])
                                        if k < 3:
                                            c.op('dve', lambda e: e.tensor_tensor(ac[:, 0:n], ac[:, 0:n], tm[:, 0:n], ALU.add), [ac, tm], [ac])
                                        else:
                                            c.op('dve', lambda e: e.tensor_tensor(mergedT[:, db, o:o + n], ac[:, 0:n], tm[:, 0:n], ALU.add), [ac, tm], [mergedT])
                    c.barrier()
                    with ExitStack() as s:
                        gs = GemmScr(s, KC, 256)
                        zo = [c.sb([128, 344], F32, es=s) for _ in range(3)]
                        zi = [0]
                        def epi(ps, col, si, t0, n):
                            ob = zo[zi[0] % 3]; zi[0] += 1
                            c.op('act', lambda e: e.copy(ob[:, 0:n], ps[:, 0:n]), [ps], [ob])
                            c.dma('sp', MIX.t[col:col + 128, s0 + t0:s0 + t0 + n], ob[:, 0:n], reads=[ob], writes=[MIX])
                        gemm(gs, I['w_out'].t[l, :, :], I['w_out'], [(cb * 256, 256) for cb in range(8)], lambda k, t0, n: mergedT[:, k, t0:t0 + n], [mergedT], SUBS, epi)
                    c.barrier()
                with ExitStack() as s:
                    nsc = NormScr(s, resid=True)
                    for (o, n) in SUBS:
                        norm_sub(nsc, MIXv, MIX, s0 + o, n, 1, l, None, None, resid=True)
                c.barrier()
                with ExitStack() as s3:
                    hid = c.sb([128, 44, SUP], BF16, es=s3)
                    with ExitStack() as s:
                        hnT = c.sb([128, KC, SUP], BF16, es=s)
                        with ExitStack() as sn:
                            nsc = NormScr(sn)
                            for (o, n) in SUBS:
                                norm_sub(nsc, Hv, H, s0 + o, n, 2, l, lambda k, o=o, n=n: hnT[:, k, o:o + n], hnT)
                        c.barrier()
                        Wgu = I['ffn_w_gu'].t[l, :, :]
                        st = [c.sb([128, KC, 2, 128], F32, es=s) for _ in range(2)]
                        wb = [c.sb([128, KC, 2, 128], BF16, es=s) for _ in range(2)]
                        sgb = [c.sb([128, 344], F32, es=s) for _ in range(3)]
                        ii = 0
                        def loadgu(j):
                            stj = st[j % 2]; wbj = wb[j % 2]
                            c.dma('sp', stj[:, :, 0, :], Wgu[:, j * 128:(j + 1) * 128].rearrange("(k p) n -> p k n", p=128), reads=[I['ffn_w_gu']], writes=[stj])
                            c.dma('sp', stj[:, :, 1, :], Wgu[:, FH + j * 128:FH + (j + 1) * 128].rearrange("(k p) n -> p k n", p=128), reads=[I['ffn_w_gu']], writes=[stj])
                            c.op('dve', lambda e: e.tensor_copy(wbj[:, :, :, :], stj[:, :, :, :]), [stj], [wbj])
                        loadgu(0)
                        for j in range(44):
                            wbj = wb[j % 2]
                            if j + 1 < 44:
                                loadgu(j + 1)
                            for si, (o, n) in enumerate(SUBS):
                                sg = sgb[ii % 3]; ii += 1
                                ps = psf(0, 4)
                                c.mm([(ps[:, 0:n], wbj[:, kk, 0, :], hnT[:, kk, o:o + n]) for kk in range(KC)], [wbj, hnT], ps)
                                c.op('act', lambda e: e.activation(sg[:, 0:n], ps[:, 0:n], AF.Silu), [ps], [sg])
                                ps2 = psf(0, 4)
                                c.mm([(ps2[:, 0:n], wbj[:, kk, 1, :], hnT[:, kk, o:o + n]) for kk in range(KC)], [wbj, hnT], ps2)
                                c.op('dve', lambda e: e.tensor_tensor(hid[:, j, o:o + n], sg[:, 0:n], ps2[:, 0:n], ALU.mult), [sg, ps2], [hid])
                    c.barrier()
                    with ExitStack() as s:
                        gs = GemmScr(s, 44, 128)
                        zo = [c.sb([128, 344], F32, es=s) for _ in range(3)]
                        zi = [0]
                        def epi(ps, col, si, t0, n):
                            ob = zo[zi[0] % 3]; zi[0] += 1
                            c.op('act', lambda e: e.copy(ob[:, 0:n], ps[:, 0:n]), [ps], [ob])
                            c.dma('sp', MIX.t[col:col + 128, s0 + t0:s0 + t0 + n], ob[:, 0:n], reads=[ob], writes=[MIX])
                        gemm(gs, I['ffn_w_down'].t[l, :, :], I['ffn_w_down'], [(cb * 128, 128) for cb in range(16)], lambda k, t0, n: hid[:, k, t0:t0 + n], [hid], SUBS, epi)
                    c.barrier()
                with ExitStack() as s:
                    nsc = NormScr(s, resid=True)
                    for (o, n) in SUBS:
                        norm_sub(nsc, MIXv, MIX, s0 + o, n, 3, l, None, None, resid=True)
                c.barrier()

        if stop_after is None:
            with ExitStack() as s:
                hin = [c.sb([128, KC, 128], F32, es=s) for _ in range(2)]
                ostg = [c.sb([128, D], F32, es=s) for _ in range(2)]
                for j in range(16):
                    t0 = 16 + 128 * j
                    hi_ = hin[j % 2]; og = ostg[j % 2]
                    c.dma('sp', hi_[:, :, :], Hv[:, :, t0:t0 + 128], reads=[H], writes=[hi_])
                    for kg in range(4):
                        ps = psf(0, 4)
                        c.mm_raw([(lambda e, jj=jj: e.transpose(ps[:, jj * 128:(jj + 1) * 128], hi_[:, 4 * kg + jj, :], idf[:, :])) for jj in range(4)], [hi_, idf], ps)
                        c.op('act', lambda e: e.copy(og[:, kg * 512:(kg + 1) * 512], ps[:, :]), [ps], [og])
                    c.dma('sp', OUT.t[128 * j:128 * (j + 1), :], og[:, :], reads=[og], writes=[OUT])
        else:
            zt = c.sb([128, D], F32, 'zt')
            c.op('dve', lambda e: e.memset(zt[:, :], 0.0), [], [zt])
            c.dma('sp', OUT.t[0:128, :], zt[:, :], reads=[zt], writes=[OUT])
        c.barrier()
        c.final_wait()
    return nc


def kernel(**inputs):
    nc = build_program()
    oh = t5_onehot()
    active = {0: 0, 1: 1, 4: 2, 5: 3}
    in_maps = []
    zeros = {}
    for core in range(8):
        m = {}
        for k, v in inputs.items():
            v = np.asarray(v)
            if core in active:
                m[k] = np.ascontiguousarray(v[active[core]]) if k == 'x' else np.ascontiguousarray(v)
            else:
                if k not in zeros:
                    zeros[k] = np.zeros(v.shape[1:] if k == 'x' else v.shape, v.dtype)
                m[k] = zeros[k]
        m['t5oh'] = oh
        in_maps.append(m)
    res = run_bass_kernel_spmd(nc, in_maps, core_ids=list(range(8)))
    cores = sorted(active, key=lambda c_: active[c_])
    return np.stack([np.asarray(res.results[c_]['out']) for c_ in cores], axis=0).astype(np.float32)
```

```python
import numpy as np, math
from contextlib import ExitStack
import concourse.bass as bass
import concourse.mybir as mybir
from concourse.bass_utils import run_bass_kernel_spmd

F32 = mybir.dt.float32
BF16 = mybir.dt.bfloat16
AF = mybir.ActivationFunctionType
ALU = mybir.AluOpType
AX = mybir.AxisListType


class Buf:
    def __init__(self, t, name):
        self.t = t
        self.name = name
        self.w = None
        self.r = []

    def __getitem__(self, idx):
        return self.t[idx]


class Ctx:
    NDS = 24

    def __init__(self, nc, es):
        self.nc, self.es = nc, es
        self.engs = {'pe': nc.tensor, 'act': nc.scalar, 'dve': nc.vector,
                     'pool': nc.gpsimd, 'sp': nc.sync}
        self.sem = {k: es.enter_context(nc.semaphore('s_' + k)) for k in ['pe', 'act', 'dve', 'pool']}
        self.cnt = {k: 0 for k in self.sem}
        self.dsem = [es.enter_context(nc.semaphore('d%d' % i)) for i in range(self.NDS)]
        self.dcnt = [0] * self.NDS
        self.dnext = 0
        self.seen = {e: {} for e in self.engs}
        self.nbuf = 0

    def sb(self, shape, dt, name=None, es=None):
        self.nbuf += 1
        name = name or 'sb%d' % self.nbuf
        t = (es or self.es).enter_context(self.nc.sbuf_tensor(name, list(shape), dt))
        return Buf(t, name)

    def ps(self, shape, dt, name=None, es=None):
        self.nbuf += 1
        name = name or 'ps%d' % self.nbuf
        t = (es or self.es).enter_context(self.nc.psum_tensor(name, list(shape), dt))
        return Buf(t, name)

    def dram(self, name, shape, dt, kind='Internal'):
        t = self.nc.dram_tensor(name, list(shape), dt, kind=kind)
        return Buf(t.ap(), name)

    def _need(self, e, tk, raw):
        if tk is None:
            return
        key, val = tk
        if key == e and e == 'pe':
            return
        if self.seen[e].get(key, 0) >= val:
            return
        if isinstance(key, str):
            self.engs[e].wait_ge(self.sem[key], val)
        else:
            self.engs[e].wait_ge(self.dsem[key[1]], 16 * val)
        self.seen[e][key] = val

    def _deps(self, e, reads, writes):
        for b in reads:
            self._need(e, b.w, True)
        for b in writes:
            self._need(e, b.w, False)
            for tk in b.r:
                self._need(e, tk, False)

    def _commit(self, tk, reads, writes):
        for b in reads:
            b.r.append(tk)
            if len(b.r) > 12:
                d = {}
                for k, v in b.r:
                    d[k] = max(d.get(k, 0), v)
                b.r = list(d.items())
        for b in writes:
            b.w = tk
            b.r = []

    def op(self, e, fn, reads=(), writes=()):
        self._deps(e, reads, writes)
        ins = fn(self.engs[e])
        self.cnt[e] += 1
        ins.then_inc(self.sem[e], 1)
        self._commit((e, self.cnt[e]), reads, writes)
        return ins

    def mm(self, items, reads, out, start=True, stop=True):
        e = 'pe'
        self._deps(e, reads, [out])
        n = len(items)
        ins = None
        for i, (o, l, r) in enumerate(items):
            ins = self.nc.tensor.matmul(o, l, r, start=(start and i == 0), stop=(stop and i == n - 1))
        self.cnt[e] += 1
        ins.then_inc(self.sem[e], 1)
        self._commit((e, self.cnt[e]), reads, [out])

    def mm_raw(self, fns, reads, out):
        e = 'pe'
        self._deps(e, reads, [out])
        ins = None
        for f in fns:
            ins = f(self.nc.tensor)
        self.cnt[e] += 1
        ins.then_inc(self.sem[e], 1)
        self._commit((e, self.cnt[e]), reads, [out])

    def dma(self, q, out_ap, in_ap, reads=(), writes=(), **kw):
        i = self.dnext
        self.dnext = (self.dnext + 1) % self.NDS
        if self.dcnt[i] > 0:
            self._need(q, (('d', i), self.dcnt[i]), True)
        self._deps(q, reads, writes)
        ins = self.engs[q].dma_start(out=out_ap, in_=in_ap, **kw)
        ins.then_inc(self.dsem[i], 16)
        self.dcnt[i] += 1
        self._commit((('d', i), self.dcnt[i]), reads, writes)

    def barrier(self):
        tks = [(k, self.cnt[k]) for k in self.cnt if self.cnt[k] > 0]
        tks += [(('d', i), self.dcnt[i]) for i in range(self.NDS) if self.dcnt[i] > 0]
        for e in self.engs:
            for tk in tks:
                if tk[0] == e:
                    continue
                self._need(e, tk, True)

    def final_wait(self):
        tks = [(k, self.cnt[k]) for k in self.cnt if self.cnt[k] > 0]
        tks += [(('d', i), self.dcnt[i]) for i in range(self.NDS) if self.dcnt[i] > 0]
        for tk in tks:
            self._need('sp', tk, True)

def _vec(c, dst_ap, dstb, srcb, src_ap):
    c.dma('sp', dst_ap, src_ap, reads=[srcb], writes=[dstb], allow_slow_non_contiguous=True)


def mixer_lru(c, g, I, l, Z, YS, psf):
    with ExitStack() as s:
        CW = c.sb([128, 4, 4], F32, es=s); CB = c.sb([128, 4], F32, es=s)
        BA = c.sb([128, 4], F32, es=s); BX = c.sb([128, 4], F32, es=s); LM = c.sb([128, 4], F32, es=s)
        CN = c.sb([128, 4], F32, es=s)
        for j_ in range(4):
            _vec(c, CW[:, :, j_], CW, I['lru_conv_w'], I['lru_conv_w'].t[l, j_, :].rearrange("(h p) -> p h", p=128))
        _vec(c, CB[:, :], CB, I['lru_conv_b'], I['lru_conv_b'].t[l, :].rearrange("(h p) -> p h", p=128))
        _vec(c, BA[:, :], BA, I['lru_b_a'], I['lru_b_a'].t[l, :].rearrange("(h p) -> p h", p=128))
        _vec(c, BX[:, :], BX, I['lru_b_x'], I['lru_b_x'].t[l, :].rearrange("(h p) -> p h", p=128))
        _vec(c, LM[:, :], LM, I['lru_lambda'], I['lru_lambda'].t[l, :].rearrange("(h p) -> p h", p=128))
        c.op('act', lambda e: e.activation(CN[:, :], LM[:, :], AF.Exp, scale=-1.0), [LM], [CN])
        c.op('dve', lambda e: e.tensor_scalar_add(CN[:, :], CN[:, :], 1.0), [CN], [CN])
        c.op('act', lambda e: e.activation(CN[:, :], CN[:, :], AF.Ln), [CN], [CN])
        c.op('dve', lambda e: e.tensor_scalar_mul(CN[:, :], CN[:, :], -8.0), [CN], [CN])
        Wst = c.sb([128, 2, 4, 128], F32, es=s); Wb = c.sb([128, 2, 4, 128], BF16, es=s)
        c.dma('sp', Wst[:, 0, :, :], I['lru_w_a'].t[l, :, :, :].rearrange("h i j -> i h j"), reads=[I['lru_w_a']], writes=[Wst])
        c.dma('sp', Wst[:, 1, :, :], I['lru_w_x'].t[l, :, :, :].rearrange("h i j -> i h j"), reads=[I['lru_w_x']], writes=[Wst])
        c.op('act', lambda e: e.copy(Wb[:, :, :, :], Wst[:, :, :, :]), [Wst], [Wb])
        U = c.sb([128, T], F32, es=s); Gt = c.sb([128, T], F32, es=s); XC = c.sb([128, T], F32, es=s)
        R = c.sb([128, T], F32, es=s); IG = c.sb([128, T], F32, es=s); A = c.sb([128, T], F32, es=s)
        TH = c.sb([128, T], F32, es=s); P1 = c.sb([128, T], F32, es=s); HS = c.sb([128, T], F32, es=s)
        XCb = c.sb([128, T], BF16, es=s); Yb = c.sb([128, T], BF16, es=s)
        for h in range(4):
            c.dma('sp', U[:, :], Z.t[h * 128:(h + 1) * 128, :], reads=[Z], writes=[U])
            c.dma('sp', Gt[:, :], Z.t[512 + h * 128:512 + (h + 1) * 128, :], reads=[Z], writes=[Gt])
            c.op('dve', lambda e: e.tensor_scalar(XC[:, :], U[:, :], CW[:, h, 3:4], CB[:, h:h + 1], ALU.mult, ALU.add), [U, CW, CB], [XC])
            for j in range(3):
                sh = 3 - j
                c.op('dve', lambda e: e.scalar_tensor_tensor(XC[:, sh:T], U[:, 0:T - sh], CW[:, h, j:j + 1], XC[:, sh:T], ALU.mult, ALU.add), [U, CW, XC], [XC])
            c.op('act', lambda e: e.copy(XCb[:, :], XC[:, :]), [XC], [XCb])
            for (t0, n) in CH512:
                ps = psf()
                c.mm([(ps[:, 0:n], Wb[:, 0, h, :], XCb[:, t0:t0 + n])], [Wb, XCb], ps)
                c.op('act', lambda e: e.activation(R[:, t0:t0 + n], ps[:, 0:n], AF.Sigmoid, bias=BA[:, h:h + 1]), [ps, BA], [R])
                ps2 = psf()
                c.mm([(ps2[:, 0:n], Wb[:, 1, h, :], XCb[:, t0:t0 + n])], [Wb, XCb], ps2)
                c.op('act', lambda e: e.activation(IG[:, t0:t0 + n], ps2[:, 0:n], AF.Sigmoid, bias=BX[:, h:h + 1]), [ps2, BX], [IG])
            c.op('act', lambda e: e.activation(A[:, :], R[:, :], AF.Exp, scale=CN[:, h:h + 1]), [R, CN], [A])
            c.op('act', lambda e: e.activation(TH[:, :], R[:, :], AF.Tanh, scale=CN[:, h:h + 1]), [R, CN], [TH])
            c.op('dve', lambda e: e.tensor_scalar(P1[:, :], TH[:, :], -1.0, 1.0, ALU.mult, ALU.add), [TH], [P1])
            c.op('dve', lambda e: e.reciprocal(P1[:, :], P1[:, :]), [P1], [P1])
            c.op('dve', lambda e: e.scalar_tensor_tensor(P1[:, :], TH[:, :], -2.0, P1[:, :], ALU.mult, ALU.mult), [TH, P1], [P1])
            c.op('act', lambda e: e.activation(P1[:, :], P1[:, :], AF.Sqrt), [P1], [P1])
            c.op('dve', lambda e: e.tensor_tensor(IG[:, :], IG[:, :], XC[:, :], ALU.mult), [IG, XC], [IG])
            c.op('dve', lambda e: e.tensor_tensor(P1[:, :], P1[:, :], IG[:, :], ALU.mult), [P1, IG], [P1])
            c.op('dve', lambda e: e.tensor_tensor_scan(HS[:, :], A[:, :], P1[:, :], 0.0, ALU.mult, ALU.add), [A, P1], [HS])
            c.op('act', lambda e: e.activation(Gt[:, :], Gt[:, :], AF.Gelu_apprx_tanh), [Gt], [Gt])
            c.op('dve', lambda e: e.tensor_tensor(Yb[:, :], HS[:, :], Gt[:, :], ALU.mult), [HS, Gt], [Yb])
            c.dma('sp', YS.t[h * 128:(h + 1) * 128, :], Yb[:, :], reads=[Yb], writes=[YS])


def mixer_pool(c, g, I, l, Z, YS, psf, IC):
    with ExitStack() as s:
        PSc = c.sb([128, 4], F32, es=s)
        _vec(c, PSc[:, :], PSc, I['pool_scale'], I['pool_scale'].t[l, :].rearrange("(h p) -> p h", p=128))
        Wst = c.sb([128, 4, 128], F32, es=s); Wb = c.sb([128, 4, 128], BF16, es=s)
        c.dma('sp', Wst[:, :, :], I['pool_w'].t[l, :, :, :].rearrange("h i j -> i h j"), reads=[I['pool_w']], writes=[Wst])
        c.op('act', lambda e: e.copy(Wb[:, :, :], Wst[:, :, :]), [Wst], [Wb])
        TP = T + 16
        Up = c.sb([128, TP], F32, es=s); Pa = c.sb([128, TP], F32, es=s); Pb = c.sb([128, TP], F32, es=s)
        PL = c.sb([128, T], F32, es=s); PLb = c.sb([128, T], BF16, es=s); Yb = c.sb([128, T], BF16, es=s)
        t16 = c.sb([128, 16], F32, es=s)
        for b_ in (Up, Pa, Pb):
            c.op('dve', lambda e: e.memset(b_[:, 0:16], 0.0), [], [b_])
        for gi in range(4):
            win = 2 ** (gi + 1)
            c.dma('sp', Up[:, 16:TP], Z.t[1024 + gi * 128:1024 + (gi + 1) * 128, :], reads=[Z], writes=[Up])
            src = Up
            dsts = [Pa, Pb]
            for si, step in enumerate([1, 2, 4, 8][:gi + 1]):
                dst = dsts[si % 2]
                c.op('dve', lambda e: e.tensor_tensor(dst[:, 16:TP], src[:, 16:TP], src[:, 16 - step:TP - step], ALU.add), [src], [dst])
                src = dst
            c.op('dve', lambda e: e.scalar_tensor_tensor(PL[:, :], src[:, 16:TP], 1.0 / win, Up[:, 16:TP], ALU.mult, ALU.subtract), [src, Up], [PL])
            c.op('dve', lambda e: e.tensor_tensor(t16[:, :], src[:, 16:32], IC[:, gi, :], ALU.mult), [src, IC], [t16])
            c.op('dve', lambda e: e.tensor_tensor(PL[:, 0:16], t16[:, :], Up[:, 16:32], ALU.subtract), [t16, Up], [PL])
            c.op('act', lambda e: e.copy(PLb[:, :], PL[:, :]), [PL], [PLb])
            for (t0, n) in CH512:
                ps = psf()
                c.mm([(ps[:, 0:n], Wb[:, gi, :], PLb[:, t0:t0 + n])], [Wb, PLb], ps)
                c.op('dve', lambda e: e.tensor_scalar_mul(Yb[:, t0:t0 + n], ps[:, 0:n], PSc[:, gi:gi + 1]), [ps, PSc], [Yb])
            c.dma('sp', YS.t[512 + gi * 128:512 + (gi + 1) * 128, :], Yb[:, :], reads=[Yb], writes=[YS])


def mixer_hgrn(c, g, I, l, Z, YS, psf, psb, MK, HM, idb, ones_b, epsb):
    with ExitStack() as s:
        LBS = c.sb([128, 2, 4], F32, es=s)
        for l_ in range(2):
            _vec(c, LBS[:, l_, :], LBS, I['hgrn_lower_bounds'], I['hgrn_lower_bounds'].t[l_, :].rearrange("(h p) -> p h", p=128))
        LB = c.sb([128, 4], F32, es=s); OML = c.sb([128, 4], F32, es=s); NOML = c.sb([128, 4], F32, es=s)
        if l == 0:
            c.op('dve', lambda e: e.memset(LB[:, :], 0.0), [], [LB])
        else:
            SM = c.sb([128, 4], F32, es=s)
            c.op('act', lambda e: e.activation(LBS[:, :, :], LBS[:, :, :], AF.Exp), [LBS], [LBS])
            c.op('dve', lambda e: e.tensor_tensor(SM[:, :], LBS[:, 0, :], LBS[:, 1, :], ALU.add), [LBS], [SM])
            c.op('dve', lambda e: e.reciprocal(SM[:, :], SM[:, :]), [SM], [SM])
            c.op('dve', lambda e: e.tensor_tensor(LB[:, :], LBS[:, 1, :], SM[:, :], ALU.mult), [LBS, SM], [LB])
        c.op('dve', lambda e: e.tensor_scalar(OML[:, :], LB[:, :], -1.0, 1.0, ALU.mult, ALU.add), [LB], [OML])
        c.op('dve', lambda e: e.tensor_scalar_add(NOML[:, :], LB[:, :], -1.0), [LB], [NOML])
        HNW = c.sb([128, 1], F32, es=s)
        MK = c.sb([128, T], F32, es=s)
        c.op('dve', lambda e: e.memset(MK[:, :], 1.0), [], [MK])
        c.op('dve', lambda e: e.memset(MK[:, 0:1], 0.0), [], [MK])
        c.op('dve', lambda e: e.memset(MK[:, 16:T:64], 0.0), [], [MK])
        _vec(c, HNW[:, :], HNW, I['hgrn_norm'], I['hgrn_norm'].t[l, :].rearrange("(p o) -> p o", o=1))
        def ft():
            return c.sb([128, T], F32, es=s)
        Qf, Fz, Vf, OG, SIG, Gg, Kk, Bc, TMP, EK, Oo = [ft() for _ in range(11)]
        QT, KT, KH, VB, Yb, SQ = [c.sb([128, T], BF16, es=s) for _ in range(6)]
        Dd = c.sb([128, 33], F32, es=s)
        S = c.sb([128, 128], F32, es=s)
        Sbf = c.sb([128, 33, 128], BF16, es=s)
        ktok = [c.sb([128, 128], BF16, es=s) for _ in range(2)]
        vtok = [c.sb([128, 128], BF16, es=s) for _ in range(2)]
        Asb = [c.sb([128, 128], BF16, es=s) for _ in range(2)]
        for h in range(4):
            c.dma('sp', Qf[:, :], Z.t[1536 + h * 128:1536 + (h + 1) * 128, :], reads=[Z], writes=[Qf])
            c.dma('sp', Fz[:, :], Z.t[2048 + h * 128:2048 + (h + 1) * 128, :], reads=[Z], writes=[Fz])
            c.dma('sp', Vf[:, :], Z.t[2560 + h * 128:2560 + (h + 1) * 128, :], reads=[Z], writes=[Vf])
            c.dma('sp', OG[:, :], Z.t[3072 + h * 128:3072 + (h + 1) * 128, :], reads=[Z], writes=[OG])
            c.op('act', lambda e: e.activation(SIG[:, :], Fz[:, :], AF.Sigmoid), [Fz], [SIG])
            c.op('dve', lambda e: e.tensor_scalar(Gg[:, :], SIG[:, :], OML[:, h:h + 1], LB[:, h:h + 1], ALU.mult, ALU.add), [SIG, OML, LB], [Gg])
            c.op('act', lambda e: e.activation(Gg[:, :], Gg[:, :], AF.Ln), [Gg], [Gg])
            c.op('dve', lambda e: e.tensor_scalar(Kk[:, :], SIG[:, :], NOML[:, h:h + 1], OML[:, h:h + 1], ALU.mult, ALU.add), [SIG, OML, NOML], [Kk])
            c.op('dve', lambda e: e.tensor_tensor_scan(Bc[:, :], MK[:, :], Gg[:, :], 0.0, ALU.mult, ALU.add), [MK, Gg], [Bc])
            c.op('act', lambda e: e.activation(TMP[:, :], Bc[:, :], AF.Exp), [Bc], [TMP])
            c.op('dve', lambda e: e.tensor_tensor(QT[:, :], Qf[:, :], TMP[:, :], ALU.mult), [Qf, TMP], [QT])
            c.op('dve', lambda e: e.tensor_scalar(TMP[:, :], Bc[:, :], -1.0, 80.0, ALU.mult, ALU.min), [Bc], [TMP])
            c.op('act', lambda e: e.activation(TMP[:, :], TMP[:, :], AF.Exp), [TMP], [TMP])
            c.op('dve', lambda e: e.tensor_tensor(KT[:, :], Kk[:, :], TMP[:, :], ALU.mult), [Kk, TMP], [KT])
            c.op('act', lambda e: e.activation(Dd[:, 0:1], Bc[:, 15:16], AF.Exp), [Bc], [Dd])
            c.op('act', lambda e: e.activation(Dd[:, 1:33], Bc[:, 79:T:64], AF.Exp), [Bc], [Dd])
            chunks = [(0, 16)] + [(16 + 64 * n, 64) for n in range(32)]
            for (c0, cl) in chunks:
                c.op('act', lambda e: e.activation(EK[:, c0:c0 + cl], Bc[:, c0:c0 + cl], AF.Exp, bias=Bc[:, c0 + cl - 1:c0 + cl], scale=-1.0), [Bc], [EK])
            c.op('dve', lambda e: e.tensor_tensor(KH[:, :], Kk[:, :], EK[:, :], ALU.mult), [Kk, EK], [KH])
            c.op('act', lambda e: e.copy(VB[:, :], Vf[:, :]), [Vf], [VB])
            c.op('dve', lambda e: e.memset(S[:, :], 0.0), [], [S])
            n_ = 0
            def tposes(bi):
                t0, L = BLKS[bi]
                kt = ktok[bi % 2]; vt = vtok[bi % 2]
                pb1 = psb()
                c.mm_raw([lambda e: e.transpose(pb1[0:L, 0:128], KH[:, t0:t0 + L], idb[:, :])], [KH, idb], pb1)
                c.op('act', lambda e: e.copy(kt[0:L, :], pb1[0:L, 0:128]), [pb1], [kt])
                pb2 = psb()
                c.mm_raw([lambda e: e.transpose(pb2[0:L, 0:128], VB[:, t0:t0 + L], idb[:, :])], [VB, idb], pb2)
                c.op('act', lambda e: e.copy(vt[0:L, :], pb2[0:L, 0:128]), [pb2], [vt])
            tposes(0)
            for bi, (t0, L) in enumerate(BLKS):
                kt = ktok[bi % 2]; vt = vtok[bi % 2]; ab = Asb[bi % 2]
                if bi + 1 < len(BLKS):
                    tposes(bi + 1)
                cks = [(0, 16)] if bi == 0 else [(0, 64), (64, 64)]
                n_first = n_
                for (r0, rl) in cks:
                    ps = psf()
                    c.mm([(ps[:, 0:128], kt[r0:r0 + rl, :], vt[r0:r0 + rl, :])], [kt, vt], ps)
                    c.op('act', lambda e: e.copy(Sbf[:, n_, :], S[:, :]), [S], [Sbf])
                    c.op('dve', lambda e: e.scalar_tensor_tensor(S[:, :], S[:, :], Dd[:, n_:n_ + 1], ps[:, 0:128], ALU.mult, ALU.add), [S, Dd, ps], [S])
                    n_ += 1
                psA = psf()
                c.mm([(psA[0:L, 0:L], KT[:, t0:t0 + L], QT[:, t0:t0 + L])], [KT, QT], psA)
                c.op('dve', lambda e: e.tensor_tensor(ab[0:L, 0:L], psA[0:L, 0:L], HM[0:L, 0:L], ALU.mult), [psA, HM], [ab])
                pso = psf()
                items = [(pso[:, 0:L], vt[0:L, :], ab[0:L, 0:L])]
                for ci, (r0, rl) in enumerate(cks):
                    items.append((pso[:, r0:r0 + rl], Sbf[:, n_first + ci, :], QT[:, t0 + r0:t0 + r0 + rl]))
                c.mm(items, [vt, ab, Sbf, QT], pso)
                c.op('act', lambda e: e.copy(Oo[:, t0:t0 + L], pso[:, 0:L]), [pso], [Oo])
            c.op('act', lambda e: e.activation(SQ[:, :], Oo[:, :], AF.Square), [Oo], [SQ])
            for (t0, n) in CH512:
                ps = psf()
                c.mm([(ps[:, 0:n], ones_b[:, :], SQ[:, t0:t0 + n])], [ones_b, SQ], ps)
                c.op('act', lambda e: e.activation(TMP[:, t0:t0 + n], ps[:, 0:n], AF.Sqrt, bias=epsb[:, 0:1], scale=1.0 / 128), [ps, epsb], [TMP])
            c.op('dve', lambda e: e.reciprocal(TMP[:, :], TMP[:, :]), [TMP], [TMP])
            c.op('dve', lambda e: e.scalar_tensor_tensor(Oo[:, :], Oo[:, :], HNW[:, 0:1], TMP[:, :], ALU.mult, ALU.mult), [Oo, HNW, TMP], [Oo])
            c.op('act', lambda e: e.activation(OG[:, :], OG[:, :], AF.Silu), [OG], [OG])
            c.op('dve', lambda e: e.tensor_tensor(Yb[:, :], Oo[:, :], OG[:, :], ALU.mult), [Oo, OG], [Yb])
            c.dma('sp', YS.t[1024 + h * 128:1024 + (h + 1) * 128, :], Yb[:, :], reads=[Yb], writes=[YS])


def mixer_attn(c, g, I, l, Z, YS, psf, psb, idb, RB, EB0, EB1, EBM0, EBMM, epsb, lambda_init):
    SCALE = 0.125
    with ExitStack() as s:
        DL = c.sb([128, 4, 64], F32, es=s)
        c.dma('sp', DL[:, :, :], I['diff_lambda'].t[l, :, :].rearrange("a b -> (a b)").partition_broadcast(128), reads=[I['diff_lambda']], writes=[DL])
        PR = c.sb([128, 2, 64], F32, es=s); SS = c.sb([128, 2], F32, es=s); LAMN = c.sb([128, 1], F32, es=s)
        c.op('dve', lambda e: e.tensor_tensor(PR[:, 0, :], DL[:, 0, :], DL[:, 1, :], ALU.mult), [DL], [PR])
        c.op('dve', lambda e: e.tensor_tensor(PR[:, 1, :], DL[:, 2, :], DL[:, 3, :], ALU.mult), [DL], [PR])
        c.op('dve', lambda e: e.reduce_sum(SS[:, :], PR[:, :, :], AX.X), [PR], [SS])
        c.op('act', lambda e: e.activation(SS[:, :], SS[:, :], AF.Exp), [SS], [SS])
        c.op('dve', lambda e: e.tensor_tensor(LAMN[:, :], SS[:, 1:2], SS[:, 0:1], ALU.subtract), [SS], [LAMN])
        c.op('dve', lambda e: e.tensor_scalar_add(LAMN[:, :], LAMN[:, :], -lambda_init), [LAMN], [LAMN])
        SUBW = c.sb([128, 128], F32, es=s)
        c.dma('sp', SUBW[:, :], I['diff_subln'].t[l, :].partition_broadcast(128), reads=[I['diff_subln']], writes=[SUBW])
        c.op('dve', lambda e: e.tensor_scalar_mul(SUBW[:, :], SUBW[:, :], 1.0 - lambda_init), [SUBW], [SUBW])
        Qf = c.sb([128, T], F32, es=s); Kf = c.sb([128, T], F32, es=s); Vf = c.sb([128, T], F32, es=s)
        qb = c.sb([128, T], BF16, es=s); kb = c.sb([128, T], BF16, es=s); vb = c.sb([128, T], BF16, es=s)
        V1 = c.sb([128, 17, 132], BF16, es=s)
        c.op('dve', lambda e: e.memset(V1[:, :, 128:129], 1.0), [], [V1])
        YD = c.sb([128, T], BF16, es=s)
        Eb = [c.sb([128, 512], BF16, es=s) for _ in range(3)]
        Om = [c.sb([128, 4, 132], F32, es=s) for _ in range(2)]
        rr = [c.sb([128, 2], F32, es=s) for _ in range(2)]
        ot = [c.sb([128, 128], F32, es=s) for _ in range(2)]
        jk = [c.sb([128, 128], F32, es=s) for _ in range(2)]
        s1 = [c.sb([128, 1], F32, es=s) for _ in range(2)]
        yb = [c.sb([128, 128], BF16, es=s) for _ in range(2)]
        PACC = g.PACC
        g.ei = 0
        g.ci = 0

        def combine(h, Lq, j, t0):
            i = g.ci % 2; g.ci += 1
            r = rr[i]; o = ot[i]; sq = jk[i]; s_ = s1[i]; y = yb[i]
            c.op('dve', lambda e: e.reciprocal(r[0:Lq, 0:1], Om[0][0:Lq, j, 128:129]), [Om[0]], [r])
            c.op('dve', lambda e: e.reciprocal(r[0:Lq, 1:2], Om[1][0:Lq, j, 128:129]), [Om[1]], [r])
            c.op('dve', lambda e: e.tensor_tensor(r[0:Lq, 1:2], r[0:Lq, 1:2], LAMN[0:Lq, 0:1], ALU.mult), [r, LAMN], [r])
            c.op('dve', lambda e: e.tensor_scalar_mul(o[0:Lq, :], Om[0][0:Lq, j, 0:128], r[0:Lq, 0:1]), [Om[0], r], [o])
            c.op('dve', lambda e: e.scalar_tensor_tensor(o[0:Lq, :], Om[1][0:Lq, j, 0:128], r[0:Lq, 1:2], o[0:Lq, :], ALU.mult, ALU.add), [Om[1], r, o], [o])
            c.op('act', lambda e: e.activation(sq[0:Lq, :], o[0:Lq, :], AF.Square, accum_out=s_[0:Lq, 0:1]), [o], [sq, s_])
            c.op('act', lambda e: e.activation(s_[0:Lq, :], s_[0:Lq, :], AF.Sqrt, bias=epsb[0:Lq, 0:1], scale=1.0 / 128), [s_, epsb], [s_])
            c.op('dve', lambda e: e.reciprocal(s_[0:Lq, :], s_[0:Lq, :]), [s_], [s_])
            c.op('dve', lambda e: e.scalar_tensor_tensor(y[0:Lq, :], o[0:Lq, :], s_[0:Lq, 0:1], SUBW[0:Lq, :], ALU.mult, ALU.mult), [o, s_, SUBW], [y])
            pb = psb()
            c.mm_raw([lambda e: e.transpose(pb[:, 0:Lq], y[0:Lq, :], idb[0:Lq, 0:Lq])], [y, idb], pb)
            c.op('act', lambda e: e.copy(YD[:, t0:t0 + Lq], pb[:, 0:Lq]), [pb], [YD])

        for h in range(4):
            c.dma('sp', Qf[:, :], Z.t[3584 + h * 128:3584 + (h + 1) * 128, :], reads=[Z], writes=[Qf])
            c.dma('sp', Kf[:, :], Z.t[4096 + h * 128:4096 + (h + 1) * 128, :], reads=[Z], writes=[Kf])
            c.dma('sp', Vf[:, :], Z.t[4608 + h * 128:4608 + (h + 1) * 128, :], reads=[Z], writes=[Vf])
            c.op('act', lambda e: e.copy(qb[:, :], Qf[:, :]), [Qf], [qb])
            c.op('act', lambda e: e.copy(kb[:, :], Kf[:, :]), [Kf], [kb])
            c.op('act', lambda e: e.copy(vb[:, :], Vf[:, :]), [Vf], [vb])
            for bi, (t0, L) in enumerate(BLKS):
                pb = psb()
                c.mm_raw([lambda e: e.transpose(pb[0:L, 0:128], vb[:, t0:t0 + L], idb[:, :])], [vb, idb], pb)
                c.op('act', lambda e: e.copy(V1[0:L, bi, 0:128], pb[0:L, 0:128]), [pb], [V1])
            cb = RB[:, 31 * 4 + h:31 * 4 + h + 1]
            for m in range(2):
                ms = slice(m * 64, (m + 1) * 64)
                ps = psf(4, 6)
                c.mm([(ps[0:16, 0:16], kb[ms, 0:16], qb[ms, 0:16])], [kb, qb], ps)
                eb = Eb[g.ei % 3]; g.ei += 1
                c.op('act', lambda e: e.activation(eb[0:16, 0:16], ps[0:16, 0:16], AF.Exp, scale=SCALE), [ps], [eb])
                c.op('dve', lambda e: e.tensor_tensor(eb[0:16, 0:16], eb[0:16, 0:16], EBMM[0:16, h, :], ALU.mult), [eb, EBMM], [eb])
                pa = PACC[0]
                c.mm([(pa[0:16, 0:129], eb[0:16, 0:16], V1[0:16, 0, 0:129])], [eb, V1], pa)
                c.op('act', lambda e: e.copy(Om[m][0:16, 0, 0:129], pa[0:16, 0:129]), [pa], [Om[m]])
            combine(h, 16, 0, 0)
            for qc in range(4):
                ct0 = 16 + 512 * qc
                for m in range(2):
                    ms = slice(m * 64, (m + 1) * 64)
                    tasks = ['meta'] + list(range(4 * qc + 4))
                    def score(tk):
                        ps = psf(4, 6)
                        if tk == 'meta':
                            c.mm([(ps[0:16, 0:512], kb[ms, 0:16], qb[ms, ct0:ct0 + 512])], [kb, qb], ps)
                        else:
                            kt0 = 16 + 128 * tk
                            j0 = max(tk - 4 * qc, 0)
                            ncols = (4 - j0) * 128
                            q0 = ct0 + j0 * 128
                            c.mm([(ps[:, 0:ncols], kb[ms, kt0:kt0 + 128], qb[ms, q0:q0 + ncols])], [kb, qb], ps)
                        return ps
                    def consume(tk, ps):
                        eb = Eb[g.ei % 3]; g.ei += 1
                        if tk == 'meta':
                            if qc == 0:
                                c.op('act', lambda e: e.activation(eb[0:16, 0:128], ps[0:16, 0:128], AF.Exp, scale=SCALE), [ps], [eb])
                                c.op('dve', lambda e: e.tensor_tensor(eb[0:16, 0:128], eb[0:16, 0:128], EBM0[0:16, h, :], ALU.mult), [eb, EBM0], [eb])
                                c.op('act', lambda e: e.activation(eb[0:16, 128:512], ps[0:16, 128:512], AF.Exp, scale=SCALE, bias=cb[0:16, :]), [ps, RB], [eb])
                            else:
                                c.op('act', lambda e: e.activation(eb[0:16, 0:512], ps[0:16, 0:512], AF.Exp, scale=SCALE, bias=cb[0:16, :]), [ps, RB], [eb])
                            for j in range(4):
                                c.mm([(PACC[j][:, 0:129], eb[0:16, j * 128:(j + 1) * 128], V1[0:16, 0, 0:129])], [eb, V1], PACC[j], start=True, stop=False)
                            return
                        kbi = tk
                        j0 = max(kbi - 4 * qc, 0)
                        ncols = (4 - j0) * 128
                        nnear = 0
                        for j in range(j0, 4):
                            if 4 * qc + j - kbi <= 1:
                                nnear += 1
                        if nnear > 0:
                            c.op('act', lambda e: e.activation(eb[:, 0:nnear * 128], ps[:, 0:nnear * 128], AF.Exp, scale=SCALE), [ps], [eb])
                        if ncols > nnear * 128:
                            c.op('act', lambda e: e.activation(eb[:, nnear * 128:ncols], ps[:, nnear * 128:ncols], AF.Exp, scale=SCALE, bias=cb), [ps, RB], [eb])
                        for j in range(j0, 4):
                            dl = 4 * qc + j - kbi
                            co = (j - j0) * 128
                            if dl == 0:
                                c.op('dve', lambda e: e.tensor_tensor(eb[:, co:co + 128], eb[:, co:co + 128], EB0[:, h, :], ALU.mult), [eb, EB0], [eb])
                            elif dl == 1:
                                c.op('dve', lambda e: e.tensor_tensor(eb[:, co:co + 128], eb[:, co:co + 128], EB1[:, h, :], ALU.mult), [eb, EB1], [eb])
                        for j in range(j0, 4):
                            dl = 4 * qc + j - kbi
                            co = (j - j0) * 128
                            c.mm([(PACC[j][:, 0:129], eb[:, co:co + 128], V1[:, kbi + 1, 0:129])], [eb, V1], PACC[j], start=False, stop=(dl == 0))
                    psn = score(tasks[0])
                    for ti, tk in enumerate(tasks):
                        pscur = psn
                        if ti + 1 < len(tasks):
                            psn = score(tasks[ti + 1])
                        consume(tk, pscur)
                    for j in range(4):
                        c.op('dve', lambda e: e.tensor_copy(Om[m][:, j, 0:129], PACC[j][:, 0:129]), [PACC[j]], [Om[m]])
                for j in range(4):
                    combine(h, 128, j, ct0 + j * 128)
            c.dma('sp', YS.t[1536 + h * 128:1536 + (h + 1) * 128, :], YD[:, :], reads=[YD], writes=[YS])

D = 2048
T = 2064
NM = 16
KC = 16
FH = 5632
NIN = 13312
EPS = 1e-6
SUP = 1032
SUBS = [(0, 344), (344, 344), (688, 344)]
CH512 = [(0, 512), (512, 512), (1024, 512), (1536, 512), (2048, 16)]
BLKS = [(0, 16)] + [(16 + 128 * j, 128) for j in range(16)]
DEPTH = 2


def t5_onehot():
    n = np.arange(0, 272)
    max_exact = 16
    nf = np.maximum(n, 1).astype(np.float32)
    large = max_exact + (np.log(nf / np.float32(max_exact)) / np.float32(math.log(128 / max_exact)) * np.float32(32 - max_exact)).astype(np.int32)
    large = np.minimum(large, 31)
    b = np.where(n < max_exact, n, large)
    oh = np.zeros((32, 400), np.float32)
    oh[b, 128 + n] = 1.0
    return oh


class K:
    pass


def build_program(dbg=(), stop_after=None, nlayers=DEPTH):
    nc = bass.Bass("TRN2", target_bir_lowering=False)
    es = ExitStack()
    with es:
        c = Ctx(nc, es)
        g = K()
        I = {}
        def inp(name, shape):
            I[name] = c.dram(name, shape, F32, kind="ExternalInput")
        inp('x', [2048, D]); inp('meta_tokens', [NM, D]); inp('rel_bias', [32, 4]); inp('hgrn_lower_bounds', [2, 512])
        for n_ in ['norm_mix_pre', 'norm_mix_post', 'norm_ffn_pre', 'norm_ffn_post']:
            inp(n_, [2, D])
        inp('w_in', [2, D, NIN]); inp('lru_conv_w', [2, 4, 512]); inp('lru_conv_b', [2, 512])
        inp('lru_w_a', [2, 4, 128, 128]); inp('lru_b_a', [2, 512]); inp('lru_w_x', [2, 4, 128, 128]); inp('lru_b_x', [2, 512])
        inp('lru_lambda', [2, 512]); inp('pool_w', [2, 4, 128, 128]); inp('pool_scale', [2, 512]); inp('hgrn_norm', [2, 128])
        inp('diff_lambda', [2, 4, 64]); inp('diff_subln', [2, 128]); inp('w_branch', [2, 4, 512, D]); inp('w_out', [2, D, D])
        inp('ffn_w_gu', [2, D, 2 * FH]); inp('ffn_w_down', [2, FH, D]); inp('t5oh', [32, 400])
        OUT = c.dram('out', [2048, D], F32, kind="ExternalOutput")

        def scr(name, shape, dt):
            return c.dram(name, shape, dt, kind=("ExternalOutput" if name in dbg else "Internal"))
        H = scr('H', [D, T], F32)
        HN = scr('HN', [D, T], BF16)
        Z = scr('Z', [5120, T], F32)
        YS = scr('YS', [D, T], BF16)
        MIX = scr('MIX', [D, T], F32)
        BD = scr('BD', [4, 400], F32)
        Hv = H.t.rearrange("(k p) t -> p k t", p=128)
        HNv = HN.t.rearrange("(k p) t -> p k t", p=128)
        YSv = YS.t.rearrange("(k p) t -> p k t", p=128)
        MIXv = MIX.t.rearrange("(k p) t -> p k t", p=128)

        PSF = [c.ps([128, 512], F32, name='psf%d' % i) for i in range(6)]
        PSB = [c.ps([128, 1024], BF16, name='psb%d' % i) for i in range(2)]
        g.pi = 0
        def psf(lo=0, hi=6):
            g.pi += 1
            return PSF[lo + g.pi % (hi - lo)]
        g.pb = 0
        g.PACC = PSF[0:4]
        def psb():
            g.pb += 1
            return PSB[g.pb % 2]

        ones_f = c.sb([128, 128], F32, 'ones_f')
        c.op('pool', lambda e: e.memset(ones_f[:, :], 1.0), [], [ones_f])
        ones_b = c.sb([128, 128], BF16, 'ones_b')
        c.op('pool', lambda e: e.memset(ones_b[:, :], 1.0), [], [ones_b])
        idf = c.sb([128, 128], F32, 'idf')
        c.op('pool', lambda e: e.affine_select(idf[:, :], ones_f[:, :], [[-1, 128]], ALU.is_equal, 0.0, base=0, channel_multiplier=1), [ones_f], [idf])
        idb = c.sb([128, 128], BF16, 'idb')
        J128 = c.sb([128, 128], F32, 'J128')
        c.op('pool', lambda e: e.affine_select(J128[:, :], ones_f[:, :], [[1, 128]], ALU.is_equal, 0.0, base=-127, channel_multiplier=1), [ones_f], [J128])
        J16 = c.sb([16, 16], F32, 'J16')
        c.op('pool', lambda e: e.affine_select(J16[:, :], ones_f[0:16, 0:16], [[1, 16]], ALU.is_equal, 0.0, base=-15, channel_multiplier=1), [ones_f], [J16])
        epsb = c.sb([128, 1], F32, 'epsb')
        c.op('pool', lambda e: e.memset(epsb[:, :], EPS), [], [epsb])

        MK = None
        HM = c.sb([128, 128], F32, 'HM')
        c.op('pool', lambda e: e.affine_select(HM[:, :], ones_f[:, :], [[1, 128]], ALU.is_ge, 0.0, base=0, channel_multiplier=-1), [ones_f], [HM])
        c.op('pool', lambda e: e.memset(HM[0:64, 64:128], 0.0), [], [HM])
        IC = c.sb([128, 4, 16], F32, 'IC')
        ioti = c.sb([128, 16], mybir.dt.int32, 'ioti')
        c.op('pool', lambda e: e.iota(ioti[:, :], [[1, 16]], base=1, channel_multiplier=0), [], [ioti])
        iotf = c.sb([128, 16], F32, 'iotf')

        CM = c.sb([128, 128], F32, 'CM')
        c.op('pool', lambda e: e.affine_select(CM[:, :], ones_f[:, :], [[1, 128]], ALU.is_ge, 0.0, base=0, channel_multiplier=-1), [ones_f], [CM])
        c.barrier()
        c.op('dve', lambda e: e.tensor_copy(idb[:, :], idf[:, :]), [idf], [idb])
        c.op('dve', lambda e: e.tensor_copy(iotf[:, :], ioti[:, :]), [ioti], [iotf])
        for gi in range(4):
            c.op('dve', lambda e, gi=gi: e.tensor_scalar_min(IC[:, gi, :], iotf[:, :], float(2 ** (gi + 1))), [iotf], [IC])
        c.op('dve', lambda e: e.reciprocal(IC[:, :, :], IC[:, :, :]), [IC], [IC])

        def vecload(dst_ap, src_ap, dstb, srcb):
            c.dma('sp', dst_ap, src_ap, reads=[srcb], writes=[dstb], allow_slow_non_contiguous=True)

        NW = c.sb([128, 4, 2, 16], F32, 'NW')
        for i_, n_ in enumerate(['norm_mix_pre', 'norm_mix_post', 'norm_ffn_pre', 'norm_ffn_post']):
            for l in range(DEPTH):
                vecload(NW[:, i_, l, :], I[n_].t[l, :].rearrange("(k p) -> p k", p=128), NW, I[n_])

        def norm_sub(sc, src_v, srcb, t0, n, wk, l, dst_fn, dstb, resid=False, second=None):
            xt = sc.xt[sc.i % 2]; sq = sc.sq[sc.i % 2]; rs = sc.rs[sc.i % 2]; sc.i += 1
            c.dma('sp', xt[:, :, 0:n], src_v[:, :, t0:t0 + n], reads=[srcb], writes=[xt])
            if resid:
                ht = sc.ht[sc.i % 2]
                c.dma('sp', ht[:, :, 0:n], Hv[:, :, t0:t0 + n], reads=[H], writes=[ht])
            c.op('act', lambda e: e.activation(sq[:, :, 0:n], xt[:, :, 0:n], AF.Square), [xt], [sq])
            ps = psf(4, 6)
            c.mm([(ps[:, 0:n], ones_b[:, :], sq[:, k, 0:n]) for k in range(KC)], [ones_b, sq], ps)
            c.op('act', lambda e: e.activation(rs[:, 0:n], ps[:, 0:n], AF.Sqrt, bias=epsb[:, 0:1], scale=1.0 / D), [ps, epsb], [rs])
            c.op('dve', lambda e: e.reciprocal(rs[:, 0:n], rs[:, 0:n]), [rs], [rs])
            if not resid:
                for k in range(KC):
                    c.op('dve', lambda e, k=k: e.scalar_tensor_tensor(dst_fn(k), xt[:, k, 0:n], NW[:, wk, l, k:k + 1], rs[:, 0:n], ALU.mult, ALU.mult), [xt, NW, rs], [dstb])
            else:
                for k in range(KC):
                    c.op('dve', lambda e, k=k: e.scalar_tensor_tensor(xt[:, k, 0:n], xt[:, k, 0:n], NW[:, wk, l, k:k + 1], rs[:, 0:n], ALU.mult, ALU.mult), [xt, NW, rs], [xt])
                c.op('dve', lambda e: e.tensor_tensor(ht[:, :, 0:n], ht[:, :, 0:n], xt[:, :, 0:n], ALU.add), [ht, xt], [ht])
                c.dma('sp', Hv[:, :, t0:t0 + n], ht[:, :, 0:n], reads=[ht], writes=[H])
                if second is not None:
                    wk2, l2, dst_fn2, dstb2, after = second
                    c.op('act', lambda e: e.activation(sq[:, :, 0:n], ht[:, :, 0:n], AF.Square), [ht], [sq])
                    ps2 = psf(4, 6)
                    c.mm([(ps2[:, 0:n], ones_b[:, :], sq[:, k, 0:n]) for k in range(KC)], [ones_b, sq], ps2)
                    c.op('act', lambda e: e.activation(rs[:, 0:n], ps2[:, 0:n], AF.Sqrt, bias=epsb[:, 0:1], scale=1.0 / D), [ps2, epsb], [rs])
                    c.op('dve', lambda e: e.reciprocal(rs[:, 0:n], rs[:, 0:n]), [rs], [rs])
                    for k in range(KC):
                        c.op('dve', lambda e, k=k: e.scalar_tensor_tensor(dst_fn2(k), ht[:, k, 0:n], NW[:, wk2, l2, k:k + 1], rs[:, 0:n], ALU.mult, ALU.mult), [ht, NW, rs], [dstb2])
                    if after is not None:
                        after()

        class NormScr:
            def __init__(self, s, resid=False):
                self.i = 0
                self.xt = [c.sb([128, KC, 344], F32, es=s) for _ in range(2)]
                self.sq = [c.sb([128, KC, 344], BF16, es=s) for _ in range(2)]
                self.rs = [c.sb([128, 344], F32, es=s) for _ in range(2)]
                if resid:
                    self.ht = [c.sb([128, KC, 344], F32, es=s) for _ in range(2)]

        class GemmScr:
            def __init__(self, s, kc, wmax):
                self.i = 0
                self.kc = kc
                self.st = [c.sb([128, kc, wmax], F32, es=s) for _ in range(2)]
                self.wb = [c.sb([128, kc, wmax], BF16, es=s) for _ in range(2)]

        def gemm(gs, W2d, Wb, groups, x_fn, xbufs, subs, epi):
            kc = gs.kc
            def load(gi):
                col0, width = groups[gi]
                st = gs.st[gs.i % 2]; wb = gs.wb[gs.i % 2]; gs.i += 1
                c.dma('sp', st[:, :, 0:width], W2d[:, col0:col0 + width].rearrange("(k p) n -> p k n", p=128), reads=[Wb], writes=[st])
                c.op('dve', lambda e: e.tensor_copy(wb[:, :, 0:width], st[:, :, 0:width]), [st], [wb])
                return wb
            nxt = load(0)
            for gi, (col0, width) in enumerate(groups):
                wb = nxt
                if gi + 1 < len(groups):
                    nxt = load(gi + 1)
                for bi in range(width // 128):
                    for si, (t0, n) in enumerate(subs):
                        ps = psf(0, 4)
                        c.mm([(ps[:, 0:n], wb[:, k, bi * 128:(bi + 1) * 128], x_fn(k, t0, n)) for k in range(kc)], [wb] + xbufs, ps)
                        epi(ps, col0 + bi * 128, si, t0, n)

        with ExitStack() as s:
            xin = [c.sb([128, D], F32, es=s) for _ in range(2)]
            stg = [c.sb([128, KC, 128], F32, es=s) for _ in range(2)]
            for bi, (t0, L) in enumerate(BLKS):
                xi = xin[bi % 2]; sg = stg[bi % 2]
                if bi == 0:
                    c.dma('sp', xi[0:L, :], I['meta_tokens'].t[:, :], reads=[I['meta_tokens']], writes=[xi])
                else:
                    c.dma('sp', xi[0:L, :], I['x'].t[t0 - 16:t0 - 16 + L, :], reads=[I['x']], writes=[xi])
                for kg in range(4):
                    ps = psf(0, 4)
                    c.mm_raw([(lambda e, j=j: e.transpose(ps[:, j * 128:j * 128 + L], xi[0:L, (4 * kg + j) * 128:(4 * kg + j + 1) * 128], idf[0:L, 0:L])) for j in range(4)], [xi, idf], ps)
                    c.op('act', lambda e: e.copy(sg[:, 4 * kg:4 * kg + 4, 0:L], ps[:, :].rearrange("p (j t) -> p j t", j=4)[:, :, 0:L]), [ps], [sg])
                c.dma('sp', Hv[:, :, t0:t0 + L], sg[:, :, 0:L], reads=[sg], writes=[H])
        c.barrier()

        RB = c.sb([128, 128], F32, 'RB')
        c.dma('sp', RB[:, :], I['rel_bias'].t.rearrange("a b -> (a b)").partition_broadcast(128), reads=[I['rel_bias']], writes=[RB])
        EB0 = c.sb([128, 4, 128], BF16, 'EB0'); EB1 = c.sb([128, 4, 128], BF16, 'EB1')
        EBM0 = c.sb([16, 4, 128], BF16, 'EBM0'); EBMM = c.sb([16, 4, 16], BF16, 'EBMM')
        with ExitStack() as s:
            rbs = c.sb([32, 4], F32, es=s); ohs = c.sb([32, 400], F32, es=s); bds = c.sb([4, 400], F32, es=s)
            c.dma('sp', rbs[:, :], I['rel_bias'].t[:, :], reads=[I['rel_bias']], writes=[rbs])
            c.dma('sp', ohs[:, :], I['t5oh'].t[:, :], reads=[I['t5oh']], writes=[ohs])
            ps = psf()
            c.mm([(ps[0:4, 0:400], rbs[:, :], ohs[:, :])], [rbs, ohs], ps)
            c.op('dve', lambda e: e.tensor_copy(bds[:, :], ps[0:4, 0:400]), [ps], [bds])
            c.dma('sp', BD.t[:, :], bds[:, :], reads=[bds], writes=[BD])
            hk = [c.sb([128, 128], F32, es=s) for _ in range(2)]
            ef = [c.sb([128, 128], F32, es=s) for _ in range(2)]
            i_ = 0
            for h in range(4):
                for (dst, base, P, N, causal) in [(EB0, 1, 128, 128, True), (EB1, 129, 128, 128, False), (EBM0, 129, 16, 128, False), (EBMM, 113, 16, 16, True)]:
                    hb = hk[i_ % 2]; eb = ef[i_ % 2]; i_ += 1
                    c.dma('sp', hb[0:P, 0:N], bass.AP(BD.t.tensor, h * 400 + base, [[1, P], [1, N]]), reads=[BD], writes=[hb])
                    ps = psf()
                    Jm = J128 if P == 128 else J16
                    c.mm([(ps[0:P, 0:N], Jm[0:P, 0:P], hb[0:P, 0:N])], [Jm, hb], ps)
                    c.op('act', lambda e: e.activation(eb[0:P, 0:N], ps[0:P, 0:N], AF.Exp), [ps], [eb])
                    if causal:
                        c.op('dve', lambda e: e.tensor_tensor(dst[0:P, h, 0:N], eb[0:P, 0:N], CM[0:P, 0:N], ALU.mult), [eb, CM], [dst])
                    else:
                        c.op('dve', lambda e: e.tensor_copy(dst[0:P, h, 0:N], eb[0:P, 0:N]), [eb], [dst])
        c.barrier()

        for l in range(nlayers):
            lambda_init = 0.8 - 0.6 * math.exp(-0.3 * l)
            Win = I['w_in'].t[l, :, :]
            with ExitStack() as s:
                hnT = c.sb([128, KC, T], BF16, es=s)
                if l == 0:
                    with ExitStack() as sn:
                        nsc = NormScr(sn)
                        for sp_ in range(2):
                            for (o, n) in SUBS:
                                norm_sub(nsc, Hv, H, sp_ * SUP + o, n, 0, l, lambda k, o=o, n=n, sp_=sp_: hnT[:, k, sp_ * SUP + o:sp_ * SUP + o + n], hnT)
                        for sp_ in range(2):
                            c.dma('sp', HNv[:, :, sp_ * SUP:(sp_ + 1) * SUP], hnT[:, :, sp_ * SUP:(sp_ + 1) * SUP], reads=[hnT], writes=[HN])
                    c.barrier()
                else:
                    for sp_ in range(2):
                        c.dma('sp', hnT[:, :, sp_ * SUP:(sp_ + 1) * SUP], HNv[:, :, sp_ * SUP:(sp_ + 1) * SUP], reads=[HN], writes=[hnT])
                gs = GemmScr(s, KC, 256)
                zo = [c.sb([128, 344], F32, es=s) for _ in range(3)]
                zi = [0]
                def epi(ps, col, si, t0, n):
                    ob = zo[zi[0] % 3]; zi[0] += 1
                    c.op('act', lambda e: e.copy(ob[:, 0:n], ps[:, 0:n]), [ps], [ob])
                    c.dma('sp', Z.t[col:col + 128, t0:t0 + n], ob[:, 0:n], reads=[ob], writes=[Z])
                gemm(gs, Win, I['w_in'], [(cb * 256, 256) for cb in range(20)], lambda k, t0, n: hnT[:, k, t0:t0 + n], [hnT], [(sp_ * SUP + o, n) for sp_ in range(2) for (o, n) in SUBS], epi)
            c.barrier()
            if stop_after == 'p1':
                break

            mixer_lru(c, g, I, l, Z, YS, psf)
            c.barrier()
            mixer_pool(c, g, I, l, Z, YS, psf, IC)
            c.barrier()
            mixer_hgrn(c, g, I, l, Z, YS, psf, psb, MK, HM, idb, ones_b, epsb)
            c.barrier()
            mixer_attn(c, g, I, l, Z, YS, psf, psb, idb, RB, EB0, EB1, EBM0, EBMM, epsb, lambda_init)
            c.barrier()
            if stop_after == 'p2':
                break

            for sp_ in range(2):
                s0 = sp_ * SUP
                with ExitStack() as s3:
                    mergedT = c.sb([128, KC, SUP], BF16, es=s3)
                    with ExitStack() as s:
                        hnT = c.sb([128, KC, SUP], BF16, es=s)
                        ysT = c.sb([128, KC, SUP], BF16, es=s)
                        c.dma('sp', hnT[:, :, :], HNv[:, :, s0:s0 + SUP], reads=[HN], writes=[hnT])
                        c.dma('sp', ysT[:, :, :], YSv[:, :, s0:s0 + SUP], reads=[YS], writes=[ysT])
                        gst = [c.sb([128, KC, 128], F32, es=s) for _ in range(3)]
                        wg = [c.sb([128, 4, KC, 128], BF16, es=s) for _ in range(2)]
                        pst = [c.sb([128, 4, 128], F32, es=s) for _ in range(3)]
                        wp = [c.sb([128, 4, 4, 128], BF16, es=s) for _ in range(2)]
                        sgb = [c.sb([128, 344], F32, es=s) for _ in range(3)]
                        tmpb = [c.sb([128, 344], F32, es=s) for _ in range(3)]
                        acc = [c.sb([128, 344], F32, es=s) for _ in range(3)]
                        ii = 0
                        g.li = 0
                        def load3(db):
                            wgb = wg[db % 2]; wpb = wp[db % 2]
                            for k in range(4):
                                st = gst[g.li % 3]; st2 = pst[g.li % 3]; g.li += 1
                                col = 5120 + k * D + db * 128
                                c.dma('sp', st[:, :, :], Win[:, col:col + 128].rearrange("(k p) n -> p k n", p=128), reads=[I['w_in']], writes=[st])
                                c.op('dve', lambda e: e.tensor_copy(wgb[:, k, :, :], st[:, :, :]), [st], [wgb])
                                c.dma('sp', st2[:, :, :], I['w_branch'].t[l, k, :, db * 128:(db + 1) * 128].rearrange("(k p) n -> p k n", p=128), reads=[I['w_branch']], writes=[st2])
                                c.op('dve', lambda e: e.tensor_copy(wpb[:, k, :, :], st2[:, :, :]), [st2], [wpb])
                        load3(0)
                        for db in range(16):
                            wgb = wg[db % 2]; wpb = wp[db % 2]
                            if db + 1 < 16:
                                load3(db + 1)
                            for si, (o, n) in enumerate(SUBS):
                                ac = acc[si]
                                for k in range(4):
                                    sg = sgb[ii % 3]; tm = tmpb[ii % 3]; ii += 1
                                    ps = psf(0, 4)
                                    c.mm([(ps[:, 0:n], wgb[:, k, kk, :], hnT[:, kk, o:o + n]) for kk in range(KC)], [wgb, hnT], ps)
                                    c.op('act', lambda e: e.activation(sg[:, 0:n], ps[:, 0:n], AF.Sigmoid), [ps], [sg])
                                    ps2 = psf(0, 4)
                                    c.mm([(ps2[:, 0:n], wpb[:, k, kk, :], ysT[:, 4 * k + kk, o:o + n]) for kk in range(4)], [wpb, ysT], ps2)
                                    if k == 0:
                                        c.op('dve', lambda e: e.tensor_tensor(ac[:, 0:n], sg[:, 0:n], ps2[:, 0:n], ALU.mult), [sg, ps2], [ac])
                                    else:
                                        c.op('dve', lambda e: e.tensor_tensor(tm[:, 0:n], sg[:, 0:n], ps2[:, 0:n], ALU.mult), [sg, ps2], [tm])
                                        if k < 3:
                                            c.op('dve', lambda e: e.tensor_tensor(ac[:, 0:n], ac[:, 0:n], tm[:, 0:n], ALU.add), [ac, tm], [ac])
                                        else:
                                            c.op('dve', lambda e: e.tensor_tensor(mergedT[:, db, o:o + n], ac[:, 0:n], tm[:, 0:n], ALU.add), [ac, tm], [mergedT])
                    c.barrier()
                    with ExitStack() as s:
                        gs = GemmScr(s, KC, 256)
                        zo = [c.sb([128, 344], F32, es=s) for _ in range(3)]
                        zi = [0]
                        def epi(ps, col, si, t0, n):
                            ob = zo[zi[0] % 3]; zi[0] += 1
                            c.op('act', lambda e: e.copy(ob[:, 0:n], ps[:, 0:n]), [ps], [ob])
                            c.dma('sp', MIX.t[col:col + 128, s0 + t0:s0 + t0 + n], ob[:, 0:n], reads=[ob], writes=[MIX])
                        gemm(gs, I['w_out'].t[l, :, :], I['w_out'], [(cb * 256, 256) for cb in range(8)], lambda k, t0, n: mergedT[:, k, t0:t0 + n], [mergedT], SUBS, epi)
                    c.barrier()
                with ExitStack() as s3:
                    hnT = c.sb([128, KC, SUP], BF16, es=s3)
                    with ExitStack() as sn:
                        nsc = NormScr(sn, resid=True)
                        for (o, n) in SUBS:
                            norm_sub(nsc, MIXv, MIX, s0 + o, n, 1, l, None, None, resid=True,
                                     second=(2, l, lambda k, o=o, n=n: hnT[:, k, o:o + n], hnT, None))
                    c.barrier()
                    hid = c.sb([128, 44, SUP], BF16, es=s3)
                    with ExitStack() as s:
                        Wgu = I['ffn_w_gu'].t[l, :, :]
                        st = [c.sb([128, KC, 2, 128], F32, es=s) for _ in range(2)]
                        wb = [c.sb([128, KC, 2, 128], BF16, es=s) for _ in range(2)]
                        sgb = [c.sb([128, 344], F32, es=s) for _ in range(3)]
                        ii = 0
                        def loadgu(j):
                            stj = st[j % 2]; wbj = wb[j % 2]
                            c.dma('sp', stj[:, :, 0, :], Wgu[:, j * 128:(j + 1) * 128].rearrange("(k p) n -> p k n", p=128), reads=[I['ffn_w_gu']], writes=[stj])
                            c.dma('sp', stj[:, :, 1, :], Wgu[:, FH + j * 128:FH + (j + 1) * 128].rearrange("(k p) n -> p k n", p=128), reads=[I['ffn_w_gu']], writes=[stj])
                            c.op('dve', lambda e: e.tensor_copy(wbj[:, :, :, :], stj[:, :, :, :]), [stj], [wbj])
                        loadgu(0)
                        for j in range(44):
                            wbj = wb[j % 2]
                            if j + 1 < 44:
                                loadgu(j + 1)
                            for si, (o, n) in enumerate(SUBS):
                                sg = sgb[ii % 3]; ii += 1
                                ps = psf(0, 4)
                                c.mm([(ps[:, 0:n], wbj[:, kk, 0, :], hnT[:, kk, o:o + n]) for kk in range(KC)], [wbj, hnT], ps)
                                c.op('act', lambda e: e.activation(sg[:, 0:n], ps[:, 0:n], AF.Silu), [ps], [sg])
                                ps2 = psf(0, 4)
                                c.mm([(ps2[:, 0:n], wbj[:, kk, 1, :], hnT[:, kk, o:o + n]) for kk in range(KC)], [wbj, hnT], ps2)
                                c.op('dve', lambda e: e.tensor_tensor(hid[:, j, o:o + n], sg[:, 0:n], ps2[:, 0:n], ALU.mult), [sg, ps2], [hid])
                    c.barrier()
                    with ExitStack() as s:
                        gs = GemmScr(s, 44, 128)
                        zo = [c.sb([128, 344], F32, es=s) for _ in range(3)]
                        zi = [0]
                        def epi(ps, col, si, t0, n):
                            ob = zo[zi[0] % 3]; zi[0] += 1
                            c.op('act', lambda e: e.copy(ob[:, 0:n], ps[:, 0:n]), [ps], [ob])
                            c.dma('sp', MIX.t[col:col + 128, s0 + t0:s0 + t0 + n], ob[:, 0:n], reads=[ob], writes=[MIX])
                        gemm(gs, I['ffn_w_down'].t[l, :, :], I['ffn_w_down'], [(cb * 128, 128) for cb in range(16)], lambda k, t0, n: hid[:, k, t0:t0 + n], [hid], SUBS, epi)
                    c.barrier()
                with ExitStack() as s:
                    nsc = NormScr(s, resid=True)
                    if l + 1 < nlayers:
                        hb = [c.sb([128, KC, 344], BF16, es=s) for _ in range(2)]
                    for si_, (o, n) in enumerate(SUBS):
                        if l + 1 < nlayers:
                            hbt = hb[si_ % 2]
                            def after(hbt=hbt, o=o, n=n):
                                c.dma('sp', HNv[:, :, s0 + o:s0 + o + n], hbt[:, :, 0:n], reads=[hbt], writes=[HN])
                            norm_sub(nsc, MIXv, MIX, s0 + o, n, 3, l, None, None, resid=True,
                                     second=(0, l + 1, lambda k, hbt=hbt, n=n: hbt[:, k, 0:n], hbt, after))
                        else:
                            norm_sub(nsc, MIXv, MIX, s0 + o, n, 3, l, None, None, resid=True)
                c.barrier()

        if stop_after is None:
            with ExitStack() as s:
                hin = [c.sb([128, KC, 128], F32, es=s) for _ in range(2)]
                ostg = [c.sb([128, D], F32, es=s) for _ in range(2)]
                for j in range(16):
                    t0 = 16 + 128 * j
                    hi_ = hin[j % 2]; og = ostg[j % 2]
                    c.dma('sp', hi_[:, :, :], Hv[:, :, t0:t0 + 128], reads=[H], writes=[hi_])
                    for kg in range(4):
                        ps = psf(0, 4)
                        c.mm_raw([(lambda e, jj=jj: e.transpose(ps[:, jj * 128:(jj + 1) * 128], hi_[:, 4 * kg + jj, :], idf[:, :])) for jj in range(4)], [hi_, idf], ps)
                        c.op('act', lambda e: e.copy(og[:, kg * 512:(kg + 1) * 512], ps[:, :]), [ps], [og])
                    c.dma('sp', OUT.t[128 * j:128 * (j + 1), :], og[:, :], reads=[og], writes=[OUT])
        else:
            zt = c.sb([128, D], F32, 'zt')
            c.op('dve', lambda e: e.memset(zt[:, :], 0.0), [], [zt])
            c.dma('sp', OUT.t[0:128, :], zt[:, :], reads=[zt], writes=[OUT])
        c.barrier()
        c.final_wait()
    return nc


def kernel(**inputs):
    nc = build_program()
    oh = t5_onehot()
    active = {0: 0, 1: 1, 4: 2, 5: 3}
    in_maps = []
    zeros = {}
    for core in range(8):
        m = {}
        for k, v in inputs.items():
            v = np.asarray(v)
            if core in active:
                m[k] = np.ascontiguousarray(v[active[core]]) if k == 'x' else np.ascontiguousarray(v)
            else:
                if k not in zeros:
                    zeros[k] = np.zeros(v.shape[1:] if k == 'x' else v.shape, v.dtype)
                m[k] = zeros[k]
        m['t5oh'] = oh
        in_maps.append(m)
    res = run_bass_kernel_spmd(nc, in_maps, core_ids=list(range(8)))
    cores = sorted(active, key=lambda c_: active[c_])
    return np.stack([np.asarray(res.results[c_]['out']) for c_ in cores], axis=0).astype(np.float32)
```

```python
import numpy as np, math
from contextlib import ExitStack
import concourse.bass as bass
import concourse.mybir as mybir
from concourse.bass_utils import run_bass_kernel_spmd

F32 = mybir.dt.float32
BF16 = mybir.dt.bfloat16
AF = mybir.ActivationFunctionType
ALU = mybir.AluOpType
AX = mybir.AxisListType


class Buf:
    def __init__(self, t, name):
        self.t = t
        self.name = name
        self.w = None
        self.r = []

    def __getitem__(self, idx):
        return self.t[idx]


class BufView(Buf):
    def __init__(self, parent, t):
        self.parent = parent
        self.t = t
        self.name = parent.name + '_v'

    w = property(lambda s: s.parent.w, lambda s, v: setattr(s.parent, 'w', v))
    r = property(lambda s: s.parent.r, lambda s, v: setattr(s.parent, 'r', v))


def run_streams(outer, inner, k):
    for tok in outer:
        if tok == 'ALLOC_DONE':
            break
    outer_done = False
    inner_done = False
    while not (outer_done and inner_done):
        if not outer_done:
            for _ in range(k):
                tok = next(outer, 'STOP')
                if tok == 'END' or tok == 'STOP':
                    outer_done = True
                    break
        if not inner_done:
            if next(inner, 'STOP') == 'STOP':
                inner_done = True
    for _ in outer:
        pass


class Ctx:
    NDS = 24

    def __init__(self, nc, es):
        self.nc, self.es = nc, es
        self.engs = {'pe': nc.tensor, 'act': nc.scalar, 'dve': nc.vector,
                     'pool': nc.gpsimd, 'sp': nc.sync}
        self.sem = {k: es.enter_context(nc.semaphore('s_' + k)) for k in ['pe', 'act', 'dve', 'pool']}
        self.cnt = {k: 0 for k in self.sem}
        self.dsem = [es.enter_context(nc.semaphore('d%d' % i)) for i in range(self.NDS)]
        self.dcnt = [0] * self.NDS
        self.dnext = 0
        self.seen = {e: {} for e in self.engs}
        self.nbuf = 0

    def sb(self, shape, dt, name=None, es=None):
        self.nbuf += 1
        name = name or 'sb%d' % self.nbuf
        t = (es or self.es).enter_context(self.nc.sbuf_tensor(name, list(shape), dt))
        return Buf(t, name)

    def ps(self, shape, dt, name=None, es=None):
        self.nbuf += 1
        name = name or 'ps%d' % self.nbuf
        t = (es or self.es).enter_context(self.nc.psum_tensor(name, list(shape), dt))
        return Buf(t, name)

    def dram(self, name, shape, dt, kind='Internal'):
        t = self.nc.dram_tensor(name, list(shape), dt, kind=kind)
        return Buf(t.ap(), name)

    def _need(self, e, tk, raw):
        if tk is None:
            return
        key, val = tk
        if key == e and e == 'pe':
            return
        if self.seen[e].get(key, 0) >= val:
            return
        if isinstance(key, str):
            self.engs[e].wait_ge(self.sem[key], val)
        else:
            self.engs[e].wait_ge(self.dsem[key[1]], 16 * val)
        self.seen[e][key] = val

    def _deps(self, e, reads, writes):
        for b in reads:
            self._need(e, b.w, True)
        for b in writes:
            self._need(e, b.w, False)
            for tk in b.r:
                self._need(e, tk, False)

    def _commit(self, tk, reads, writes):
        for b in reads:
            b.r.append(tk)
            if len(b.r) > 12:
                d = {}
                for k, v in b.r:
                    d[k] = max(d.get(k, 0), v)
                b.r = list(d.items())
        for b in writes:
            b.w = tk
            b.r = []

    def op(self, e, fn, reads=(), writes=()):
        self._deps(e, reads, writes)
        ins = fn(self.engs[e])
        self.cnt[e] += 1
        ins.then_inc(self.sem[e], 1)
        self._commit((e, self.cnt[e]), reads, writes)
        return ins

    def mm(self, items, reads, out, start=True, stop=True):
        e = 'pe'
        self._deps(e, reads, [out])
        n = len(items)
        ins = None
        for i, (o, l, r) in enumerate(items):
            ins = self.nc.tensor.matmul(o, l, r, start=(start and i == 0), stop=(stop and i == n - 1))
        self.cnt[e] += 1
        ins.then_inc(self.sem[e], 1)
        self._commit((e, self.cnt[e]), reads, [out])

    def mm_raw(self, fns, reads, out):
        e = 'pe'
        self._deps(e, reads, [out])
        ins = None
        for f in fns:
            ins = f(self.nc.tensor)
        self.cnt[e] += 1
        ins.then_inc(self.sem[e], 1)
        self._commit((e, self.cnt[e]), reads, [out])

    def dma(self, q, out_ap, in_ap, reads=(), writes=(), **kw):
        i = self.dnext
        self.dnext = (self.dnext + 1) % self.NDS
        if self.dcnt[i] > 0:
            self._need(q, (('d', i), self.dcnt[i]), True)
        self._deps(q, reads, writes)
        ins = self.engs[q].dma_start(out=out_ap, in_=in_ap, **kw)
        ins.then_inc(self.dsem[i], 16)
        self.dcnt[i] += 1
        self._commit((('d', i), self.dcnt[i]), reads, writes)

    def barrier(self):
        tks = [(k, self.cnt[k]) for k in self.cnt if self.cnt[k] > 0]
        tks += [(('d', i), self.dcnt[i]) for i in range(self.NDS) if self.dcnt[i] > 0]
        for e in self.engs:
            for tk in tks:
                if tk[0] == e:
                    continue
                self._need(e, tk, True)

    def final_wait(self):
        tks = [(k, self.cnt[k]) for k in self.cnt if self.cnt[k] > 0]
        tks += [(('d', i), self.dcnt[i]) for i in range(self.NDS) if self.dcnt[i] > 0]
        for tk in tks:
            self._need('sp', tk, True)

def _vec(c, dst_ap, dstb, srcb, src_ap):
    c.dma('sp', dst_ap, src_ap, reads=[srcb], writes=[dstb], allow_slow_non_contiguous=True)


def mixer_lru(c, g, I, l, Z, YS, psf):
    with ExitStack() as s:
        CW = c.sb([128, 4, 4], F32, es=s); CB = c.sb([128, 4], F32, es=s)
        BA = c.sb([128, 4], F32, es=s); BX = c.sb([128, 4], F32, es=s); LM = c.sb([128, 4], F32, es=s)
        CN = c.sb([128, 4], F32, es=s)
        for j_ in range(4):
            _vec(c, CW[:, :, j_], CW, I['lru_conv_w'], I['lru_conv_w'].t[l, j_, :].rearrange("(h p) -> p h", p=128))
            yield
        _vec(c, CB[:, :], CB, I['lru_conv_b'], I['lru_conv_b'].t[l, :].rearrange("(h p) -> p h", p=128))
        yield
        _vec(c, BA[:, :], BA, I['lru_b_a'], I['lru_b_a'].t[l, :].rearrange("(h p) -> p h", p=128))
        yield
        _vec(c, BX[:, :], BX, I['lru_b_x'], I['lru_b_x'].t[l, :].rearrange("(h p) -> p h", p=128))
        yield
        _vec(c, LM[:, :], LM, I['lru_lambda'], I['lru_lambda'].t[l, :].rearrange("(h p) -> p h", p=128))
        yield
        c.op('act', lambda e: e.activation(CN[:, :], LM[:, :], AF.Exp, scale=-1.0), [LM], [CN])
        yield
        c.op('dve', lambda e: e.tensor_scalar_add(CN[:, :], CN[:, :], 1.0), [CN], [CN])
        yield
        c.op('act', lambda e: e.activation(CN[:, :], CN[:, :], AF.Ln), [CN], [CN])
        yield
        c.op('dve', lambda e: e.tensor_scalar_mul(CN[:, :], CN[:, :], -8.0), [CN], [CN])
        yield
        Wst = c.sb([128, 2, 4, 128], F32, es=s); Wb = c.sb([128, 2, 4, 128], BF16, es=s)
        c.dma('sp', Wst[:, 0, :, :], I['lru_w_a'].t[l, :, :, :].rearrange("h i j -> i h j"), reads=[I['lru_w_a']], writes=[Wst])
        yield
        c.dma('sp', Wst[:, 1, :, :], I['lru_w_x'].t[l, :, :, :].rearrange("h i j -> i h j"), reads=[I['lru_w_x']], writes=[Wst])
        yield
        c.op('act', lambda e: e.copy(Wb[:, :, :, :], Wst[:, :, :, :]), [Wst], [Wb])
        yield
        U = c.sb([128, T], F32, es=s); Gt = c.sb([128, T], F32, es=s); XC = c.sb([128, T], F32, es=s)
        R = c.sb([128, T], F32, es=s); IG = c.sb([128, T], F32, es=s); A = c.sb([128, T], F32, es=s)
        TH = c.sb([128, T], F32, es=s); P1 = c.sb([128, T], F32, es=s); HS = c.sb([128, T], F32, es=s)
        XCb = c.sb([128, T], BF16, es=s); Yb = c.sb([128, T], BF16, es=s)
        for h in range(4):
            c.dma('sp', U[:, :], Z.t[h * 128:(h + 1) * 128, :], reads=[Z], writes=[U])
            yield
            c.dma('sp', Gt[:, :], Z.t[512 + h * 128:512 + (h + 1) * 128, :], reads=[Z], writes=[Gt])
            yield
            c.op('dve', lambda e: e.tensor_scalar(XC[:, :], U[:, :], CW[:, h, 3:4], CB[:, h:h + 1], ALU.mult, ALU.add), [U, CW, CB], [XC])
            yield
            for j in range(3):
                sh = 3 - j
                c.op('dve', lambda e: e.scalar_tensor_tensor(XC[:, sh:T], U[:, 0:T - sh], CW[:, h, j:j + 1], XC[:, sh:T], ALU.mult, ALU.add), [U, CW, XC], [XC])
                yield
            c.op('act', lambda e: e.copy(XCb[:, :], XC[:, :]), [XC], [XCb])
            yield
            for (t0, n) in CH512:
                ps = psf()
                c.mm([(ps[:, 0:n], Wb[:, 0, h, :], XCb[:, t0:t0 + n])], [Wb, XCb], ps)
                yield
                c.op('act', lambda e: e.activation(R[:, t0:t0 + n], ps[:, 0:n], AF.Sigmoid, bias=BA[:, h:h + 1]), [ps, BA], [R])
                yield
                ps2 = psf()
                c.mm([(ps2[:, 0:n], Wb[:, 1, h, :], XCb[:, t0:t0 + n])], [Wb, XCb], ps2)
                yield
                c.op('act', lambda e: e.activation(IG[:, t0:t0 + n], ps2[:, 0:n], AF.Sigmoid, bias=BX[:, h:h + 1]), [ps2, BX], [IG])
                yield
            c.op('act', lambda e: e.activation(A[:, :], R[:, :], AF.Exp, scale=CN[:, h:h + 1]), [R, CN], [A])
            yield
            c.op('act', lambda e: e.activation(TH[:, :], R[:, :], AF.Tanh, scale=CN[:, h:h + 1]), [R, CN], [TH])
            yield
            c.op('dve', lambda e: e.tensor_scalar(P1[:, :], TH[:, :], -1.0, 1.0, ALU.mult, ALU.add), [TH], [P1])
            yield
            c.op('dve', lambda e: e.reciprocal(P1[:, :], P1[:, :]), [P1], [P1])
            yield
            c.op('dve', lambda e: e.scalar_tensor_tensor(P1[:, :], TH[:, :], -2.0, P1[:, :], ALU.mult, ALU.mult), [TH, P1], [P1])
            yield
            c.op('act', lambda e: e.activation(P1[:, :], P1[:, :], AF.Sqrt), [P1], [P1])
            yield
            c.op('dve', lambda e: e.tensor_tensor(IG[:, :], IG[:, :], XC[:, :], ALU.mult), [IG, XC], [IG])
            yield
            c.op('dve', lambda e: e.tensor_tensor(P1[:, :], P1[:, :], IG[:, :], ALU.mult), [P1, IG], [P1])
            yield
            c.op('dve', lambda e: e.tensor_tensor_scan(HS[:, :], A[:, :], P1[:, :], 0.0, ALU.mult, ALU.add), [A, P1], [HS])
            yield
            c.op('act', lambda e: e.activation(Gt[:, :], Gt[:, :], AF.Gelu_apprx_tanh), [Gt], [Gt])
            yield
            c.op('dve', lambda e: e.tensor_tensor(Yb[:, :], HS[:, :], Gt[:, :], ALU.mult), [HS, Gt], [Yb])
            yield
            c.dma('sp', YS.t[h * 128:(h + 1) * 128, :], Yb[:, :], reads=[Yb], writes=[YS])
            yield
        yield 'END'


def mixer_pool(c, g, I, l, Z, YS, psf, IC):
    with ExitStack() as s:
        PSc = c.sb([128, 4], F32, es=s)
        _vec(c, PSc[:, :], PSc, I['pool_scale'], I['pool_scale'].t[l, :].rearrange("(h p) -> p h", p=128))
        yield
        Wst = c.sb([128, 4, 128], F32, es=s); Wb = c.sb([128, 4, 128], BF16, es=s)
        c.dma('sp', Wst[:, :, :], I['pool_w'].t[l, :, :, :].rearrange("h i j -> i h j"), reads=[I['pool_w']], writes=[Wst])
        yield
        c.op('act', lambda e: e.copy(Wb[:, :, :], Wst[:, :, :]), [Wst], [Wb])
        yield
        TP = T + 16
        Up = c.sb([128, TP], F32, es=s); Pa = c.sb([128, TP], F32, es=s); Pb = c.sb([128, TP], F32, es=s)
        PL = c.sb([128, T], F32, es=s); PLb = c.sb([128, T], BF16, es=s); Yb = c.sb([128, T], BF16, es=s)
        t16 = c.sb([128, 16], F32, es=s)
        for b_ in (Up, Pa, Pb):
            c.op('dve', lambda e: e.memset(b_[:, 0:16], 0.0), [], [b_])
            yield
        for gi in range(4):
            win = 2 ** (gi + 1)
            c.dma('sp', Up[:, 16:TP], Z.t[1024 + gi * 128:1024 + (gi + 1) * 128, :], reads=[Z], writes=[Up])
            yield
            src = Up
            dsts = [Pa, Pb]
            for si, step in enumerate([1, 2, 4, 8][:gi + 1]):
                dst = dsts[si % 2]
                c.op('dve', lambda e: e.tensor_tensor(dst[:, 16:TP], src[:, 16:TP], src[:, 16 - step:TP - step], ALU.add), [src], [dst])
                yield
                src = dst
            c.op('dve', lambda e: e.scalar_tensor_tensor(PL[:, :], src[:, 16:TP], 1.0 / win, Up[:, 16:TP], ALU.mult, ALU.subtract), [src, Up], [PL])
            yield
            c.op('dve', lambda e: e.tensor_tensor(t16[:, :], src[:, 16:32], IC[:, gi, :], ALU.mult), [src, IC], [t16])
            yield
            c.op('dve', lambda e: e.tensor_tensor(PL[:, 0:16], t16[:, :], Up[:, 16:32], ALU.subtract), [t16, Up], [PL])
            yield
            c.op('act', lambda e: e.copy(PLb[:, :], PL[:, :]), [PL], [PLb])
            yield
            for (t0, n) in CH512:
                ps = psf()
                c.mm([(ps[:, 0:n], Wb[:, gi, :], PLb[:, t0:t0 + n])], [Wb, PLb], ps)
                yield
                c.op('dve', lambda e: e.tensor_scalar_mul(Yb[:, t0:t0 + n], ps[:, 0:n], PSc[:, gi:gi + 1]), [ps, PSc], [Yb])
                yield
            c.dma('sp', YS.t[512 + gi * 128:512 + (gi + 1) * 128, :], Yb[:, :], reads=[Yb], writes=[YS])
            yield
        yield 'END'


def mixer_hgrn(c, g, I, l, Z, YS, psf, psb, MK, HM, idb, ones_b, epsb):
    with ExitStack() as s:
        LBS = c.sb([128, 2, 4], F32, es=s)
        for l_ in range(2):
            _vec(c, LBS[:, l_, :], LBS, I['hgrn_lower_bounds'], I['hgrn_lower_bounds'].t[l_, :].rearrange("(h p) -> p h", p=128))
            yield
        LB = c.sb([128, 4], F32, es=s); OML = c.sb([128, 4], F32, es=s); NOML = c.sb([128, 4], F32, es=s)
        if l == 0:
            c.op('dve', lambda e: e.memset(LB[:, :], 0.0), [], [LB])
            yield
        else:
            SM = c.sb([128, 4], F32, es=s)
            c.op('act', lambda e: e.activation(LBS[:, :, :], LBS[:, :, :], AF.Exp), [LBS], [LBS])
            yield
            c.op('dve', lambda e: e.tensor_tensor(SM[:, :], LBS[:, 0, :], LBS[:, 1, :], ALU.add), [LBS], [SM])
            yield
            c.op('dve', lambda e: e.reciprocal(SM[:, :], SM[:, :]), [SM], [SM])
            yield
            c.op('dve', lambda e: e.tensor_tensor(LB[:, :], LBS[:, 1, :], SM[:, :], ALU.mult), [LBS, SM], [LB])
            yield
        c.op('dve', lambda e: e.tensor_scalar(OML[:, :], LB[:, :], -1.0, 1.0, ALU.mult, ALU.add), [LB], [OML])
        yield
        c.op('dve', lambda e: e.tensor_scalar_add(NOML[:, :], LB[:, :], -1.0), [LB], [NOML])
        yield
        HNW = c.sb([128, 1], F32, es=s)
        MK = c.sb([128, T], F32, es=s)
        c.op('dve', lambda e: e.memset(MK[:, :], 1.0), [], [MK])
        yield
        c.op('dve', lambda e: e.memset(MK[:, 0:1], 0.0), [], [MK])
        yield
        c.op('dve', lambda e: e.memset(MK[:, 16:T:64], 0.0), [], [MK])
        yield
        _vec(c, HNW[:, :], HNW, I['hgrn_norm'], I['hgrn_norm'].t[l, :].rearrange("(p o) -> p o", o=1))
        yield
        def ft():
            return c.sb([128, T], F32, es=s)
        Qf, Fz, Vf, OG, SIG, Gg, Kk, Bc, TMP, EK, Oo = [ft() for _ in range(11)]
        QT, KT, KH, VB, Yb, SQ = [c.sb([128, T], BF16, es=s) for _ in range(6)]
        Dd = c.sb([128, 33], F32, es=s)
        S = c.sb([128, 128], F32, es=s)
        Sbf = c.sb([128, 33, 128], BF16, es=s)
        ktok = [c.sb([128, 128], BF16, es=s) for _ in range(2)]
        vtok = [c.sb([128, 128], BF16, es=s) for _ in range(2)]
        Asb = [c.sb([128, 128], BF16, es=s) for _ in range(2)]
        for h in range(4):
            c.dma('sp', Qf[:, :], Z.t[1536 + h * 128:1536 + (h + 1) * 128, :], reads=[Z], writes=[Qf])
            yield
            c.dma('sp', Fz[:, :], Z.t[2048 + h * 128:2048 + (h + 1) * 128, :], reads=[Z], writes=[Fz])
            yield
            c.dma('sp', Vf[:, :], Z.t[2560 + h * 128:2560 + (h + 1) * 128, :], reads=[Z], writes=[Vf])
            yield
            c.dma('sp', OG[:, :], Z.t[3072 + h * 128:3072 + (h + 1) * 128, :], reads=[Z], writes=[OG])
            yield
            c.op('act', lambda e: e.activation(SIG[:, :], Fz[:, :], AF.Sigmoid), [Fz], [SIG])
            yield
            c.op('dve', lambda e: e.tensor_scalar(Gg[:, :], SIG[:, :], OML[:, h:h + 1], LB[:, h:h + 1], ALU.mult, ALU.add), [SIG, OML, LB], [Gg])
            yield
            c.op('act', lambda e: e.activation(Gg[:, :], Gg[:, :], AF.Ln), [Gg], [Gg])
            yield
            c.op('dve', lambda e: e.tensor_scalar(Kk[:, :], SIG[:, :], NOML[:, h:h + 1], OML[:, h:h + 1], ALU.mult, ALU.add), [SIG, OML, NOML], [Kk])
            yield
            c.op('dve', lambda e: e.tensor_tensor_scan(Bc[:, :], MK[:, :], Gg[:, :], 0.0, ALU.mult, ALU.add), [MK, Gg], [Bc])
            yield
            c.op('act', lambda e: e.activation(TMP[:, :], Bc[:, :], AF.Exp), [Bc], [TMP])
            yield
            c.op('dve', lambda e: e.tensor_tensor(QT[:, :], Qf[:, :], TMP[:, :], ALU.mult), [Qf, TMP], [QT])
            yield
            c.op('dve', lambda e: e.tensor_scalar(TMP[:, :], Bc[:, :], -1.0, 80.0, ALU.mult, ALU.min), [Bc], [TMP])
            yield
            c.op('act', lambda e: e.activation(TMP[:, :], TMP[:, :], AF.Exp), [TMP], [TMP])
            yield
            c.op('dve', lambda e: e.tensor_tensor(KT[:, :], Kk[:, :], TMP[:, :], ALU.mult), [Kk, TMP], [KT])
            yield
            c.op('act', lambda e: e.activation(Dd[:, 0:1], Bc[:, 15:16], AF.Exp), [Bc], [Dd])
            yield
            c.op('act', lambda e: e.activation(Dd[:, 1:33], Bc[:, 79:T:64], AF.Exp), [Bc], [Dd])
            yield
            chunks = [(0, 16)] + [(16 + 64 * n, 64) for n in range(32)]
            for (c0, cl) in chunks:
                c.op('act', lambda e: e.activation(EK[:, c0:c0 + cl], Bc[:, c0:c0 + cl], AF.Exp, bias=Bc[:, c0 + cl - 1:c0 + cl], scale=-1.0), [Bc], [EK])
                yield
            c.op('dve', lambda e: e.tensor_tensor(KH[:, :], Kk[:, :], EK[:, :], ALU.mult), [Kk, EK], [KH])
            yield
            c.op('act', lambda e: e.copy(VB[:, :], Vf[:, :]), [Vf], [VB])
            yield
            c.op('dve', lambda e: e.memset(S[:, :], 0.0), [], [S])
            yield
            n_ = 0
            def tposes(bi):
                t0, L = BLKS[bi]
                kt = ktok[bi % 2]; vt = vtok[bi % 2]
                pb1 = psb()
                c.mm_raw([lambda e: e.transpose(pb1[0:L, 0:128], KH[:, t0:t0 + L], idb[:, :])], [KH, idb], pb1)
                yield
                c.op('act', lambda e: e.copy(kt[0:L, :], pb1[0:L, 0:128]), [pb1], [kt])
                yield
                pb2 = psb()
                c.mm_raw([lambda e: e.transpose(pb2[0:L, 0:128], VB[:, t0:t0 + L], idb[:, :])], [VB, idb], pb2)
                yield
                c.op('act', lambda e: e.copy(vt[0:L, :], pb2[0:L, 0:128]), [pb2], [vt])
                yield
            yield from tposes(0)
            for bi, (t0, L) in enumerate(BLKS):
                kt = ktok[bi % 2]; vt = vtok[bi % 2]; ab = Asb[bi % 2]
                if bi + 1 < len(BLKS):
                    yield from tposes(bi + 1)
                cks = [(0, 16)] if bi == 0 else [(0, 64), (64, 64)]
                n_first = n_
                for (r0, rl) in cks:
                    ps = psf()
                    c.mm([(ps[:, 0:128], kt[r0:r0 + rl, :], vt[r0:r0 + rl, :])], [kt, vt], ps)
                    yield
                    c.op('act', lambda e: e.copy(Sbf[:, n_, :], S[:, :]), [S], [Sbf])
                    yield
                    c.op('dve', lambda e: e.scalar_tensor_tensor(S[:, :], S[:, :], Dd[:, n_:n_ + 1], ps[:, 0:128], ALU.mult, ALU.add), [S, Dd, ps], [S])
                    yield
                    n_ += 1
                psA = psf()
                c.mm([(psA[0:L, 0:L], KT[:, t0:t0 + L], QT[:, t0:t0 + L])], [KT, QT], psA)
                yield
                c.op('dve', lambda e: e.tensor_tensor(ab[0:L, 0:L], psA[0:L, 0:L], HM[0:L, 0:L], ALU.mult), [psA, HM], [ab])
                yield
                pso = psf()
                items = [(pso[:, 0:L], vt[0:L, :], ab[0:L, 0:L])]
                for ci, (r0, rl) in enumerate(cks):
                    items.append((pso[:, r0:r0 + rl], Sbf[:, n_first + ci, :], QT[:, t0 + r0:t0 + r0 + rl]))
                c.mm(items, [vt, ab, Sbf, QT], pso)
                yield
                c.op('act', lambda e: e.copy(Oo[:, t0:t0 + L], pso[:, 0:L]), [pso], [Oo])
                yield
            c.op('act', lambda e: e.activation(SQ[:, :], Oo[:, :], AF.Square), [Oo], [SQ])
            yield
            for (t0, n) in CH512:
                ps = psf()
                c.mm([(ps[:, 0:n], ones_b[:, :], SQ[:, t0:t0 + n])], [ones_b, SQ], ps)
                yield
                c.op('act', lambda e: e.activation(TMP[:, t0:t0 + n], ps[:, 0:n], AF.Sqrt, bias=epsb[:, 0:1], scale=1.0 / 128), [ps, epsb], [TMP])
                yield
            c.op('dve', lambda e: e.reciprocal(TMP[:, :], TMP[:, :]), [TMP], [TMP])
            yield
            c.op('dve', lambda e: e.scalar_tensor_tensor(Oo[:, :], Oo[:, :], HNW[:, 0:1], TMP[:, :], ALU.mult, ALU.mult), [Oo, HNW, TMP], [Oo])
            yield
            c.op('act', lambda e: e.activation(OG[:, :], OG[:, :], AF.Silu), [OG], [OG])
            yield
            c.op('dve', lambda e: e.tensor_tensor(Yb[:, :], Oo[:, :], OG[:, :], ALU.mult), [Oo, OG], [Yb])
            yield
            c.dma('sp', YS.t[1024 + h * 128:1024 + (h + 1) * 128, :], Yb[:, :], reads=[Yb], writes=[YS])
            yield
        yield 'END'


def mixer_attn(c, g, I, l, Z, YS, psf, psb, idb, RB, EB0, EB1, EBM0, EBMM, epsb, lambda_init):
    SCALE = 0.125
    with ExitStack() as s:
        DL = c.sb([128, 4, 64], F32, es=s)
        c.dma('sp', DL[:, :, :], I['diff_lambda'].t[l, :, :].rearrange("a b -> (a b)").partition_broadcast(128), reads=[I['diff_lambda']], writes=[DL])
        yield
        PR = c.sb([128, 2, 64], F32, es=s); SS = c.sb([128, 2], F32, es=s); LAMN = c.sb([128, 1], F32, es=s)
        c.op('dve', lambda e: e.tensor_tensor(PR[:, 0, :], DL[:, 0, :], DL[:, 1, :], ALU.mult), [DL], [PR])
        yield
        c.op('dve', lambda e: e.tensor_tensor(PR[:, 1, :], DL[:, 2, :], DL[:, 3, :], ALU.mult), [DL], [PR])
        yield
        c.op('dve', lambda e: e.reduce_sum(SS[:, :], PR[:, :, :], AX.X), [PR], [SS])
        yield
        c.op('act', lambda e: e.activation(SS[:, :], SS[:, :], AF.Exp), [SS], [SS])
        yield
        c.op('dve', lambda e: e.tensor_tensor(LAMN[:, :], SS[:, 1:2], SS[:, 0:1], ALU.subtract), [SS], [LAMN])
        yield
        c.op('dve', lambda e: e.tensor_scalar_add(LAMN[:, :], LAMN[:, :], -lambda_init), [LAMN], [LAMN])
        yield
        SUBW = c.sb([128, 128], F32, es=s)
        c.dma('sp', SUBW[:, :], I['diff_subln'].t[l, :].partition_broadcast(128), reads=[I['diff_subln']], writes=[SUBW])
        yield
        c.op('dve', lambda e: e.tensor_scalar_mul(SUBW[:, :], SUBW[:, :], 1.0 - lambda_init), [SUBW], [SUBW])
        yield
        Qf = c.sb([128, T], F32, es=s); Kf = c.sb([128, T], F32, es=s); Vf = c.sb([128, T], F32, es=s)
        qb = c.sb([128, T], BF16, es=s); kb = c.sb([128, T], BF16, es=s); vb = c.sb([128, T], BF16, es=s)
        V1 = c.sb([128, 17, 132], BF16, es=s)
        c.op('dve', lambda e: e.memset(V1[:, :, 128:129], 1.0), [], [V1])
        yield
        YD = c.sb([128, T], BF16, es=s)
        Eb = [c.sb([128, 512], BF16, es=s) for _ in range(3)]
        Om = [c.sb([128, 4, 132], F32, es=s) for _ in range(2)]
        rr = [c.sb([128, 2], F32, es=s) for _ in range(2)]
        ot = [c.sb([128, 128], F32, es=s) for _ in range(2)]
        jk = [c.sb([128, 128], F32, es=s) for _ in range(2)]
        s1 = [c.sb([128, 1], F32, es=s) for _ in range(2)]
        yb = [c.sb([128, 128], BF16, es=s) for _ in range(2)]
        PACC = g.PACC
        g.ei = 0
        g.ci = 0

        def combine(h, Lq, j, t0):
            i = g.ci % 2; g.ci += 1
            r = rr[i]; o = ot[i]; sq = jk[i]; s_ = s1[i]; y = yb[i]
            c.op('dve', lambda e: e.reciprocal(r[0:Lq, 0:1], Om[0][0:Lq, j, 128:129]), [Om[0]], [r])
            yield
            c.op('dve', lambda e: e.reciprocal(r[0:Lq, 1:2], Om[1][0:Lq, j, 128:129]), [Om[1]], [r])
            yield
            c.op('dve', lambda e: e.tensor_tensor(r[0:Lq, 1:2], r[0:Lq, 1:2], LAMN[0:Lq, 0:1], ALU.mult), [r, LAMN], [r])
            yield
            c.op('dve', lambda e: e.tensor_scalar_mul(o[0:Lq, :], Om[0][0:Lq, j, 0:128], r[0:Lq, 0:1]), [Om[0], r], [o])
            yield
            c.op('dve', lambda e: e.scalar_tensor_tensor(o[0:Lq, :], Om[1][0:Lq, j, 0:128], r[0:Lq, 1:2], o[0:Lq, :], ALU.mult, ALU.add), [Om[1], r, o], [o])
            yield
            c.op('act', lambda e: e.activation(sq[0:Lq, :], o[0:Lq, :], AF.Square, accum_out=s_[0:Lq, 0:1]), [o], [sq, s_])
            yield
            c.op('act', lambda e: e.activation(s_[0:Lq, :], s_[0:Lq, :], AF.Sqrt, bias=epsb[0:Lq, 0:1], scale=1.0 / 128), [s_, epsb], [s_])
            yield
            c.op('dve', lambda e: e.reciprocal(s_[0:Lq, :], s_[0:Lq, :]), [s_], [s_])
            yield
            c.op('dve', lambda e: e.scalar_tensor_tensor(y[0:Lq, :], o[0:Lq, :], s_[0:Lq, 0:1], SUBW[0:Lq, :], ALU.mult, ALU.mult), [o, s_, SUBW], [y])
            yield
            pb = psb()
            c.mm_raw([lambda e: e.transpose(pb[:, 0:Lq], y[0:Lq, :], idb[0:Lq, 0:Lq])], [y, idb], pb)
            yield
            c.op('act', lambda e: e.copy(YD[:, t0:t0 + Lq], pb[:, 0:Lq]), [pb], [YD])
            yield

        yield 'ALLOC_DONE'
        for h in range(4):
            c.dma('sp', Qf[:, :], Z.t[3584 + h * 128:3584 + (h + 1) * 128, :], reads=[Z], writes=[Qf])
            yield
            c.dma('sp', Kf[:, :], Z.t[4096 + h * 128:4096 + (h + 1) * 128, :], reads=[Z], writes=[Kf])
            yield
            c.dma('sp', Vf[:, :], Z.t[4608 + h * 128:4608 + (h + 1) * 128, :], reads=[Z], writes=[Vf])
            yield
            c.op('act', lambda e: e.copy(qb[:, :], Qf[:, :]), [Qf], [qb])
            yield
            c.op('act', lambda e: e.copy(kb[:, :], Kf[:, :]), [Kf], [kb])
            yield
            c.op('act', lambda e: e.copy(vb[:, :], Vf[:, :]), [Vf], [vb])
            yield
            for bi, (t0, L) in enumerate(BLKS):
                pb = psb()
                c.mm_raw([lambda e: e.transpose(pb[0:L, 0:128], vb[:, t0:t0 + L], idb[:, :])], [vb, idb], pb)
                yield
                c.op('act', lambda e: e.copy(V1[0:L, bi, 0:128], pb[0:L, 0:128]), [pb], [V1])
                yield
            cb = RB[:, 31 * 4 + h:31 * 4 + h + 1]
            for m in range(2):
                ms = slice(m * 64, (m + 1) * 64)
                ps = psf(4, 6)
                c.mm([(ps[0:16, 0:16], kb[ms, 0:16], qb[ms, 0:16])], [kb, qb], ps)
                yield
                eb = Eb[g.ei % 3]; g.ei += 1
                c.op('act', lambda e: e.activation(eb[0:16, 0:16], ps[0:16, 0:16], AF.Exp, scale=SCALE), [ps], [eb])
                yield
                c.op('dve', lambda e: e.tensor_tensor(eb[0:16, 0:16], eb[0:16, 0:16], EBMM[0:16, h, :], ALU.mult), [eb, EBMM], [eb])
                yield
                pa = PACC[0]
                c.mm([(pa[0:16, 0:129], eb[0:16, 0:16], V1[0:16, 0, 0:129])], [eb, V1], pa)
                yield
                c.op('act', lambda e: e.copy(Om[m][0:16, 0, 0:129], pa[0:16, 0:129]), [pa], [Om[m]])
                yield
            yield from combine(h, 16, 0, 0)
            for qc in range(4):
                ct0 = 16 + 512 * qc
                for m in range(2):
                    ms = slice(m * 64, (m + 1) * 64)
                    tasks = ['meta'] + list(range(4 * qc + 4))
                    def score(tk):
                        ps = psf(4, 6)
                        if tk == 'meta':
                            c.mm([(ps[0:16, 0:512], kb[ms, 0:16], qb[ms, ct0:ct0 + 512])], [kb, qb], ps)
                            yield
                        else:
                            kt0 = 16 + 128 * tk
                            j0 = max(tk - 4 * qc, 0)
                            ncols = (4 - j0) * 128
                            q0 = ct0 + j0 * 128
                            c.mm([(ps[:, 0:ncols], kb[ms, kt0:kt0 + 128], qb[ms, q0:q0 + ncols])], [kb, qb], ps)
                            yield
                        return ps
                    def consume(tk, ps):
                        eb = Eb[g.ei % 3]; g.ei += 1
                        if tk == 'meta':
                            if qc == 0:
                                c.op('act', lambda e: e.activation(eb[0:16, 0:128], ps[0:16, 0:128], AF.Exp, scale=SCALE), [ps], [eb])
                                yield
                                c.op('dve', lambda e: e.tensor_tensor(eb[0:16, 0:128], eb[0:16, 0:128], EBM0[0:16, h, :], ALU.mult), [eb, EBM0], [eb])
                                yield
                                c.op('act', lambda e: e.activation(eb[0:16, 128:512], ps[0:16, 128:512], AF.Exp, scale=SCALE, bias=cb[0:16, :]), [ps, RB], [eb])
                                yield
                            else:
                                c.op('act', lambda e: e.activation(eb[0:16, 0:512], ps[0:16, 0:512], AF.Exp, scale=SCALE, bias=cb[0:16, :]), [ps, RB], [eb])
                                yield
                            for j in range(4):
                                c.mm([(PACC[j][:, 0:129], eb[0:16, j * 128:(j + 1) * 128], V1[0:16, 0, 0:129])], [eb, V1], PACC[j], start=True, stop=False)
                                yield
                            return
                        kbi = tk
                        j0 = max(kbi - 4 * qc, 0)
                        ncols = (4 - j0) * 128
                        nnear = 0
                        for j in range(j0, 4):
                            if 4 * qc + j - kbi <= 1:
                                nnear += 1
                        if nnear > 0:
                            c.op('act', lambda e: e.activation(eb[:, 0:nnear * 128], ps[:, 0:nnear * 128], AF.Exp, scale=SCALE), [ps], [eb])
                            yield
                        if ncols > nnear * 128:
                            c.op('act', lambda e: e.activation(eb[:, nnear * 128:ncols], ps[:, nnear * 128:ncols], AF.Exp, scale=SCALE, bias=cb), [ps, RB], [eb])
                            yield
                        for j in range(j0, 4):
                            dl = 4 * qc + j - kbi
                            co = (j - j0) * 128
                            if dl == 0:
                                c.op('dve', lambda e: e.tensor_tensor(eb[:, co:co + 128], eb[:, co:co + 128], EB0[:, h, :], ALU.mult), [eb, EB0], [eb])
                                yield
                            elif dl == 1:
                                c.op('dve', lambda e: e.tensor_tensor(eb[:, co:co + 128], eb[:, co:co + 128], EB1[:, h, :], ALU.mult), [eb, EB1], [eb])
                                yield
                        for j in range(j0, 4):
                            dl = 4 * qc + j - kbi
                            co = (j - j0) * 128
                            c.mm([(PACC[j][:, 0:129], eb[:, co:co + 128], V1[:, kbi + 1, 0:129])], [eb, V1], PACC[j], start=False, stop=(dl == 0))
                            yield
                    psn = yield from score(tasks[0])
                    for ti, tk in enumerate(tasks):
                        pscur = psn
                        if ti + 1 < len(tasks):
                            psn = yield from score(tasks[ti + 1])
                        yield from consume(tk, pscur)
                    for j in range(4):
                        c.op('dve', lambda e: e.tensor_copy(Om[m][:, j, 0:129], PACC[j][:, 0:129]), [PACC[j]], [Om[m]])
                        yield
                for j in range(4):
                    yield from combine(h, 128, j, ct0 + j * 128)
            c.dma('sp', YS.t[1536 + h * 128:1536 + (h + 1) * 128, :], YD[:, :], reads=[YD], writes=[YS])
            yield
        yield 'END'


D = 2048
T = 2064
NM = 16
KC = 16
FH = 5632
NIN = 13312
EPS = 1e-6
SUP = 1032
SUBS = [(0, 344), (344, 344), (688, 344)]
CH512 = [(0, 512), (512, 512), (1024, 512), (1536, 512), (2048, 16)]
BLKS = [(0, 16)] + [(16 + 128 * j, 128) for j in range(16)]
DEPTH = 2


def t5_onehot():
    n = np.arange(0, 272)
    max_exact = 16
    nf = np.maximum(n, 1).astype(np.float32)
    large = max_exact + (np.log(nf / np.float32(max_exact)) / np.float32(math.log(128 / max_exact)) * np.float32(32 - max_exact)).astype(np.int32)
    large = np.minimum(large, 31)
    b = np.where(n < max_exact, n, large)
    oh = np.zeros((32, 400), np.float32)
    oh[b, 128 + n] = 1.0
    return oh


class K:
    pass


def build_program(dbg=(), stop_after=None, nlayers=DEPTH):
    nc = bass.Bass("TRN2", target_bir_lowering=False)
    es = ExitStack()
    with es:
        c = Ctx(nc, es)
        g = K()
        I = {}
        def inp(name, shape):
            I[name] = c.dram(name, shape, F32, kind="ExternalInput")
        inp('x', [2048, D]); inp('meta_tokens', [NM, D]); inp('rel_bias', [32, 4]); inp('hgrn_lower_bounds', [2, 512])
        for n_ in ['norm_mix_pre', 'norm_mix_post', 'norm_ffn_pre', 'norm_ffn_post']:
            inp(n_, [2, D])
        inp('w_in', [2, D, NIN]); inp('lru_conv_w', [2, 4, 512]); inp('lru_conv_b', [2, 512])
        inp('lru_w_a', [2, 4, 128, 128]); inp('lru_b_a', [2, 512]); inp('lru_w_x', [2, 4, 128, 128]); inp('lru_b_x', [2, 512])
        inp('lru_lambda', [2, 512]); inp('pool_w', [2, 4, 128, 128]); inp('pool_scale', [2, 512]); inp('hgrn_norm', [2, 128])
        inp('diff_lambda', [2, 4, 64]); inp('diff_subln', [2, 128]); inp('w_branch', [2, 4, 512, D]); inp('w_out', [2, D, D])
        inp('ffn_w_gu', [2, D, 2 * FH]); inp('ffn_w_down', [2, FH, D]); inp('t5oh', [32, 400])
        OUT = c.dram('out', [2048, D], F32, kind="ExternalOutput")

        def scr(name, shape, dt):
            return c.dram(name, shape, dt, kind=("ExternalOutput" if name in dbg else "Internal"))
        H = scr('H', [D, T], F32)
        HN = scr('HN', [D, T], BF16)
        Z = scr('Z', [5120, T], F32)
        YS = scr('YS', [D, T], BF16)
        MIX = scr('MIX', [D, T], F32)
        BD = scr('BD', [4, 400], F32)
        Hv = H.t.rearrange("(k p) t -> p k t", p=128)
        HNv = HN.t.rearrange("(k p) t -> p k t", p=128)
        YSv = YS.t.rearrange("(k p) t -> p k t", p=128)
        MIXv = MIX.t.rearrange("(k p) t -> p k t", p=128)

        PSF = [c.ps([128, 512], F32, name='psf%d' % i) for i in range(8)]
        PSV = [BufView(PSF[i], PSF[i].t[:, :].bitcast(BF16)) for i in range(8)]
        g.pi = 0
        def psf(lo=0, hi=6):
            g.pi += 1
            return PSF[lo + g.pi % (hi - lo)]
        g.pb = 0
        g.PACC = PSF[0:4]
        def psb():
            g.pb += 1
            return PSV[6 + g.pb % 2]
        def psb6():
            return PSV[6]
        def psf7(lo=0, hi=0):
            return PSF[7]

        ones_f = c.sb([128, 128], F32, 'ones_f')
        c.op('pool', lambda e: e.memset(ones_f[:, :], 1.0), [], [ones_f])
        ones_b = c.sb([128, 128], BF16, 'ones_b')
        c.op('pool', lambda e: e.memset(ones_b[:, :], 1.0), [], [ones_b])
        idf = c.sb([128, 128], F32, 'idf')
        c.op('pool', lambda e: e.affine_select(idf[:, :], ones_f[:, :], [[-1, 128]], ALU.is_equal, 0.0, base=0, channel_multiplier=1), [ones_f], [idf])
        idb = c.sb([128, 128], BF16, 'idb')
        J128 = c.sb([128, 128], F32, 'J128')
        c.op('pool', lambda e: e.affine_select(J128[:, :], ones_f[:, :], [[1, 128]], ALU.is_equal, 0.0, base=-127, channel_multiplier=1), [ones_f], [J128])
        J16 = c.sb([16, 16], F32, 'J16')
        c.op('pool', lambda e: e.affine_select(J16[:, :], ones_f[0:16, 0:16], [[1, 16]], ALU.is_equal, 0.0, base=-15, channel_multiplier=1), [ones_f], [J16])
        epsb = c.sb([128, 1], F32, 'epsb')
        c.op('pool', lambda e: e.memset(epsb[:, :], EPS), [], [epsb])

        MK = None
        HM = c.sb([128, 128], F32, 'HM')
        c.op('pool', lambda e: e.affine_select(HM[:, :], ones_f[:, :], [[1, 128]], ALU.is_ge, 0.0, base=0, channel_multiplier=-1), [ones_f], [HM])
        c.op('pool', lambda e: e.memset(HM[0:64, 64:128], 0.0), [], [HM])
        IC = c.sb([128, 4, 16], F32, 'IC')
        ioti = c.sb([128, 16], mybir.dt.int32, 'ioti')
        c.op('pool', lambda e: e.iota(ioti[:, :], [[1, 16]], base=1, channel_multiplier=0), [], [ioti])
        iotf = c.sb([128, 16], F32, 'iotf')

        CM = c.sb([128, 128], F32, 'CM')
        c.op('pool', lambda e: e.affine_select(CM[:, :], ones_f[:, :], [[1, 128]], ALU.is_ge, 0.0, base=0, channel_multiplier=-1), [ones_f], [CM])
        c.barrier()
        c.op('dve', lambda e: e.tensor_copy(idb[:, :], idf[:, :]), [idf], [idb])
        c.op('dve', lambda e: e.tensor_copy(iotf[:, :], ioti[:, :]), [ioti], [iotf])
        for gi in range(4):
            c.op('dve', lambda e, gi=gi: e.tensor_scalar_min(IC[:, gi, :], iotf[:, :], float(2 ** (gi + 1))), [iotf], [IC])
        c.op('dve', lambda e: e.reciprocal(IC[:, :, :], IC[:, :, :]), [IC], [IC])

        def vecload(dst_ap, src_ap, dstb, srcb):
            c.dma('sp', dst_ap, src_ap, reads=[srcb], writes=[dstb], allow_slow_non_contiguous=True)

        NW = c.sb([128, 4, 2, 16], F32, 'NW')
        for i_, n_ in enumerate(['norm_mix_pre', 'norm_mix_post', 'norm_ffn_pre', 'norm_ffn_post']):
            for l in range(DEPTH):
                vecload(NW[:, i_, l, :], I[n_].t[l, :].rearrange("(k p) -> p k", p=128), NW, I[n_])

        def norm_sub(sc, src_v, srcb, t0, n, wk, l, dst_fn, dstb, resid=False, second=None):
            xt = sc.xt[sc.i % 2]; sq = sc.sq[sc.i % 2]; rs = sc.rs[sc.i % 2]; sc.i += 1
            c.dma('sp', xt[:, :, 0:n], src_v[:, :, t0:t0 + n], reads=[srcb], writes=[xt])
            if resid:
                ht = sc.ht[sc.i % 2]
                c.dma('sp', ht[:, :, 0:n], Hv[:, :, t0:t0 + n], reads=[H], writes=[ht])
            c.op('act', lambda e: e.activation(sq[:, :, 0:n], xt[:, :, 0:n], AF.Square), [xt], [sq])
            ps = psf(4, 6)
            c.mm([(ps[:, 0:n], ones_b[:, :], sq[:, k, 0:n]) for k in range(KC)], [ones_b, sq], ps)
            c.op('act', lambda e: e.activation(rs[:, 0:n], ps[:, 0:n], AF.Sqrt, bias=epsb[:, 0:1], scale=1.0 / D), [ps, epsb], [rs])
            c.op('dve', lambda e: e.reciprocal(rs[:, 0:n], rs[:, 0:n]), [rs], [rs])
            if not resid:
                for k in range(KC):
                    c.op('dve', lambda e, k=k: e.scalar_tensor_tensor(dst_fn(k), xt[:, k, 0:n], NW[:, wk, l, k:k + 1], rs[:, 0:n], ALU.mult, ALU.mult), [xt, NW, rs], [dstb])
            else:
                for k in range(KC):
                    c.op('dve', lambda e, k=k: e.scalar_tensor_tensor(xt[:, k, 0:n], xt[:, k, 0:n], NW[:, wk, l, k:k + 1], rs[:, 0:n], ALU.mult, ALU.mult), [xt, NW, rs], [xt])
                c.op('dve', lambda e: e.tensor_tensor(ht[:, :, 0:n], ht[:, :, 0:n], xt[:, :, 0:n], ALU.add), [ht, xt], [ht])
                c.dma('sp', Hv[:, :, t0:t0 + n], ht[:, :, 0:n], reads=[ht], writes=[H])
                if second is not None:
                    wk2, l2, dst_fn2, dstb2, after = second
                    c.op('act', lambda e: e.activation(sq[:, :, 0:n], ht[:, :, 0:n], AF.Square), [ht], [sq])
                    ps2 = psf(4, 6)
                    c.mm([(ps2[:, 0:n], ones_b[:, :], sq[:, k, 0:n]) for k in range(KC)], [ones_b, sq], ps2)
                    c.op('act', lambda e: e.activation(rs[:, 0:n], ps2[:, 0:n], AF.Sqrt, bias=epsb[:, 0:1], scale=1.0 / D), [ps2, epsb], [rs])
                    c.op('dve', lambda e: e.reciprocal(rs[:, 0:n], rs[:, 0:n]), [rs], [rs])
                    for k in range(KC):
                        c.op('dve', lambda e, k=k: e.scalar_tensor_tensor(dst_fn2(k), ht[:, k, 0:n], NW[:, wk2, l2, k:k + 1], rs[:, 0:n], ALU.mult, ALU.mult), [ht, NW, rs], [dstb2])
                    if after is not None:
                        after()

        class NormScr:
            def __init__(self, s, resid=False):
                self.i = 0
                self.xt = [c.sb([128, KC, 344], F32, es=s) for _ in range(2)]
                self.sq = [c.sb([128, KC, 344], BF16, es=s) for _ in range(2)]
                self.rs = [c.sb([128, 344], F32, es=s) for _ in range(2)]
                if resid:
                    self.ht = [c.sb([128, KC, 344], F32, es=s) for _ in range(2)]

        class GemmScr:
            def __init__(self, s, kc, wmax):
                self.i = 0
                self.kc = kc
                self.st = [c.sb([128, kc, wmax], F32, es=s) for _ in range(2)]
                self.wb = [c.sb([128, kc, wmax], BF16, es=s) for _ in range(2)]

        def gemm(gs, W2d, Wb, groups, x_fn, xbufs, subs, epi):
            kc = gs.kc
            def load(gi):
                col0, width = groups[gi]
                st = gs.st[gs.i % 2]; wb = gs.wb[gs.i % 2]; gs.i += 1
                c.dma('sp', st[:, :, 0:width], W2d[:, col0:col0 + width].rearrange("(k p) n -> p k n", p=128), reads=[Wb], writes=[st])
                c.op('dve', lambda e: e.tensor_copy(wb[:, :, 0:width], st[:, :, 0:width]), [st], [wb])
                return wb
            nxt = load(0)
            for gi, (col0, width) in enumerate(groups):
                wb = nxt
                if gi + 1 < len(groups):
                    nxt = load(gi + 1)
                for bi in range(width // 128):
                    for si, (t0, n) in enumerate(subs):
                        ps = psf(0, 4)
                        c.mm([(ps[:, 0:n], wb[:, k, bi * 128:(bi + 1) * 128], x_fn(k, t0, n)) for k in range(kc)], [wb] + xbufs, ps)
                        epi(ps, col0 + bi * 128, si, t0, n)

        with ExitStack() as s:
            xin = [c.sb([128, D], F32, es=s) for _ in range(2)]
            stg = [c.sb([128, KC, 128], F32, es=s) for _ in range(2)]
            for bi, (t0, L) in enumerate(BLKS):
                xi = xin[bi % 2]; sg = stg[bi % 2]
                if bi == 0:
                    c.dma('sp', xi[0:L, :], I['meta_tokens'].t[:, :], reads=[I['meta_tokens']], writes=[xi])
                else:
                    c.dma('sp', xi[0:L, :], I['x'].t[t0 - 16:t0 - 16 + L, :], reads=[I['x']], writes=[xi])
                for kg in range(4):
                    ps = psf(0, 4)
                    c.mm_raw([(lambda e, j=j: e.transpose(ps[:, j * 128:j * 128 + L], xi[0:L, (4 * kg + j) * 128:(4 * kg + j + 1) * 128], idf[0:L, 0:L])) for j in range(4)], [xi, idf], ps)
                    c.op('act', lambda e: e.copy(sg[:, 4 * kg:4 * kg + 4, 0:L], ps[:, :].rearrange("p (j t) -> p j t", j=4)[:, :, 0:L]), [ps], [sg])
                c.dma('sp', Hv[:, :, t0:t0 + L], sg[:, :, 0:L], reads=[sg], writes=[H])
        c.barrier()

        RB = c.sb([128, 128], F32, 'RB')
        c.dma('sp', RB[:, :], I['rel_bias'].t.rearrange("a b -> (a b)").partition_broadcast(128), reads=[I['rel_bias']], writes=[RB])
        EB0 = c.sb([128, 4, 128], BF16, 'EB0'); EB1 = c.sb([128, 4, 128], BF16, 'EB1')
        EBM0 = c.sb([16, 4, 128], BF16, 'EBM0'); EBMM = c.sb([16, 4, 16], BF16, 'EBMM')
        with ExitStack() as s:
            rbs = c.sb([32, 4], F32, es=s); ohs = c.sb([32, 400], F32, es=s); bds = c.sb([4, 400], F32, es=s)
            c.dma('sp', rbs[:, :], I['rel_bias'].t[:, :], reads=[I['rel_bias']], writes=[rbs])
            c.dma('sp', ohs[:, :], I['t5oh'].t[:, :], reads=[I['t5oh']], writes=[ohs])
            ps = psf()
            c.mm([(ps[0:4, 0:400], rbs[:, :], ohs[:, :])], [rbs, ohs], ps)
            c.op('dve', lambda e: e.tensor_copy(bds[:, :], ps[0:4, 0:400]), [ps], [bds])
            c.dma('sp', BD.t[:, :], bds[:, :], reads=[bds], writes=[BD])
            hk = [c.sb([128, 128], F32, es=s) for _ in range(2)]
            ef = [c.sb([128, 128], F32, es=s) for _ in range(2)]
            i_ = 0
            for h in range(4):
                for (dst, base, P, N, causal) in [(EB0, 1, 128, 128, True), (EB1, 129, 128, 128, False), (EBM0, 129, 16, 128, False), (EBMM, 113, 16, 16, True)]:
                    hb = hk[i_ % 2]; eb = ef[i_ % 2]; i_ += 1
                    c.dma('sp', hb[0:P, 0:N], bass.AP(BD.t.tensor, h * 400 + base, [[1, P], [1, N]]), reads=[BD], writes=[hb])
                    ps = psf()
                    Jm = J128 if P == 128 else J16
                    c.mm([(ps[0:P, 0:N], Jm[0:P, 0:P], hb[0:P, 0:N])], [Jm, hb], ps)
                    c.op('act', lambda e: e.activation(eb[0:P, 0:N], ps[0:P, 0:N], AF.Exp), [ps], [eb])
                    if causal:
                        c.op('dve', lambda e: e.tensor_tensor(dst[0:P, h, 0:N], eb[0:P, 0:N], CM[0:P, 0:N], ALU.mult), [eb, CM], [dst])
                    else:
                        c.op('dve', lambda e: e.tensor_copy(dst[0:P, h, 0:N], eb[0:P, 0:N]), [eb], [dst])
        c.barrier()

        for l in range(nlayers):
            lambda_init = 0.8 - 0.6 * math.exp(-0.3 * l)
            Win = I['w_in'].t[l, :, :]
            with ExitStack() as s:
                hnT = c.sb([128, KC, T], BF16, es=s)
                if l == 0:
                    with ExitStack() as sn:
                        nsc = NormScr(sn)
                        for sp_ in range(2):
                            for (o, n) in SUBS:
                                norm_sub(nsc, Hv, H, sp_ * SUP + o, n, 0, l, lambda k, o=o, n=n, sp_=sp_: hnT[:, k, sp_ * SUP + o:sp_ * SUP + o + n], hnT)
                        for sp_ in range(2):
                            c.dma('sp', HNv[:, :, sp_ * SUP:(sp_ + 1) * SUP], hnT[:, :, sp_ * SUP:(sp_ + 1) * SUP], reads=[hnT], writes=[HN])
                    c.barrier()
                else:
                    for sp_ in range(2):
                        c.dma('sp', hnT[:, :, sp_ * SUP:(sp_ + 1) * SUP], HNv[:, :, sp_ * SUP:(sp_ + 1) * SUP], reads=[HN], writes=[hnT])
                gs = GemmScr(s, KC, 256)
                zo = [c.sb([128, 344], F32, es=s) for _ in range(3)]
                zi = [0]
                def epi(ps, col, si, t0, n):
                    ob = zo[zi[0] % 3]; zi[0] += 1
                    c.op('act', lambda e: e.copy(ob[:, 0:n], ps[:, 0:n]), [ps], [ob])
                    c.dma('sp', Z.t[col:col + 128, t0:t0 + n], ob[:, 0:n], reads=[ob], writes=[Z])
                gemm(gs, Win, I['w_in'], [(cb * 256, 256) for cb in range(20)], lambda k, t0, n: hnT[:, k, t0:t0 + n], [hnT], [(sp_ * SUP + o, n) for sp_ in range(2) for (o, n) in SUBS], epi)
            c.barrier()
            if stop_after == 'p1':
                break

            for _ in mixer_hgrn(c, g, I, l, Z, YS, psf, psb, MK, HM, idb, ones_b, epsb):
                pass
            c.barrier()
            def stream_a():
                yield from mixer_lru(c, g, I, l, Z, YS, psf7)
                c.barrier()
                yield from mixer_pool(c, g, I, l, Z, YS, psf7, IC)
            run_streams(mixer_attn(c, g, I, l, Z, YS, psf, psb6, idb, RB, EB0, EB1, EBM0, EBMM, epsb, lambda_init), stream_a(), 10)
            c.barrier()
            if stop_after == 'p2':
                break

            for sp_ in range(2):
                s0 = sp_ * SUP
                with ExitStack() as s3:
                    mergedT = c.sb([128, KC, SUP], BF16, es=s3)
                    with ExitStack() as s:
                        hnT = c.sb([128, KC, SUP], BF16, es=s)
                        ysT = c.sb([128, KC, SUP], BF16, es=s)
                        c.dma('sp', hnT[:, :, :], HNv[:, :, s0:s0 + SUP], reads=[HN], writes=[hnT])
                        c.dma('sp', ysT[:, :, :], YSv[:, :, s0:s0 + SUP], reads=[YS], writes=[ysT])
                        gst = [c.sb([128, KC, 128], F32, es=s) for _ in range(3)]
                        wg = [c.sb([128, 4, KC, 128], BF16, es=s) for _ in range(2)]
                        pst = [c.sb([128, 4, 128], F32, es=s) for _ in range(3)]
                        wp = [c.sb([128, 4, 4, 128], BF16, es=s) for _ in range(2)]
                        sgb = [c.sb([128, 344], F32, es=s) for _ in range(3)]
                        tmpb = [c.sb([128, 344], F32, es=s) for _ in range(3)]
                        acc = [c.sb([128, 344], F32, es=s) for _ in range(3)]
                        ii = 0
                        g.li = 0
                        def load3(db):
                            wgb = wg[db % 2]; wpb = wp[db % 2]
                            for k in range(4):
                                st = gst[g.li % 3]; st2 = pst[g.li % 3]; g.li += 1
                                col = 5120 + k * D + db * 128
                                c.dma('sp', st[:, :, :], Win[:, col:col + 128].rearrange("(k p) n -> p k n", p=128), reads=[I['w_in']], writes=[st])
                                c.op('dve', lambda e: e.tensor_copy(wgb[:, k, :, :], st[:, :, :]), [st], [wgb])
                                c.dma('sp', st2[:, :, :], I['w_branch'].t[l, k, :, db * 128:(db + 1) * 128].rearrange("(k p) n -> p k n", p=128), reads=[I['w_branch']], writes=[st2])
                                c.op('dve', lambda e: e.tensor_copy(wpb[:, k, :, :], st2[:, :, :]), [st2], [wpb])
                        load3(0)
                        for db in range(16):
                            wgb = wg[db % 2]; wpb = wp[db % 2]
                            if db + 1 < 16:
                                load3(db + 1)
                            for si, (o, n) in enumerate(SUBS):
                                ac = acc[si]
                                for k in range(4):
                                    sg = sgb[ii % 3]; tm = tmpb[ii % 3]; ii += 1
                                    ps = psf(0, 4)
                                    c.mm([(ps[:, 0:n], wgb[:, k, kk, :], hnT[:, kk, o:o + n]) for kk in range(KC)], [wgb, hnT], ps)
                                    c.op('act', lambda e: e.activation(sg[:, 0:n], ps[:, 0:n], AF.Sigmoid), [ps], [sg])
                                    ps2 = psf(0, 4)
                                    c.mm([(ps2[:, 0:n], wpb[:, k, kk, :], ysT[:, 4 * k + kk, o:o + n]) for kk in range(4)], [wpb, ysT], ps2)
                                    if k == 0:
                                        c.op('dve', lambda e: e.tensor_tensor(ac[:, 0:n], sg[:, 0:n], ps2[:, 0:n], ALU.mult), [sg, ps2], [ac])
                                    else:
                                        c.op('dve', lambda e: e.tensor_tensor(tm[:, 0:n], sg[:, 0:n], ps2[:, 0:n], ALU.mult), [sg, ps2], [tm])
                                        if k < 3:
                                            c.op('dve', lambda e: e.tensor_tensor(ac[:, 0:n], ac[:, 0:n], tm[:, 0:n], ALU.add), [ac, tm], [ac])
                                        else:
                                            c.op('dve', lambda e: e.tensor_tensor(mergedT[:, db, o:o + n], ac[:, 0:n], tm[:, 0:n], ALU.add), [ac, tm], [mergedT])
                    c.barrier()
                    with ExitStack() as s:
                        gs = GemmScr(s, KC, 256)
                        zo = [c.sb([128, 344], F32, es=s) for _ in range(3)]
                        zi = [0]
                        def epi(ps, col, si, t0, n):
                            ob = zo[zi[0] % 3]; zi[0] += 1
                            c.op('act', lambda e: e.copy(ob[:, 0:n], ps[:, 0:n]), [ps], [ob])
                            c.dma('sp', MIX.t[col:col + 128, s0 + t0:s0 + t0 + n], ob[:, 0:n], reads=[ob], writes=[MIX])
                        gemm(gs, I['w_out'].t[l, :, :], I['w_out'], [(cb * 256, 256) for cb in range(8)], lambda k, t0, n: mergedT[:, k, t0:t0 + n], [mergedT], SUBS, epi)
                    c.barrier()
                with ExitStack() as s3:
                    hnT = c.sb([128, KC, SUP], BF16, es=s3)
                    with ExitStack() as sn:
                        nsc = NormScr(sn, resid=True)
                        for (o, n) in SUBS:
                            norm_sub(nsc, MIXv, MIX, s0 + o, n, 1, l, None, None, resid=True,
                                     second=(2, l, lambda k, o=o, n=n: hnT[:, k, o:o + n], hnT, None))
                    c.barrier()
                    hid = c.sb([128, 44, SUP], BF16, es=s3)
                    with ExitStack() as s:
                        Wgu = I['ffn_w_gu'].t[l, :, :]
                        st = [c.sb([128, KC, 2, 128], F32, es=s) for _ in range(2)]
                        wb = [c.sb([128, KC, 2, 128], BF16, es=s) for _ in range(2)]
                        sgb = [c.sb([128, 344], F32, es=s) for _ in range(3)]
                        ii = 0
                        def loadgu(j):
                            stj = st[j % 2]; wbj = wb[j % 2]
                            c.dma('sp', stj[:, :, 0, :], Wgu[:, j * 128:(j + 1) * 128].rearrange("(k p) n -> p k n", p=128), reads=[I['ffn_w_gu']], writes=[stj])
                            c.dma('sp', stj[:, :, 1, :], Wgu[:, FH + j * 128:FH + (j + 1) * 128].rearrange("(k p) n -> p k n", p=128), reads=[I['ffn_w_gu']], writes=[stj])
                            c.op('dve', lambda e: e.tensor_copy(wbj[:, :, :, :], stj[:, :, :, :]), [stj], [wbj])
                        loadgu(0)
                        for j in range(44):
                            wbj = wb[j % 2]
                            if j + 1 < 44:
                                loadgu(j + 1)
                            for si, (o, n) in enumerate(SUBS):
                                sg = sgb[ii % 3]; ii += 1
                                ps = psf(0, 4)
                                c.mm([(ps[:, 0:n], wbj[:, kk, 0, :], hnT[:, kk, o:o + n]) for kk in range(KC)], [wbj, hnT], ps)
                                c.op('act', lambda e: e.activation(sg[:, 0:n], ps[:, 0:n], AF.Silu), [ps], [sg])
                                ps2 = psf(0, 4)
                                c.mm([(ps2[:, 0:n], wbj[:, kk, 1, :], hnT[:, kk, o:o + n]) for kk in range(KC)], [wbj, hnT], ps2)
                                c.op('dve', lambda e: e.tensor_tensor(hid[:, j, o:o + n], sg[:, 0:n], ps2[:, 0:n], ALU.mult), [sg, ps2], [hid])
                    c.barrier()
                    with ExitStack() as s:
                        gs = GemmScr(s, 44, 128)
                        zo = [c.sb([128, 344], F32, es=s) for _ in range(3)]
                        zi = [0]
                        def epi(ps, col, si, t0, n):
                            ob = zo[zi[0] % 3]; zi[0] += 1
                            c.op('act', lambda e: e.copy(ob[:, 0:n], ps[:, 0:n]), [ps], [ob])
                            c.dma('sp', MIX.t[col:col + 128, s0 + t0:s0 + t0 + n], ob[:, 0:n], reads=[ob], writes=[MIX])
                        gemm(gs, I['ffn_w_down'].t[l, :, :], I['ffn_w_down'], [(cb * 128, 128) for cb in range(16)], lambda k, t0, n: hid[:, k, t0:t0 + n], [hid], SUBS, epi)
                    c.barrier()
                with ExitStack() as s:
                    nsc = NormScr(s, resid=True)
                    if l + 1 < nlayers:
                        hb = [c.sb([128, KC, 344], BF16, es=s) for _ in range(2)]
                    for si_, (o, n) in enumerate(SUBS):
                        if l + 1 < nlayers:
                            hbt = hb[si_ % 2]
                            def after(hbt=hbt, o=o, n=n):
                                c.dma('sp', HNv[:, :, s0 + o:s0 + o + n], hbt[:, :, 0:n], reads=[hbt], writes=[HN])
                            norm_sub(nsc, MIXv, MIX, s0 + o, n, 3, l, None, None, resid=True,
                                     second=(0, l + 1, lambda k, hbt=hbt, n=n: hbt[:, k, 0:n], hbt, after))
                        else:
                            norm_sub(nsc, MIXv, MIX, s0 + o, n, 3, l, None, None, resid=True)
                c.barrier()

        if stop_after is None:
            with ExitStack() as s:
                hin = [c.sb([128, KC, 128], F32, es=s) for _ in range(2)]
                ostg = [c.sb([128, D], F32, es=s) for _ in range(2)]
                for j in range(16):
                    t0 = 16 + 128 * j
                    hi_ = hin[j % 2]; og = ostg[j % 2]
                    c.dma('sp', hi_[:, :, :], Hv[:, :, t0:t0 + 128], reads=[H], writes=[hi_])
                    for kg in range(4):
                        ps = psf(0, 4)
                        c.mm_raw([(lambda e, jj=jj: e.transpose(ps[:, jj * 128:(jj + 1) * 128], hi_[:, 4 * kg + jj, :], idf[:, :])) for jj in range(4)], [hi_, idf], ps)
                        c.op('act', lambda e: e.copy(og[:, kg * 512:(kg + 1) * 512], ps[:, :]), [ps], [og])
                    c.dma('sp', OUT.t[128 * j:128 * (j + 1), :], og[:, :], reads=[og], writes=[OUT])
        else:
            zt = c.sb([128, D], F32, 'zt')
            c.op('dve', lambda e: e.memset(zt[:, :], 0.0), [], [zt])
            c.dma('sp', OUT.t[0:128, :], zt[:, :], reads=[zt], writes=[OUT])
        c.barrier()
        c.final_wait()
    return nc


def kernel(**inputs):
    nc = build_program()
    oh = t5_onehot()
    active = {0: 0, 1: 1, 4: 2, 5: 3}
    in_maps = []
    zeros = {}
    for core in range(8):
        m = {}
        for k, v in inputs.items():
            v = np.asarray(v)
            if core in active:
                m[k] = np.ascontiguousarray(v[active[core]]) if k == 'x' else np.ascontiguousarray(v)
            else:
                if k not in zeros:
                    zeros[k] = np.zeros(v.shape[1:] if k == 'x' else v.shape, v.dtype)
                m[k] = zeros[k]
        m['t5oh'] = oh
        in_maps.append(m)
    res = run_bass_kernel_spmd(nc, in_maps, core_ids=list(range(8)))
    cores = sorted(active, key=lambda c_: active[c_])
    return np.stack([np.asarray(res.results[c_]['out']) for c_ in cores], axis=0).astype(np.float32)
```
